# Optimizing a Trainium2 kernel written in Bass

```python
import jax, jax.numpy as jnp
from jax import lax
import numpy as np

D_MODEL = 1024
BATCH = 8
SEQ = 2048
DEPTH = 2

MLA_HEADS = 8
QK_NOPE_DIM = 64
QK_ROPE_DIM = 32
QK_HEAD_DIM = QK_NOPE_DIM + QK_ROPE_DIM
V_HEAD_DIM = 64
Q_LORA_RANK = 384
KV_LORA_RANK = 256
ROPE_THETA = 10000.0
Q_BLOCK = 128
RWKV_HEAD_DIM = 64
RWKV_HEADS = 4
RWKV_WIDTH = RWKV_HEADS * RWKV_HEAD_DIM
DECAY_LORA = 64
AAA_LORA = 64
GATE_LORA = 128
MV_LORA = 32
GN_EPS = 64e-5
CONV_WIDTH = 256
CONV_K = 3
D_FF = 4 * D_MODEL
N_BRANCH = 3
NORM_EPS = 1e-6

GATE_COLS = N_BRANCH * D_MODEL
MLA_COLS = Q_LORA_RANK + KV_LORA_RANK + QK_ROPE_DIM
RWKV_COLS = 3 * RWKV_WIDTH + DECAY_LORA + AAA_LORA + GATE_LORA
CONV_COLS = 3 * CONV_WIDTH
IN_COLS = GATE_COLS + MLA_COLS + RWKV_COLS + CONV_COLS

kernel_name = 'hybrid_mla_rwkv7_shortconv_block'


def _split(x, sizes):
    idx = np.cumsum(sizes)[:-1].tolist()
    return jnp.split(x, idx, axis=-1)


def rms_norm(x, g, eps=NORM_EPS):
    xf = x.astype(jnp.float32)
    y = xf * lax.rsqrt(jnp.mean(xf * xf, axis=-1, keepdims=True) + eps)
    return (y * g.astype(jnp.float32)).astype(x.dtype)


def token_shift(x):
    return jnp.pad(x, ((0, 0), (1, 0), (0, 0)))[:, :-1]


def rope_tables(positions):
    freqs = ROPE_THETA ** (-(jnp.arange(QK_ROPE_DIM // 2, dtype=jnp.float32) * 2.0 / QK_ROPE_DIM))
    ang = positions.astype(jnp.float32)[..., None] * freqs
    return jnp.cos(ang)[:, :, None, :], jnp.sin(ang)[:, :, None, :]


def apply_rope(x, cos, sin):
    x1, x2 = x[..., :QK_ROPE_DIM // 2], x[..., QK_ROPE_DIM // 2:]
    cos, sin = cos.astype(x.dtype), sin.astype(x.dtype)
    return jnp.concatenate([x1 * cos - x2 * sin, x1 * sin + x2 * cos], axis=-1)


def causal_block_attention(q, k, v):
    scale = QK_HEAD_DIM ** -0.5
    qh, kh, vh = (jnp.swapaxes(t, 1, 2) for t in (q, k, v))
    seq = qh.shape[2]
    outs = []
    for start in range(0, seq, Q_BLOCK):
        stop = start + Q_BLOCK
        s = jnp.einsum('bhqd,bhkd->bhqk', qh[:, :, start:stop], kh[:, :, :stop]).astype(jnp.float32) * scale
        causal = (start + jnp.arange(Q_BLOCK))[:, None] >= jnp.arange(stop)[None, :]
        p = jax.nn.softmax(jnp.where(causal, s, -jnp.inf), axis=-1)
        outs.append(jnp.einsum('bhqk,bhkd->bhqd', p.astype(vh.dtype), vh[:, :, :stop]))
    return jnp.swapaxes(jnp.concatenate(outs, axis=2), 1, 2)


def mla_branch(cols, positions, q_a_norm, wq_b, kv_a_norm, wkv_b, q_norm, k_norm):
    bsz, seq, _ = cols.shape
    c_q, c_kv, k_pe = _split(cols, [Q_LORA_RANK, KV_LORA_RANK, QK_ROPE_DIM])
    q = (rms_norm(c_q, q_a_norm) @ wq_b).reshape(bsz, seq, MLA_HEADS, QK_HEAD_DIM)
    kv = (rms_norm(c_kv, kv_a_norm) @ wkv_b).reshape(bsz, seq, MLA_HEADS, QK_NOPE_DIM + V_HEAD_DIM)
    k_nope, v = kv[..., :QK_NOPE_DIM], kv[..., QK_NOPE_DIM:]
    k_pe = jnp.broadcast_to(k_pe[:, :, None, :], (bsz, seq, MLA_HEADS, QK_ROPE_DIM))
    k = jnp.concatenate([k_nope, k_pe], axis=-1)
    q = rms_norm(q, q_norm)
    k = rms_norm(k, k_norm)
    cos, sin = rope_tables(positions)
    q = jnp.concatenate([q[..., :QK_NOPE_DIM], apply_rope(q[..., QK_NOPE_DIM:], cos, sin)], axis=-1)
    k = jnp.concatenate([k[..., :QK_NOPE_DIM], apply_rope(k[..., QK_NOPE_DIM:], cos, sin)], axis=-1)
    o = causal_block_attention(q, k, v)
    return o.reshape(bsz, seq, MLA_HEADS * V_HEAD_DIM)


def wkv7_scan(r, w, k, v, a, b):
    def step(state, inp):
        r_t, w_t, k_t, v_t, a_t, b_t = inp
        sa = jnp.einsum('bhvk,bhk->bhv', state, a_t)
        state = state * w_t[:, :, None, :] + sa[..., None] * b_t[:, :, None, :] + v_t[..., None] * k_t[:, :, None, :]
        return state, jnp.einsum('bhvk,bhk->bhv', state, r_t)
    bsz, _, heads, n = r.shape
    xs = tuple(jnp.moveaxis(t, 1, 0) for t in (r, w, k, v, a, b))
    s0 = jnp.zeros((bsz, heads, n, n), jnp.float32)
    _, ys = lax.scan(step, s0, xs)
    return jnp.moveaxis(ys, 0, 1)


def rwkv7_branch(cols, h, v_first, mu, w0, w2, a0, a2, g2, k_k, k_a, r_k, ln_w, ln_b, vres):
    bsz, seq, _ = cols.shape
    f32 = jnp.float32
    cols = cols + (token_shift(cols) - cols) * mu
    r, k, v, xw, xa, xg = _split(cols, [RWKV_WIDTH, RWKV_WIDTH, RWKV_WIDTH, DECAY_LORA, AAA_LORA, GATE_LORA])
    log_w = -jax.nn.softplus(-(w0 + jnp.tanh(xw) @ w2).astype(f32)) - 0.5
    decay = jnp.exp(-jnp.exp(log_w))
    a = jax.nn.sigmoid(a0 + xa @ a2)
    g = jax.nn.sigmoid(xg) @ g2
    if vres is None:
        v_first = v
    else:
        v1, v_mu, v0, v2 = vres
        xv = h @ v1
        xv = xv + (token_shift(xv) - xv) * v_mu
        v = v + (v_first - v) * jax.nn.sigmoid(v0 + xv @ v2)
    heads = lambda t: t.reshape(bsz, seq, RWKV_HEADS, RWKV_HEAD_DIM).astype(f32)
    kk = heads(k * k_k)
    kk = kk / jnp.maximum(jnp.linalg.norm(kk, axis=-1, keepdims=True), 1e-12)
    k = k * (1 + (a - 1) * k_a)
    rh, kh, vh, ah = heads(r), heads(k), heads(v), heads(a)
    y = wkv7_scan(rh, heads(decay), kh, vh, -kk, kk * ah)
    mean = jnp.mean(y, axis=-1, keepdims=True)
    var = jnp.mean(jnp.square(y - mean), axis=-1, keepdims=True)
    y = ((y - mean) * lax.rsqrt(var + GN_EPS)).reshape(bsz, seq, RWKV_WIDTH)
    y = y * ln_w.astype(f32) + ln_b.astype(f32)
    bonus = jnp.sum(rh * kh * r_k.astype(f32), axis=-1, keepdims=True) * vh
    y = (y + bonus.reshape(bsz, seq, RWKV_WIDTH)).astype(cols.dtype)
    return y * g, v_first


def short_conv_branch(cols, conv_w):
    b_gate, c_gate, xc = _split(cols, [CONV_WIDTH, CONV_WIDTH, CONV_WIDTH])
    u = c_gate * xc
    y = lax.conv_general_dilated(
        u, conv_w.astype(u.dtype)[:, None, :], window_strides=(1,), padding=[(CONV_K - 1, 0)],
        dimension_numbers=('NWC', 'WIO', 'NWC'), feature_group_count=CONV_WIDTH)
    return b_gate * y


def setup_inputs(seed: int = 0) -> dict:
    key = jax.random.key(seed)
    ks = iter(jax.random.split(key, 64))
    nrm = lambda shape, scale: jax.random.normal(next(ks), shape, jnp.float32) * scale
    gain = lambda shape: 1.0 + nrm(shape, 0.05)
    unif = lambda shape, lo, hi: jax.random.uniform(next(ks), shape, jnp.float32, lo, hi)
    L, Lv = DEPTH, DEPTH - 1
    x = nrm((BATCH, SEQ, D_MODEL), 1.0)
    offsets = jax.random.randint(next(ks), (BATCH, 1), 0, 4096, dtype=jnp.int32)
    positions = offsets + jnp.arange(SEQ, dtype=jnp.int32)[None, :]
    return {
        'x': x,
        'positions': positions,
        'attn_norm': gain((L, D_MODEL)),
        'w_in': nrm((L, D_MODEL, IN_COLS), D_MODEL ** -0.5),
        'mla_q_a_norm': gain((L, Q_LORA_RANK)),
        'mla_wq_b': nrm((L, Q_LORA_RANK, MLA_HEADS * QK_HEAD_DIM), Q_LORA_RANK ** -0.5),
        'mla_kv_a_norm': gain((L, KV_LORA_RANK)),
        'mla_wkv_b': nrm((L, KV_LORA_RANK, MLA_HEADS * (QK_NOPE_DIM + V_HEAD_DIM)), KV_LORA_RANK ** -0.5),
        'mla_q_norm': gain((L, QK_HEAD_DIM)),
        'mla_k_norm': gain((L, QK_HEAD_DIM)),
        'mla_w_o': nrm((L, MLA_HEADS * V_HEAD_DIM, D_MODEL), (MLA_HEADS * V_HEAD_DIM) ** -0.5),
        'rwkv_mu': unif((L, RWKV_COLS), 0.0, 1.0),
        'rwkv_w0': unif((L, RWKV_WIDTH), -6.0, -1.0),
        'rwkv_w2': nrm((L, DECAY_LORA, RWKV_WIDTH), 0.5 * DECAY_LORA ** -0.5),
        'rwkv_a0': nrm((L, RWKV_WIDTH), 0.1),
        'rwkv_a2': nrm((L, AAA_LORA, RWKV_WIDTH), 0.5 * AAA_LORA ** -0.5),
        'rwkv_g2': nrm((L, GATE_LORA, RWKV_WIDTH), GATE_LORA ** -0.5),
        'rwkv_k_k': 0.85 + nrm((L, RWKV_WIDTH), 0.05),
        'rwkv_k_a': gain((L, RWKV_WIDTH)),
        'rwkv_r_k': nrm((L, RWKV_HEADS, RWKV_HEAD_DIM), 0.1),
        'rwkv_ln_w': gain((L, RWKV_WIDTH)),
        'rwkv_ln_b': nrm((L, RWKV_WIDTH), 0.02),
        'rwkv_w_o': nrm((L, RWKV_WIDTH, D_MODEL), RWKV_WIDTH ** -0.5),
        'rwkv_v1': nrm((Lv, D_MODEL, MV_LORA), D_MODEL ** -0.5),
        'rwkv_v_mu': unif((Lv, MV_LORA), 0.0, 1.0),
        'rwkv_v0': nrm((Lv, RWKV_WIDTH), 0.1),
        'rwkv_v2': nrm((Lv, MV_LORA, RWKV_WIDTH), 0.5 * MV_LORA ** -0.5),
        'conv_w': nrm((L, CONV_K, CONV_WIDTH), CONV_K ** -0.5),
        'conv_w_o': nrm((L, CONV_WIDTH, D_MODEL), CONV_WIDTH ** -0.5),
        'w_out': nrm((L, D_MODEL, D_MODEL), D_MODEL ** -0.5),
        'mlp_norm': gain((L, D_MODEL)),
        'w_up': nrm((L, D_MODEL, D_FF), D_MODEL ** -0.5),
        'w_down': nrm((L, D_FF, D_MODEL), D_FF ** -0.5),
    }


def reference(x, positions, attn_norm, w_in, mla_q_a_norm, mla_wq_b, mla_kv_a_norm, mla_wkv_b,
              mla_q_norm, mla_k_norm, mla_w_o, rwkv_mu, rwkv_w0, rwkv_w2, rwkv_a0, rwkv_a2, rwkv_g2,
              rwkv_k_k, rwkv_k_a, rwkv_r_k, rwkv_ln_w, rwkv_ln_b, rwkv_w_o, rwkv_v1, rwkv_v_mu,
              rwkv_v0, rwkv_v2, conv_w, conv_w_o, w_out, mlp_norm, w_up, w_down):
    bsz, seq, _ = x.shape
    v_first = None
    for l in range(DEPTH):
        h = rms_norm(x, attn_norm[l])
        proj = h @ w_in[l]
        gate_cols, mla_cols, rwkv_cols, conv_cols = _split(proj, [GATE_COLS, MLA_COLS, RWKV_COLS, CONV_COLS])
        o_a = mla_branch(mla_cols, positions, mla_q_a_norm[l], mla_wq_b[l], mla_kv_a_norm[l],
                         mla_wkv_b[l], mla_q_norm[l], mla_k_norm[l]) @ mla_w_o[l]
        vres = None if l == 0 else (rwkv_v1[l - 1], rwkv_v_mu[l - 1], rwkv_v0[l - 1], rwkv_v2[l - 1])
        o_b, v_first = rwkv7_branch(rwkv_cols, h, v_first, rwkv_mu[l], rwkv_w0[l], rwkv_w2[l], rwkv_a0[l],
                                    rwkv_a2[l], rwkv_g2[l], rwkv_k_k[l], rwkv_k_a[l], rwkv_r_k[l],
                                    rwkv_ln_w[l], rwkv_ln_b[l], vres)
        o_b = o_b @ rwkv_w_o[l]
        o_c = short_conv_branch(conv_cols, conv_w[l]) @ conv_w_o[l]
        g = jax.nn.sigmoid(gate_cols).reshape(bsz, seq, N_BRANCH, D_MODEL)
        merged = g[:, :, 0] * o_a + g[:, :, 1] * o_b + g[:, :, 2] * o_c
        x = x + merged @ w_out[l]
        h2 = rms_norm(x, mlp_norm[l])
        x = x + jnp.square(jax.nn.relu(h2 @ w_up[l])) @ w_down[l]
    return x
```

```python
import numpy as np
import concourse.bass as bass
import concourse.mybir as mybir
from concourse.bass_utils import run_bass_kernel_spmd

F32 = mybir.dt.float32
BF16 = mybir.dt.bfloat16
I32 = mybir.dt.int32
AF = mybir.ActivationFunctionType
ALU = mybir.AluOpType
AX = mybir.AxisListType
DTB = {F32: 4, BF16: 2, I32: 4}


class V:
    def __init__(self, ap, arena, s, e):
        self.ap, self.arena, self.s, self.e = ap, arena, s, e

    def __getitem__(self, k):
        return V(self.ap[k], self.arena, self.s, self.e)

    def re(self, pat, **kw):
        return V(self.ap.rearrange(pat, **kw), self.arena, self.s, self.e)

    def bc(self, shape):
        return V(self.ap.to_broadcast(shape), self.arena, self.s, self.e)

    def unsq(self, ax):
        return V(self.ap.unsqueeze(ax), self.arena, self.s, self.e)

    def sub(self, s, e):
        return V(self.ap, self.arena, self.s + s, self.s + e)


class Prog:
    ENGS = ("pe", "act", "dve", "pool", "sp")
    NPOOL = 12

    def __init__(self, nc, sbuf_bytes=196608, same_engine_sync=True):
        self.nc = nc
        self.same = same_engine_sync
        self.ops = {e: [] for e in self.ENGS}
        self.cnt = {e: 0 for e in self.ENGS}
        self.seen = {e: {} for e in self.ENGS}
        self.recs = {}
        self.sb_off = 0
        self.sb_bytes = sbuf_bytes
        self.dma_n = {"sp": 0, "pool": 0}
        self.nwaits = 0
        self.ninstr = 0

    def setup(self, sb, ps, sems, dsems):
        self.sb, self.ps = sb, ps
        self.sem = sems
        self.dsem = dsems

    def mark(self):
        return self.sb_off

    def release(self, m):
        self.sb_off = m

    def tile(self, shape, dt, name=""):
        n = int(np.prod(shape[1:])) * DTB[dt]
        n4 = (n + 3) // 4 * 4
        s = self.sb_off
        assert s + n4 <= self.sb_bytes, f"SBUF overflow at {name}: {s}+{n4}"
        self.sb_off += n4
        self.hwm = max(getattr(self, 'hwm', 0), self.sb_off)
        ap = self.sb[0:shape[0], s // 4:(s + n4) // 4]
        if dt != F32:
            ap = ap.bitcast(dt)
        ap = ap[:, 0:int(np.prod(shape[1:]))]
        if len(shape) > 2:
            names = " ".join(f"d{i}" for i in range(len(shape) - 1))
            ap = ap.rearrange(f"p ({names}) -> p {names}", **{f"d{i}": shape[i + 1] for i in range(len(shape) - 1)})
        return V(ap, "sb", s, s + n4)

    def psum(self, bank, shape, dt=F32, off=0):
        n = int(np.prod(shape[1:])) * DTB[dt]
        s = bank * 2048 + off
        assert off + n <= 2048 * (8 - bank)
        ap = self.ps[0:shape[0], s // 4:(s + n + 3) // 4]
        if dt != F32:
            ap = ap.bitcast(dt)
        ap = ap[:, 0:int(np.prod(shape[1:]))]
        if len(shape) > 2:
            names = " ".join(f"d{i}" for i in range(len(shape) - 1))
            ap = ap.rearrange(f"p ({names}) -> p {names}", **{f"d{i}": shape[i + 1] for i in range(len(shape) - 1)})
        return V(ap, "ps", (s // 2048) * 2048, ((s + n + 2047) // 2048) * 2048)

    def dram(self, ap, name, s=0, e=1):
        return V(ap, "dram_" + name, s, e)

    def _deps(self, reads, writes):
        toks = []
        for v in reads:
            for r in self.recs.get(v.arena, []):
                if r[4] and r[0] < v.e and v.s < r[1]:
                    toks.append(r[2:4])
        for v in writes:
            for r in self.recs.get(v.arena, []):
                if r[0] < v.e and v.s < r[1]:
                    toks.append(r[2:4])
        return toks

    def _record(self, reads, writes, tok):
        for v in writes:
            L = self.recs.setdefault(v.arena, [])
            L[:] = [r for r in L if not (v.s <= r[0] and r[1] <= v.e)]
            L.append([v.s, v.e, tok[0], tok[1], True])
        for v in reads:
            L = self.recs.setdefault(v.arena, [])
            L[:] = [r for r in L if not ((not r[4]) and r[2] == tok[0] and v.s <= r[0] and r[1] <= v.e)]
            L.append([v.s, v.e, tok[0], tok[1], False])

    def _waits(self, eng, toks, skip_sem=None):
        need = {}
        for sem, val in toks:
            if sem is skip_sem:
                continue
            if self.seen[eng].get(sem, 0) < val:
                need[sem] = max(need.get(sem, 0), val)
        for sem, val in need.items():
            self.seen[eng][sem] = val
        self.nwaits += len(need)
        return list(need.items())

    def op(self, eng, fn, outs, ins):
        outs = [o for o in outs if o is not None]
        ins = [i for i in ins if isinstance(i, V)]
        toks = self._deps(ins, outs)
        sem = self.sem[eng]
        skip = sem if (eng == "pe" or not self.same) else None
        waits = self._waits(eng, toks, skip_sem=skip)
        self.cnt[eng] += 1
        seq = self.cnt[eng]
        self.seen[eng][sem] = max(self.seen[eng].get(sem, 0), 0)
        self.ops[eng].append((waits, fn, sem, 1))
        if getattr(self, "trace_iv", None) and any(v.arena == "sb" and v.s < self.trace_iv[1] and self.trace_iv[0] < v.e for v in ins + outs):
            names = {id(v): k for k, v in self.sem.items()}
            print("TRACE", eng, "seq", seq, "waits", [(names.get(id(s_), "dma"), val) for s_, val in waits], "toks", [(names.get(id(s_), "dma"), val) for s_, val in toks][:8])
        self._record(ins, outs, (sem, seq))
        self.ninstr += 1

    def dma(self, out, in_, q="sp", **kw):
        i = self.dma_n[q]
        self.dma_n[q] += 1
        pool = self.dsem[q]
        k = i % len(pool)
        sem = pool[k]
        val = 16 * (i // len(pool) + 1)
        toks = self._deps([in_], [out])
        toks.append((sem, val - 16))
        waits = self._waits(q, toks)
        oa, ia = out.ap, in_.ap
        self.ops[q].append((waits, lambda e: e.dma_start(out=oa, in_=ia, **kw), sem, 16))
        self._record([in_], [out], (sem, val))
        self.ninstr += 1

    def final_wait(self, eng="sp"):
        toks = []
        for q in ("sp", "pool"):
            n = self.dma_n[q]
            pool = self.dsem[q]
            for k in range(len(pool)):
                cntk = (n - k + len(pool) - 1) // len(pool) if n > k else 0
                if cntk:
                    toks.append((pool[k], 16 * cntk))
        for e in self.ENGS:
            if self.cnt[e]:
                toks.append((self.sem[e], self.cnt[e]))
        waits = self._waits(eng, toks)
        self.ops[eng].append((waits, None, None, 0))

    def replay(self, block):
        nc = self.nc

        def run(engobj, lst):
            for waits, fn, sem, inc in lst:
                for s, v in waits:
                    engobj.wait_ge(s, v)
                if fn is not None:
                    fn(engobj).then_inc(sem, inc)

        block.sync(lambda e: run(e, self.ops["sp"]))
        block.gpsimd(lambda e: run(e, self.ops["pool"]))
        block.tensor(lambda e: run(e, self.ops["pe"]))
        block.scalar(lambda e: run(e, self.ops["act"]))
        block.vector(lambda e: run(e, self.ops["dve"]))

    def mm(self, out, lhsT, rhs, start=True, stop=True):
        o, l, r = out.ap, lhsT.ap, rhs.ap
        self.op("pe", lambda e: e.matmul(o, l, r, start=start, stop=stop), [out], [lhsT, rhs] + ([] if start else [out]))

    def tr(self, out, in_, ident):
        o, i, d = out.ap, in_.ap, ident.ap
        self.op("pe", lambda e: e.transpose(o, i, d), [out], [in_, ident])

    def act(self, out, in_, func, bias=None, scale=None, accum=None, eng="act"):
        o, i = out.ap, in_.ap
        kw = {}
        if bias is not None:
            kw["bias"] = bias.ap if isinstance(bias, V) else bias
        if scale is not None:
            kw["scale"] = scale.ap if isinstance(scale, V) else scale
        if accum is not None:
            kw["accum_out"] = accum.ap
        self.op(eng, lambda e: e.activation(o, i, func, **kw), [out, accum], [in_, bias, scale])

    def tt(self, out, a, b, op, eng="dve"):
        o, x, y = out.ap, a.ap, b.ap
        self.op(eng, lambda e: e.tensor_tensor(o, x, y, op), [out], [a, b])

    def ts(self, out, a, s1, op0, s2=None, op1=None, accum=None, eng="dve"):
        o, x = out.ap, a.ap
        c1 = s1.ap if isinstance(s1, V) else s1
        c2 = s2.ap if isinstance(s2, V) else s2
        kw = {}
        if op1 is not None:
            kw["op1"] = op1
        if accum is not None:
            kw["accum_out"] = accum.ap
        self.op(eng, lambda e: e.tensor_scalar(o, x, c1, c2, op0, **kw), [out, accum], [a, s1, s2])

    def stt(self, out, a, s, b, op0, op1, eng="dve"):
        o, x, y = out.ap, a.ap, b.ap
        c = s.ap if isinstance(s, V) else s
        self.op(eng, lambda e: e.scalar_tensor_tensor(o, x, c, y, op0, op1), [out], [a, s, b])

    def cp(self, out, in_, eng="dve"):
        o, i = out.ap, in_.ap
        if eng == "act":
            self.op(eng, lambda e: e.copy(o, i), [out], [in_])
        else:
            self.op(eng, lambda e: e.tensor_copy(o, i), [out], [in_])

    def memset(self, out, val, eng="dve"):
        o = out.ap
        self.op(eng, lambda e: e.memset(o, val), [out], [])

    def red(self, out, in_, op=None, axis=None, eng="dve"):
        o, i = out.ap, in_.ap
        op = op or ALU.add
        axis = axis or AX.X
        self.op(eng, lambda e: e.tensor_reduce(o, i, axis, op), [out], [in_])

    def recip(self, out, in_):
        o, i = out.ap, in_.ap
        self.op("dve", lambda e: e.reciprocal(o, i), [out], [in_])
from contextlib import ExitStack

D = 1024
SBUF_BYTES = 211968
S = 2048
NT = 16
DEPTH = 2
IN_COLS = 5536
EPS = 1e-6
GN_EPS = 64e-5
PI = 3.141592653589793

PV_GQA, PV_GKVA, PV_MU, PV_KK, PV_KA, PV_A0, PV_RK, PV_V0, PV_VMU, PV_CW = 0, 3, 5, 13, 15, 17, 19, 21, 23, 24
PV_N = 30
BV_GQ, BV_GK, BV_W0, BV_LNW, BV_LNB = 0, 96, 192, 448, 704
BV_N = 960


def host_consts():
    i = np.arange(128)
    idf = np.eye(128, dtype=np.float32)
    strict = (i[:, None] < i[None, :]).astype(np.float32)
    incl = (i[:, None] <= i[None, :]).astype(np.float32)
    mask2 = np.concatenate([strict, incl], 1)
    maskT = (i[:, None] > i[None, :]).astype(np.float32)
    tri = np.concatenate([incl, strict], 1)
    blk = (i[:, None] // 64 == i[None, :] // 64).astype(np.float32)
    ind = np.zeros((128, 8), np.float32)
    for c_ in range(2):
        ind[i, c_ * 4 + 2 * c_ + i // 64] = 1.0
    freq = (10000.0 ** (-(np.arange(16, dtype=np.float32) * 2.0 / 32))).astype(np.float32)
    freq = np.broadcast_to(freq[None, :], (128, 16)).copy()
    c = np.concatenate([idf, mask2, maskT, tri, blk, ind, freq], 1).astype(np.float32)
    return np.ascontiguousarray(c)


C_IDF, C_MASK2, C_MASKT, C_TRI, C_BLK, C_IND, C_FREQ, C_N = 0, 128, 384, 512, 768, 896, 904, 920


def host_pvec(inp, l):
    pv = np.zeros((128, PV_N), np.float32)

    def put(col, vec):
        n = vec.shape[0] // 128
        pv[:, col:col + n] = vec.reshape(n, 128).T

    put(PV_GQA, inp["mla_q_a_norm"][l])
    put(PV_GKVA, inp["mla_kv_a_norm"][l])
    put(PV_MU, inp["rwkv_mu"][l])
    put(PV_KK, inp["rwkv_k_k"][l])
    put(PV_KA, inp["rwkv_k_a"][l])
    put(PV_A0, inp["rwkv_a0"][l])
    put(PV_RK, inp["rwkv_r_k"][l].reshape(256))
    if l > 0:
        put(PV_V0, inp["rwkv_v0"][l - 1])
        pv[0:32, PV_VMU] = inp["rwkv_v_mu"][l - 1]
    cw = inp["conv_w"][l]
    for fc in range(2):
        for j in range(3):
            pv[:, PV_CW + fc * 3 + j] = cw[j, fc * 128:(fc + 1) * 128]
    return pv


def host_bvec(inp, l):
    return np.concatenate([inp["mla_q_norm"][l], inp["mla_k_norm"][l],
                           inp["rwkv_w0"][l], inp["rwkv_ln_w"][l], inp["rwkv_ln_b"][l]]).astype(np.float32)[None, :]


WNAMES = ["w_in", "mla_wq_b", "mla_wkv_b", "mla_w_o", "rwkv_w2", "rwkv_a2", "rwkv_g2", "rwkv_w_o", "rwkv_v1",
          "rwkv_v2", "conv_w_o", "w_out", "w_up", "w_down"]


def build(shapes, dbg=None, nlayers=DEPTH, phases=("mla", "conv", "rwkv", "merge", "mlp")):
    nc = bass.Bass("TRN2", target_bir_lowering=False)
    dr = {}
    for k, shp in shapes.items():
        dt = I32 if k == "pos" else F32
        dr[k] = nc.dram_tensor(k, list(shp), dt, kind="ExternalInput").ap()
    out_ap = nc.dram_tensor("out", [S, D], F32, kind="ExternalOutput").ap()
    xmid_ap = nc.dram_tensor("xmid", [S, D], F32, kind="Internal").ap()
    xl1_ap = nc.dram_tensor("xl1", [S, D], F32, kind="Internal").ap()
    vf_ap = nc.dram_tensor("vfirst", [128, 2, S], F32, kind="Internal").ap()
    dbg_ap = nc.dram_tensor("dbg", [128, 4096], F32, kind="ExternalOutput").ap() if dbg else None

    with ExitStack() as es:
        sb = es.enter_context(nc.sbuf_tensor("sb", [128, SBUF_BYTES // 4], F32))
        ps_t = es.enter_context(nc.psum_tensor("ps", [128, 4096], F32))
        import os as _os
        P = Prog(nc, sbuf_bytes=SBUF_BYTES, same_engine_sync=("NOSAMESYNC" not in _os.environ))
        sems = {e: es.enter_context(nc.semaphore("s_" + e)) for e in Prog.ENGS}
        dsems = {q: [es.enter_context(nc.semaphore(f"d_{q}{i}")) for i in range(Prog.NPOOL)] for q in ("sp", "pool")}
        P.setup(sb, ps_t, sems, dsems)
        block = es.enter_context(nc.Block())

        def DR(name, ap=None, s=0, e=1):
            return P.dram(dr[name] if ap is None else ap, name, s, e)

        OUT = lambda ap, s, e: P.dram(ap, "out", s, e)
        XMID = lambda ap, s, e: P.dram(ap, "xmid", s, e)
        XL1 = lambda ap, s, e: P.dram(ap, "xl1", s, e)

        def dump(v, cols):
            if dbg_ap is not None:
                P.dma(P.dram(dbg_ap[0:v.ap.shape[0], 0:cols], "dbg"), v)

        cst = P.tile([128, C_N], F32, "cst")
        P.dma(cst, DR("consts"))
        idf = cst[:, C_IDF:C_IDF + 128]
        mask2 = cst[:, C_MASK2:C_MASK2 + 256]
        maskT = cst[:, C_MASKT:C_MASKT + 128]
        tri = cst[:, C_TRI:C_TRI + 256]
        blk = cst[:, C_BLK:C_BLK + 128]
        ind = cst[:, C_IND:C_IND + 8]
        freq = cst[:, C_FREQ:C_FREQ + 16]
        idb = P.tile([128, 128], BF16, "idb")
        P.cp(idb, idf)
        trib = P.tile([128, 256], BF16, "trib")
        P.cp(trib, tri)
        blkb = P.tile([128, 128], BF16, "blkb")
        P.cp(blkb, blk)
        cmaskb = P.tile([128, 128], BF16, "cmaskb")
        P.cp(cmaskb, mask2[:, 128:256])
        onesb = P.tile([128, 64], BF16, "onesb")
        P.memset(onesb, 1.0)
        pos_i = P.tile([128, NT], I32, "pos_i")
        P.dma(pos_i, DR("pos"))
        pos_f = P.tile([128, NT], F32, "pos_f")
        P.cp(pos_f, pos_i)
        ang = P.tile([128, NT, 16], F32, "ang")
        P.tt(ang, pos_f.unsq(2).bc([128, NT, 16]), freq.unsq(1).bc([128, NT, 16]), ALU.mult)
        sinT = P.tile([128, NT, 16], F32, "sinT")
        cosT = P.tile([128, NT, 16], F32, "cosT")
        tmpa = P.tile([128, NT, 16], F32, "tmpa")
        tmpk = P.tile([128, NT, 16], F32, "tmpk")
        tmpi = P.tile([128, NT, 16], I32, "tmpi")
        eps_t = P.tile([128, 1], F32, "eps_t")
        P.memset(eps_t, EPS)
        gneps_t = P.tile([128, 1], F32, "gneps_t")
        P.memset(gneps_t, GN_EPS)

        def sin_table(dst, shift):
            P.ts(tmpa, ang, shift, ALU.add)
            P.ts(tmpk, tmpa, 1.0 / (2 * PI), ALU.mult)
            P.cp(tmpi, tmpk)
            P.cp(tmpk, tmpi)
            P.stt(tmpa, tmpk, -6.28125, tmpa, ALU.mult, ALU.add)
            P.stt(tmpa, tmpk, -(2 * PI - 6.28125), tmpa, ALU.mult, ALU.add)
            P.ts(tmpk, tmpa, PI, ALU.is_gt, 2 * PI, ALU.mult)
            P.tt(tmpa, tmpa, tmpk, ALU.subtract)
            P.ts(tmpk, tmpa, -PI, ALU.is_lt, 2 * PI, ALU.mult)
            P.tt(tmpa, tmpa, tmpk, ALU.add)
            P.ts(tmpa, tmpa, PI, ALU.min, -PI, ALU.max)
            P.act(dst, tmpa, AF.Sin)

        sin_table(sinT, 0.0)
        sin_table(cosT, PI / 2)

        def rsqrt(out, in_, scale, bias_t):
            P.act(out, in_, AF.Sqrt, bias=bias_t, scale=scale)
            P.recip(out, out)

        hT = P.tile([128, 8, S], BF16, "hT")
        base_mark = P.mark()

        def wload(dst, name, l, rows=None, cols=None, pat=None, **kw):
            ap = dr[name][l]
            if rows is not None:
                ap = ap[rows[0]:rows[1]]
            if cols is not None:
                ap = ap[:, cols[0]:cols[1]]
            if pat:
                ap = ap.rearrange(pat, **kw)
            P.dma(dst, P.dram(ap, name), q="pool")

        def norm_setup(gname, l):
            gb = P.tile([128, D], F32, "gb")
            P.dma(gb, P.dram(dr[gname][l:l + 1, :].partition_broadcast(128), gname))
            junk = P.tile([128, D], F32, "junk")
            hb = [P.tile([128, D], BF16, f"hb{i}") for i in range(2)]
            ss = P.tile([128, NT], F32, "ss")
            P.memset(ss, 0.0)
            return gb, junk, hb, ss

        def norm_tile(i, x_, st_, psbank):
            gb, junk, hb, ss = st_
            P.act(junk, x_, AF.Square, accum=ss[:, i:i + 1])
            rsqrt(ss[:, i:i + 1], ss[:, i:i + 1], 1.0 / D, eps_t)
            h_ = hb[i % 2]
            P.stt(h_, x_, ss[:, i:i + 1], gb, ALU.mult, ALU.mult)
            pt = P.psum(psbank, [128, 8, 128], BF16)
            for c in range(8):
                P.tr(pt[:, c, :], h_[:, c * 128:(c + 1) * 128], idb)
            P.cp(hT[:, :, i * 128:(i + 1) * 128], pt, eng="act")

        def norm_phase(xsrc, gname, l):
            m0 = P.mark()
            st_ = norm_setup(gname, l)
            xt = [P.tile([128, D], F32, f"xt{i}") for i in range(2)]
            for i in range(NT):
                x_ = xt[i % 2]
                P.dma(x_, xsrc(i))
                norm_tile(i, x_, st_, i % 2)
            P.release(m0)

        def rwkv_phase(l, pv, bvt, ygT):
            import os
            STOP = int(os.environ.get('RWKV_STOP', '99'))
            PL = os.environ.get('PLENG', 'pool')
            DBGPT = os.environ.get('DBGPT', '') if dbg == 'pt' else ''
            dbgt = P.tile([128, 512], F32, "dbgt") if DBGPT else None
            dbg_done = []

            def dbgcopy(name, v, ncols):
                if name == DBGPT and not dbg_done:
                    dbg_done.append(1)
                    P.memset(dbgt, 0.0)
                    P.cp(dbgt[0:v.ap.shape[0], 0:ncols] if v.ap.base_partition() == 0 else dbgt[v.ap.base_partition():v.ap.base_partition() + v.ap.shape[0], 0:ncols], v)
                    dump(dbgt, 512)
            Wr = P.tile([128, 8, 1024], BF16, "Wr")
            wload(Wr, "w_in", l, cols=(3744, 4768), pat="(kc p) n -> p kc n", p=128)
            w2b = P.tile([64, 256], BF16, "w2b")
            wload(w2b, "rwkv_w2", l)
            a2b = P.tile([128, 256], BF16, "a2b")
            wload(a2b[64:128, :], "rwkv_a2", l)
            g2b = P.tile([128, 256], BF16, "g2b")
            wload(g2b, "rwkv_g2", l)
            if l > 0:
                v1b = P.tile([128, 8, 32], BF16, "v1b")
                wload(v1b, "rwkv_v1", l - 1, pat="(kc p) n -> p kc n", p=128)
                v2b = P.tile([32, 256], BF16, "v2b")
                wload(v2b, "rwkv_v2", l - 1)
                carv = P.tile([32, 1], F32, "carv")
                P.memset(carv, 0.0)
            carry = P.tile([128, 8], F32, "carry")
            P.memset(carry, 0.0)
            Hm = P.tile([128, 2, 128], F32, "Hm")
            Hb = P.tile([128, 2, 128], BF16, "Hb")
            P.memset(Hm, 0.0)
            P.memset(Hb, 0.0)
            raws = [P.tile([128, 513], F32, f"raw{j_}") for j_ in range(2)]
            raw = raws[0]
            mixed = [P.tile([128, 512], F32, f"mixed{m}") for m in range(8)]
            tanhb = P.tile([64, 512], BF16, "tanhb")
            xab = P.tile([128, 512], BF16, "xab")
            sgb = P.tile([128, 512], BF16, "sgb")
            aT = [P.tile([128, 512], F32, f"aT{c}") for c in range(2)]
            kk = [P.tile([128, 512], F32, f"kk{c}") for c in range(2)]
            bT = [P.tile([128, 512], F32, f"bT{c}") for c in range(2)]
            kp = [P.tile([128, 512], F32, f"kp{c}") for c in range(2)]
            if "PADRKR" in os.environ:
                P.tile([128, 2048], F32, "padrkr")
            rkr = [P.tile([128, 512], F32, f"rkr{c}") for c in range(2)]
            print("rkr at", rkr[0].s, rkr[1].s, "mixed4 at", mixed[4].s, "kp", kp[0].s)
            tAs = [P.tile([128, 512], F32, f"tA{j_}") for j_ in range(2)]
            tA = tAs[0]
            difs = [P.tile([128, 512], F32, f"dif{j_}") for j_ in range(2)]
            sphs = [P.tile([128, 512], BF16, f"sph{j_}") for j_ in range(2)]
            spls = [P.tile([128, 512], BF16, f"spl{j_}") for j_ in range(2)]
            vh = [P.tile([128, 512], BF16, f"vh{c}") for c in range(2)]
            vl = [P.tile([128, 512], BF16, f"vl{c}") for c in range(2)]
            bh = [P.tile([128, 512], BF16, f"bh{c}") for c in range(2)]
            bl = [P.tile([128, 512], BF16, f"bl{c}") for c in range(2)]
            C0 = 0.6065306597126334

            def split(x, hi, lo, eng="dve"):
                P.cp(hi, x, eng=eng)
                P.tt(lo, x, hi, ALU.subtract, eng=eng)
            tBs = [P.tile([128, 512], F32, f"tB{j_}") for j_ in range(2)]
            tB = tBs[0]
            def two(shape, dt, nm):
                return [P.tile(shape, dt, f"{nm}{j}") for j in range(2)]
            def one(shape, dt, nm):
                t_ = P.tile(shape, dt, nm)
                return [t_, t_]
            ld_tm = two([128, 256], F32, "ld_tm")
            ldh_ = two([128, 256], BF16, "ldh")
            ldl_ = two([128, 256], BF16, "ldl")
            g_tm = two([128, 256], F32, "g_tm")
            E = two([128, 2, 384], F32, "E")
            AR = two([128, 4, 256], BF16, "AR")
            BTh = two([128, 4, 128], BF16, "BTh")
            KTh = two([128, 4, 128], BF16, "KTh")
            for j_ in range(2):
                if "NOMS1" not in os.environ:
                    P.memset(AR[j_], 0.0)
                if "NOMS2" not in os.environ:
                    P.memset(BTh[j_], 0.0, eng=PL)
                    P.memset(KTh[j_], 0.0, eng=PL)
            BT = two([128, 2, 128], BF16, "BT")
            KT = two([128, 2, 128], BF16, "KT")
            Btm = two([128, 256], BF16, "Btm")
            Ktm = two([128, 256], BF16, "Ktm")
            Vb = two([128, 256], BF16, "Vb")
            bon = two([128, 256], F32, "bon")
            AbT = two([128, 4, 256], BF16, "AbT")
            AkT = two([128, 4, 256], BF16, "AkT")
            Qp0 = two([128, 4, 128], BF16, "Qp0")
            Qb = [two([128, 4, 128], BF16, f"Qb{t_}") for t_ in range(2)]
            Qpb = [two([128, 4, 128], BF16, f"Qpb{t_}") for t_ in range(2)]
            Tt = two([128, 4, 128], BF16, "Tt")
            Tn = two([128, 4, 128], BF16, "Tn")
            Xb = one([128, 256], BF16, "Xb")
            Ub = one([128, 256], BF16, "Ub")
            ysb = one([128, 4, 64], F32, "ysb")
            ysq = one([128, 4, 64], F32, "ysq")
            yn = one([128, 4, 64], F32, "yn")
            ygb = one([128, 256], BF16, "ygb")
            st = two([128, 5, 4], F32, "st")
            tH = P.tile([128, 128], F32, "tH")
            if l > 0:
                rawv = raw[0:32, :]
                difv = tB[0:32, :]
                xvb = P.tile([32, 512], BF16, "xvb")
                vft = tA
                vg = rkr[0]
            idb4 = idb.unsq(1).bc([128, 4, 128])

            for g in range(4):
                tk = slice(g * 512, (g + 1) * 512)
                for m in range(8):
                    ps = P.psum(m % 2, [128, 512])
                    for kc in range(8):
                        P.mm(ps, Wr[:, kc, m * 128:(m + 1) * 128], hT[:, kc, tk], start=(kc == 0), stop=(kc == 7))
                    rw, df = raws[m % 2], difs[m % 2]
                    P.cp(rw[:, 0:1], carry[:, m:m + 1])
                    P.cp(rw[:, 1:513], ps, eng="act")
                    P.cp(carry[:, m:m + 1], rw[:, 512:513])
                    P.tt(df, rw[:, 0:512], rw[:, 1:513], ALU.subtract)
                    P.stt(mixed[m], df, pv[:, PV_MU + m:PV_MU + m + 1], rw[:, 1:513], ALU.mult, ALU.add)
                dbgcopy('mixed0', mixed[0], 512); dbgcopy('mixed6', mixed[6], 512); dbgcopy('mixed7', mixed[7], 512)
                r_, k_, v_ = mixed[0:2], mixed[2:4], mixed[4:6]
                if STOP <= 1: return
                if l == 0:
                    for c in range(2):
                        if "NOVF" not in os.environ:
                            P.dma(P.dram(vf_ap[:, c, tk], "vfirst", g * 2 + c, g * 2 + c + 1), v_[c])
                else:
                    psv = P.psum(0, [32, 512])
                    for kc in range(8):
                        P.mm(psv, v1b[:, kc, :], hT[:, kc, tk], start=(kc == 0), stop=(kc == 7))
                    P.cp(rawv[:, 0:1], carv)
                    P.cp(rawv[:, 1:513], psv, eng="act")
                    P.cp(carv, rawv[:, 512:513])
                    P.tt(difv, rawv[:, 0:512], rawv[:, 1:513], ALU.subtract)
                    P.stt(difv, difv, pv[0:32, PV_VMU:PV_VMU + 1], rawv[:, 1:513], ALU.mult, ALU.add)
                    P.cp(xvb, difv)
                    for c in range(2):
                        psg_ = P.psum(1, [128, 512])
                        P.mm(psg_, v2b[:, c * 128:(c + 1) * 128], xvb)
                        P.act(vg, psg_, AF.Sigmoid, bias=pv[:, PV_V0 + c:PV_V0 + c + 1])
                        P.dma(vft, P.dram(vf_ap[:, c, tk], "vfirst", g * 2 + c, g * 2 + c + 1))
                        P.tt(vft, vft, v_[c], ALU.subtract)
                        P.tt(vft, vft, vg, ALU.mult)
                        P.tt(v_[c], v_[c], vft, ALU.add)
                if STOP <= 2: return
                P.act(tanhb, mixed[6][0:64, :], AF.Tanh)
                P.cp(xab[64:128, :], mixed[6][64:128, :])
                P.act(sgb, mixed[7], AF.Sigmoid)
                for c in range(2):
                    psa = P.psum(c, [128, 512])
                    P.mm(psa, a2b[64:128, c * 128:(c + 1) * 128], xab[64:128, :])
                    P.act(aT[c], psa, AF.Sigmoid, bias=pv[:, PV_A0 + c:PV_A0 + c + 1])
                dbgcopy('aT0', aT[0], 512); dbgcopy('aT1', aT[1], 512); dbgcopy('tanhb', tanhb, 512)
                if STOP <= 3: return
                def elem(c):
                    tA, tB, sph, spl = tAs[c], tBs[c], sphs[c], spls[c]
                    P.ts(kk[c], k_[c], pv[:, PV_KK + c:PV_KK + c + 1], ALU.mult)
                    P.tt(tA, kk[c], kk[c], ALU.mult)
                    yield
                    pss = P.psum(c, [128, 512])
                    split(tA, sph, spl)
                    P.mm(pss, blkb, sph, start=True, stop=False)
                    P.mm(pss, blkb, spl, start=False, stop=True)
                    yield
                    P.act(tB, pss, AF.Sqrt)
                    yield
                    P.ts(tB, tB, 1e-12, ALU.max)
                    P.recip(tB, tB)
                    yield
                    P.tt(kk[c], kk[c], tB, ALU.mult)
                    P.tt(bT[c], kk[c], aT[c], ALU.mult)
                    yield
                    P.ts(tA, aT[c], -1.0, ALU.add, pv[:, PV_KA + c:PV_KA + c + 1], ALU.mult)
                    P.stt(kp[c], tA, 1.0, k_[c], ALU.add, ALU.mult)
                    yield
                    P.stt(rkr[c], r_[c], pv[:, PV_RK + c:PV_RK + c + 1], kp[c], ALU.mult, ALU.mult)
                    yield
                    psr = P.psum(c, [128, 512])
                    split(rkr[c], sph, spl)
                    P.mm(psr, blkb, sph, start=True, stop=False)
                    P.mm(psr, blkb, spl, start=False, stop=True)
                    yield
                    P.tt(rkr[c], psr, v_[c], ALU.mult)
                    yield
                    split(rkr[c], bh[c], bl[c])
                    yield
                    split(v_[c], vh[c], vl[c])
                    yield

                gens_ = [elem(0), elem(1)]
                while gens_:
                    for g_ in list(gens_):
                        try:
                            next(g_)
                        except StopIteration:
                            gens_.remove(g_)

                m2b = mask2.unsq(1).bc([128, 4, 256])
                mTb = maskT.unsq(1).bc([128, 4, 128])

                def prep_pieces(j):
                    i = g * 4 + j
                    pb_ = i % 2
                    tj = slice(j * 128, (j + 1) * 128)
                    ldh, ldl = ldh_[pb_], ldl_[pb_]

                    def p1():
                        psl = P.psum(1, [128, 256])
                        P.mm(psl, tanhb[:, tj], w2b)
                        P.tt(ld_tm[pb_], psl, bvt[:, BV_W0:BV_W0 + 256], ALU.add)
                        P.act(ld_tm[pb_], ld_tm[pb_], AF.Sigmoid)
                        yield
                        psg2 = P.psum(1, [128, 256], off=1024)
                        P.mm(psg2, sgb[:, tj], g2b)
                        P.cp(g_tm[pb_], psg2, eng="act")
                        yield
                        split(ld_tm[pb_], ldh, ldl)
                        yield

                    def p2():
                        psc = P.psum(0, [128, 2, 256])
                        for c in range(2):
                            P.mm(psc[:, c, :], ldh[:, c * 128:(c + 1) * 128], trib, start=True, stop=False)
                            P.mm(psc[:, c, :], ldl[:, c * 128:(c + 1) * 128], trib, start=False, stop=True)
                        P.act(E[pb_][:, :, 0:256], psc, AF.Exp, scale=-C0)
                        P.act(E[pb_][:, :, 256:384], psc[:, :, 0:128], AF.Exp, scale=C0)
                        yield

                    def p3():
                        for c in range(2):
                            P.tt(BT[pb_][:, c, :], bT[c][:, tj], E[pb_][:, c, 256:384], ALU.mult, eng=PL)
                            P.tt(KT[pb_][:, c, :], kp[c][:, tj], E[pb_][:, c, 256:384], ALU.mult, eng=PL)
                            for hh in range(2):
                                h, rows = 2 * c + hh, slice(64 * hh, 64 * hh + 64)
                                P.stt(AR[pb_][rows, h, 0:128], kk[c][rows, tj], -1.0, E[pb_][rows, c, 128:256], ALU.mult, ALU.mult)
                                P.tt(AR[pb_][rows, h, 128:256], r_[c][rows, tj], E[pb_][rows, c, 0:128], ALU.mult)
                                P.cp(BTh[pb_][rows, h, :], BT[pb_][rows, c, :], eng=PL)
                                P.cp(KTh[pb_][rows, h, :], KT[pb_][rows, c, :], eng=PL)
                                yield

                    def p4():
                        ptb = P.psum(2, [128, 4, 128], BF16)
                        ptv = P.psum(2, [128, 4, 128], BF16, off=1024)
                        for c in range(2):
                            P.tr(ptb[:, c, :], BT[pb_][:, c, :], idb)
                            P.tr(ptb[:, 2 + c, :], KT[pb_][:, c, :], idb)
                            P.tr(ptv[:, c, :], vh[c][:, tj], idb)
                            P.tr(ptv[:, 2 + c, :], vl[c][:, tj], idb)
                        P.cp(Btm[pb_], ptb[:, 0:2, :].re("p a b -> p (a b)"), eng="act")
                        P.cp(Ktm[pb_], ptb[:, 2:4, :].re("p a b -> p (a b)"), eng="act")
                        P.cp(Vb[pb_], ptv[:, 0:2, :].re("p a b -> p (a b)"), eng="act")
                        yield

                    def p5():
                        psb = P.psum(2, [128, 4, 128], BF16)
                        for c in range(2):
                            P.tr(psb[:, c, :], bh[c][:, tj], idb)
                            P.tr(psb[:, 2 + c, :], bl[c][:, tj], idb)
                        P.cp(bon[pb_], psb[:, 0:2, :].re("p a b -> p (a b)"))
                        P.tt(bon[pb_], bon[pb_], psb[:, 2:4, :].re("p a b -> p (a b)"), ALU.add)
                        yield

                    return [p1, p2, p3, p4, p5]

                def stage_a(j):
                    pb_ = (g * 4 + j) % 2
                    psAb = P.psum(4, [128, 4, 256])
                    psAk = P.psum(6, [128, 4, 256])
                    psN = P.psum(3, [128, 4, 128])
                    for h in range(4):
                        c = h // 2
                        P.mm(psAb[:, h, :], BTh[pb_][:, h, :], AR[pb_][:, h, :])
                        P.mm(psAk[:, h, :], KTh[pb_][:, h, :], AR[pb_][:, h, :])
                        P.mm(psN[:, h, :], AR[pb_][:, h, 0:128], BT[pb_][:, c, :])
                    P.tt(AbT[pb_], psAb, m2b, ALU.mult)
                    P.tt(Qp0[pb_], psN, mTb, ALU.mult)
                    P.tt(AkT[pb_], psAk, m2b, ALU.mult)

                def inverse_pair(js):
                    pbs = [(g * 4 + j) % 2 for j in js]
                    Qs = [AbT[pb_][:, :, 0:128] for pb_ in pbs]
                    Qps = [Qp0[pb_] for pb_ in pbs]
                    for t_, pb_ in enumerate(pbs):
                        P.tt(Tt[pb_], Qs[t_], idb4, ALU.add, eng=PL)
                        P.tt(Tn[pb_], Qps[t_], idb4, ALU.add, eng=PL)
                    for s_ in range(6):
                        lastst = (s_ == 5)
                        psQ = [P.psum(4 * t_, [128, 4, 128]) for t_ in range(2)]
                        psQp = [P.psum(4 * t_ + 1, [128, 4, 128]) for t_ in range(2)]
                        psT = [P.psum(4 * t_ + 2, [128, 4, 128]) for t_ in range(2)]
                        psTn = [P.psum(4 * t_ + 3, [128, 4, 128]) for t_ in range(2)]
                        for t_, pb_ in enumerate(pbs):
                            for h in range(4):
                                P.mm(psQ[t_][:, h, :], Qps[t_][:, h, :], Qs[t_][:, h, :])
                            if not lastst:
                                for h in range(4):
                                    P.mm(psQp[t_][:, h, :], Qs[t_][:, h, :], Qps[t_][:, h, :])
                        Qn = [Qb[pb_][s_ % 2] for pb_ in pbs]
                        Qpn = [Qpb[pb_][s_ % 2] for pb_ in pbs]
                        for t_, pb_ in enumerate(pbs):
                            P.cp(Qn[t_], psQ[t_], eng="act")
                            if not lastst:
                                P.cp(Qpn[t_], psQp[t_])
                        for t_, pb_ in enumerate(pbs):
                            for h in range(4):
                                P.mm(psT[t_][:, h, :], Tn[pb_][:, h, :], Qn[t_][:, h, :])
                            if not lastst:
                                for h in range(4):
                                    P.mm(psTn[t_][:, h, :], Tt[pb_][:, h, :], Qpn[t_][:, h, :])
                        for t_, pb_ in enumerate(pbs):
                            P.tt(Tt[pb_], Tt[pb_], psT[t_], ALU.add)
                            if not lastst:
                                P.tt(Tn[pb_], Tn[pb_], psTn[t_], ALU.add)
                        Qs, Qps = Qn, Qpn

                psY_of = {}

                def chain(j):
                    pb_ = (g * 4 + j) % 2
                    psX = P.psum(0, [128, 4, 64])
                    psU = P.psum(0, [128, 4, 64], off=1024)
                    psY = P.psum(1, [128, 4, 64])
                    psH = P.psum(1, [128, 2, 128], off=1024)
                    for h in range(4):
                        c = h // 2
                        hs = slice(h * 64, (h + 1) * 64)
                        P.mm(psX[:, h, :], AR[pb_][:, h, 0:128], Hb[:, c, 64 * (h % 2):64 * (h % 2) + 64], start=True, stop=False)
                        P.mm(psX[:, h, :], AkT[pb_][:, h, 0:128], Vb[pb_][:, hs], start=False, stop=True)
                    yield
                    P.cp(Xb[pb_], psX.re("p a b -> p (a b)"), eng="act")
                    yield
                    for h in range(4):
                        hs = slice(h * 64, (h + 1) * 64)
                        P.mm(psU[:, h, :], Tt[pb_][:, h, :], Xb[pb_][:, hs])
                    yield
                    P.cp(Ub[pb_], psU.re("p a b -> p (a b)"), eng="act")
                    yield
                    for c in range(2):
                        P.mm(psH[:, c, :], Btm[pb_][:, c * 128:(c + 1) * 128], Ub[pb_][:, c * 128:(c + 1) * 128], start=True, stop=False)
                        P.mm(psH[:, c, :], Ktm[pb_][:, c * 128:(c + 1) * 128], Vb[pb_][:, c * 128:(c + 1) * 128], start=False, stop=True)
                    for h in range(4):
                        c = h // 2
                        hs = slice(h * 64, (h + 1) * 64)
                        P.mm(psY[:, h, :], AR[pb_][:, h, 128:256], Hb[:, c, 64 * (h % 2):64 * (h % 2) + 64], start=True, stop=False)
                        P.mm(psY[:, h, :], AbT[pb_][:, h, 128:256], Ub[pb_][:, hs], start=False, stop=False)
                        P.mm(psY[:, h, :], AkT[pb_][:, h, 128:256], Vb[pb_][:, hs], start=False, stop=True)
                    yield
                    for c in range(2):
                        P.tt(tH, psH[:, c, :], Hm[:, c, :], ALU.add)
                        P.ts(Hm[:, c, :], tH, E[pb_][:, c, 127:128], ALU.mult)
                        P.cp(Hb[:, c, :], Hm[:, c, :])
                        yield
                    psY_of[j] = psY

                def epilogue(j):
                    psY = psY_of[j]
                    i = g * 4 + j
                    pb_ = i % 2
                    tl = slice(i * 128, (i + 1) * 128)
                    y_, q_, n_, s_t = ysb[pb_], ysq[pb_], yn[pb_], st[pb_]
                    P.cp(y_, psY)
                    yield
                    P.act(q_, y_, AF.Square)
                    P.red(s_t[:, 0, :], y_)
                    P.red(s_t[:, 1, :], q_)
                    yield
                    P.ts(s_t[:, 2, :], s_t[:, 0, :], 1.0 / 64, ALU.mult)
                    P.tt(s_t[:, 3, :], s_t[:, 2, :], s_t[:, 2, :], ALU.mult)
                    P.stt(s_t[:, 4, :], s_t[:, 1, :], 1.0 / 64, s_t[:, 3, :], ALU.mult, ALU.subtract)
                    rsqrt(s_t[:, 4, :], s_t[:, 4, :], 1.0, gneps_t)
                    yield
                    P.tt(n_, y_, s_t[:, 2, :].unsq(2).bc([128, 4, 64]), ALU.subtract)
                    P.tt(n_, n_, s_t[:, 4, :].unsq(2).bc([128, 4, 64]), ALU.mult)
                    yield
                    nf = n_.re("p a b -> p (a b)")
                    P.tt(nf, nf, bvt[:, BV_LNW:BV_LNW + 256], ALU.mult)
                    P.tt(nf, nf, bvt[:, BV_LNB:BV_LNB + 256], ALU.add)
                    P.tt(nf, nf, bon[pb_], ALU.add)
                    P.tt(ygb[pb_], nf, g_tm[pb_], ALU.mult)
                    yield
                    pty = P.psum(2, [128, 2, 128], BF16)
                    for c in range(2):
                        P.tr(pty[:, c, :], ygb[pb_][:, c * 128:(c + 1) * 128], idb)
                    P.cp(ygT[:, :, tl], pty, eng="act")
                    yield

                def run_all(*gens):
                    gens = list(gens)
                    while gens:
                        for g_ in list(gens):
                            try:
                                next(g_)
                            except StopIteration:
                                gens.remove(g_)

                def prep_gen(j):
                    for p_ in prep_pieces(j):
                        yield from p_()

                def ep_gen(j):
                    yield from epilogue(j)

                for jp in (0, 2):
                    run_all(prep_gen(jp), prep_gen(jp + 1))
                    for j in (jp, jp + 1):
                        stage_a(j)
                    inverse_pair((jp, jp + 1))
                    run_all(chain(jp))
                    run_all(chain(jp + 1), ep_gen(jp))
                    run_all(ep_gen(jp + 1))
        for l in range(nlayers):
            last = (l == DEPTH - 1)
            P.release(base_mark)
            pv = P.tile([128, PV_N], F32, "pv")
            P.dma(pv, DR("pvec", dr["pvec"][l]))
            bvt = P.tile([128, BV_N], F32, "bvt")
            P.dma(bvt, DR("bvec", dr["bvec"][l].partition_broadcast(128)))
            layer_mark0 = P.mark()
            ygT = P.tile([128, 2, S], BF16, "ygT")
            cvT = P.tile([128, 2, S], BF16, "cvT")
            layer_mark = P.mark()

            if l == 0:
                xsrc = lambda i: DR("x", dr["x"][i * 128:(i + 1) * 128, :], i, i + 1)
            else:
                xsrc = lambda i: XL1(xl1_ap[i * 128:(i + 1) * 128, :], i, i + 1)
            xdst = (lambda i: OUT(out_ap[i * 128:(i + 1) * 128, :], i, i + 1)) if last else \
                   (lambda i: XL1(xl1_ap[i * 128:(i + 1) * 128, :], i, i + 1))
            norm_phase(xsrc, "attn_norm", l)
            if dbg == f"hT{l}":
                tmpd = P.tile([128, 2048], F32, "tmpd")
                P.cp(tmpd, hT[:, 0, :])
                dump(tmpd, 2048)

            if "rwkv" in phases:
                P.release(layer_mark)
                rwkv_phase(l, pv, bvt, ygT)
                if dbg == f"ygT{l}":
                    import os
                    if "DUMPALIAS" in os.environ:
                        P.release(layer_mark)
                    else:
                        P.release(layer_mark + 20480)
                    tmpd = P.tile([128, 4096], F32, "tmpd")
                    P.cp(tmpd[:, 0:2048], ygT[:, 0, :])
                    P.cp(tmpd[:, 2048:4096], ygT[:, 1, :])
                    dump(tmpd, 4096)

            if "conv" in phases:
                P.release(layer_mark)
                Wcv = P.tile([128, 8, 768], BF16, "Wcv")
                wload(Wcv, "w_in", l, cols=(4768, 5536), pat="(kc p) n -> p kc n", p=128)
                bg = P.tile([128, S], F32, "bg")
                u = P.tile([128, S + 2], F32, "u")
                xc = P.tile([128, S], F32, "xc")
                yv = P.tile([128, S], F32, "yv")
                P.memset(u[:, 0:2], 0.0)
                for fc in range(2):
                    for part, dst in ((0, bg), (1, u[:, 2:S + 2]), (2, xc)):
                        for tc in range(4):
                            ps = P.psum((part * 4 + tc) % 4, [128, 512])
                            c0 = part * 256 + fc * 128
                            for kc in range(8):
                                P.mm(ps, Wcv[:, kc, c0:c0 + 128], hT[:, kc, tc * 512:(tc + 1) * 512],
                                     start=(kc == 0), stop=(kc == 7))
                            P.cp(dst[:, tc * 512:(tc + 1) * 512], ps, eng="act")
                    P.tt(u[:, 2:S + 2], u[:, 2:S + 2], xc, ALU.mult)
                    cw = lambda j: pv[:, PV_CW + fc * 3 + j:PV_CW + fc * 3 + j + 1]
                    P.ts(yv, u[:, 2:S + 2], cw(2), ALU.mult)
                    P.stt(yv, u[:, 1:S + 1], cw(1), yv, ALU.mult, ALU.add)
                    P.stt(yv, u[:, 0:S], cw(0), yv, ALU.mult, ALU.add)
                    P.tt(cvT[:, fc, :], yv, bg, ALU.mult)
                if dbg == f"cvT{l}":
                    P.release(layer_mark)
                    tmpd = P.tile([128, 4096], F32, "tmpd")
                    P.cp(tmpd[:, 0:2048], cvT[:, 0, :])
                    P.cp(tmpd[:, 2048:4096], cvT[:, 1, :])
                    dump(tmpd, 4096)

            P.release(layer_mark)
            oT = P.tile([64, 8, S], BF16, "oT")
            layer_mark2 = P.mark()
            if "mla" in phases:
                P.release(layer_mark2)
                Wc = P.tile([128, 8, 672], BF16, "Wc")
                wload(Wc, "w_in", l, cols=(3072, 3744), pat="(kc p) n -> p kc n", p=128)
                Wqb = P.tile([128, 3, 768], BF16, "Wqb")
                wload(Wqb, "mla_wq_b", l, pat="(kc p) n -> p kc n", p=128)
                Wkvb = P.tile([128, 2, 1024], BF16, "Wkvb")
                wload(Wkvb, "mla_wkv_b", l, pat="(kc p) n -> p kc n", p=128)
                cT = P.tile([128, 5, S], BF16, "cT")
                msa = P.tile([128, 3, NT], F32, "msa")
                rq = P.tile([128, NT], F32, "rq")
                rq2 = P.tile([128, NT], F32, "rq2")
                rkv = P.tile([128, NT], F32, "rkv")
                P.memset(msa, 0.0)
                for m in range(5):
                    for tc in range(4):
                        ps = P.psum((m * 4 + tc) % 2, [128, 512])
                        for kc in range(8):
                            P.mm(ps, Wc[:, kc, m * 128:(m + 1) * 128], hT[:, kc, tc * 512:(tc + 1) * 512],
                                 start=(kc == 0), stop=(kc == 7))
                        P.act(cT[:, m, tc * 512:(tc + 1) * 512], ps, AF.Copy, scale=pv[:, PV_GQA + m:PV_GQA + m + 1])
                mla_mark = P.mark()
                sqj = P.tile([128, 640], F32, "sqj")
                for i in range(NT):
                    ps1 = P.psum(2 + (i % 2) * 2, [128, 512])
                    ps2 = P.psum(3 + (i % 2) * 2, [128, 128])
                    for kc in range(8):
                        P.mm(ps1, hT[:, kc, i * 128:(i + 1) * 128], Wc[:, kc, 0:512], start=(kc == 0), stop=(kc == 7))
                    for kc in range(8):
                        P.mm(ps2, hT[:, kc, i * 128:(i + 1) * 128], Wc[:, kc, 512:640], start=(kc == 0), stop=(kc == 7))
                    P.act(sqj[:, 0:384], ps1[:, 0:384], AF.Square, accum=msa[:, 0, i:i + 1])
                    P.act(sqj[:, 384:512], ps1[:, 384:512], AF.Square, accum=msa[:, 1, i:i + 1])
                    P.act(sqj[:, 512:640], ps2, AF.Square, accum=msa[:, 2, i:i + 1])
                rsqrt(rq, msa[:, 0, :], 1.0 / 384, eps_t)
                P.tt(rq2, rq, rq, ALU.mult)
                P.ts(rq2, rq2, 1.0 / 96, ALU.mult)
                P.tt(rkv, msa[:, 1, :], msa[:, 2, :], ALU.add)
                rsqrt(rkv, rkv, 1.0 / 256, eps_t)
                gq = bvt[:, BV_GQ:BV_GQ + 96]
                gk = bvt[:, BV_GK:BV_GK + 96]
                for hh in range(2):
                    P.release(mla_mark)
                    qT = P.tile([96, 4, S], BF16, "qT")
                    kT = P.tile([96, 4, S], BF16, "kT")
                    vtm = P.tile([128, NT, 4, 64], BF16, "vtm")
                    def two_(shape, dt, nm):
                        return [P.tile(shape, dt, f"{nm}{j_}") for j_ in range(2)]
                    sq_ = two_([128, 4, 96], F32, "sq")
                    sqk_ = two_([128, 4, 96], F32, "sqk")
                    qn_ = two_([128, 4, 96], F32, "qn")
                    kn_ = two_([128, 4, 96], F32, "kn")
                    qb_ = two_([128, 4, 96], BF16, "qb")
                    kb_ = two_([128, 4, 96], BF16, "kb")
                    s4_ = two_([128, 4], F32, "s4")
                    s4k_ = two_([128, 4], F32, "s4k")
                    r1_ = two_([128, 4, 16], F32, "r1")
                    r2_ = two_([128, 4, 16], F32, "r2")
                    r3_ = two_([128, 4, 16], F32, "r3")
                    r4_ = two_([128, 4, 16], F32, "r4")

                    def rope(xn_, xb_, i, ta, tb, eng):
                        cosb = cosT[:, i, :].unsq(1).bc([128, 4, 16])
                        sinb = sinT[:, i, :].unsq(1).bc([128, 4, 16])
                        x1, x2 = xn_[:, :, 64:80], xn_[:, :, 80:96]
                        return [
                            lambda: P.cp(xb_[:, :, 0:64], xn_[:, :, 0:64], eng=eng),
                            lambda: P.tt(ta, x1, cosb, ALU.mult, eng=eng),
                            lambda: P.tt(tb, x2, sinb, ALU.mult, eng=eng),
                            lambda: P.tt(xb_[:, :, 64:80], ta, tb, ALU.subtract, eng=eng),
                            lambda: P.tt(ta, x1, sinb, ALU.mult, eng=eng),
                            lambda: P.tt(tb, x2, cosb, ALU.mult, eng=eng),
                            lambda: P.tt(xb_[:, :, 80:96], ta, tb, ALU.add, eng=eng),
                        ]

                    def proj(i):
                        tl_ = slice(i * 128, (i + 1) * 128)
                        b0 = 0 if i % 2 == 0 else 5
                        psq_ = P.psum(b0, [128, 4, 96])
                        pskv_ = P.psum(b0 + 1, [128, 4, 128])
                        pspe_ = P.psum(b0 + 2, [128, 32])
                        for kc in range(3):
                            P.mm(psq_, cT[:, kc, tl_], Wqb[:, kc, hh * 384:(hh + 1) * 384], start=(kc == 0), stop=(kc == 2))
                        for kc in range(2):
                            P.mm(pskv_, cT[:, 3 + kc, tl_], Wkvb[:, kc, hh * 512:(hh + 1) * 512], start=(kc == 0), stop=(kc == 1))
                        for kc in range(8):
                            P.mm(pspe_, hT[:, kc, tl_], Wc[:, kc, 640:672], start=(kc == 0), stop=(kc == 7))
                        return psq_, pskv_, pspe_

                    chains = []
                    for i in range(NT):
                        tl = slice(i * 128, (i + 1) * 128)
                        psq, pskv, pspe = proj(i)
                        pi_ = i % 2
                        sq, sqk, qn, kn, qb, kb = sq_[pi_], sqk_[pi_], qn_[pi_], kn_[pi_], qb_[pi_], kb_[pi_]
                        s4, s4k, r1, r2, r3, r4 = s4_[pi_], s4k_[pi_], r1_[pi_], r2_[pi_], r3_[pi_], r4_[pi_]
                        pt = P.psum(3, [96, 4, 128], BF16, off=1024 * pi_)
                        pt2 = P.psum(4, [96, 4, 128], BF16, off=1024 * pi_)

                        def qpath(psq=psq, sq=sq, s4=s4, qn=qn, qb=qb, r1=r1, r2=r2, pt=pt, i=i, tl=tl):
                            ops = [
                                lambda: P.act(sq, psq, AF.Square),
                                lambda: P.red(s4, sq),
                                lambda: P.act(s4, s4, AF.Sqrt, bias=eps_t, scale=rq2[:, i:i + 1]),
                                lambda: P.recip(s4, s4),
                                lambda: P.ts(s4, s4, rq[:, i:i + 1], ALU.mult),
                                lambda: P.tt(qn, psq, s4.unsq(2).bc([128, 4, 96]), ALU.mult),
                                lambda: P.tt(qn, qn, gq.unsq(1).bc([128, 4, 96]), ALU.mult),
                            ] + rope(qn, qb, i, r1, r2, "pool")
                            ops += [(lambda h=h: P.tr(pt[:, h, :], qb[:, h, :], idb)) for h in range(4)]
                            ops.append(lambda: P.cp(qT[:, :, tl], pt, eng="act"))
                            return ops

                        def kpath(pskv=pskv, pspe=pspe, sqk=sqk, s4k=s4k, kn=kn, kb=kb, r3=r3, r4=r4, pt2=pt2, i=i, tl=tl):
                            ops = [
                                lambda: P.ts(kn[:, :, 0:64], pskv[:, :, 0:64], rkv[:, i:i + 1], ALU.mult),
                                lambda: P.ts(vtm[:, i, :, :], pskv[:, :, 64:128], rkv[:, i:i + 1], ALU.mult),
                                lambda: P.cp(kn[:, :, 64:96], pspe.unsq(1).bc([128, 4, 32])),
                                lambda: P.act(sqk, kn, AF.Square),
                                lambda: P.red(s4k, sqk),
                                lambda: P.act(s4k, s4k, AF.Sqrt, bias=eps_t, scale=1.0 / 96),
                                lambda: P.recip(s4k, s4k),
                                lambda: P.tt(kn, kn, s4k.unsq(2).bc([128, 4, 96]), ALU.mult),
                                lambda: P.tt(kn, kn, gk.unsq(1).bc([128, 4, 96]), ALU.mult),
                            ] + rope(kn, kb, i, r3, r4, "pool")
                            ops += [(lambda h=h: P.tr(pt2[:, h, :], kb[:, h, :], idb)) for h in range(4)]
                            ops.append(lambda: P.cp(kT[:, :, tl], pt2, eng="act"))
                            return ops

                        chains += [kpath(), qpath()]
                        if i % 2 == 1:
                            for n_ in range(max(len(c_) for c_ in chains)):
                                for c_ in chains:
                                    if n_ < len(c_):
                                        c_[n_]()
                            chains = []
                    pTb = [P.tile([128, 512], BF16, f"pT{j}") for j in range(4)]
                    rz = P.tile([64, 512], F32, "rz")
                    it = 0
                    for h in range(4):
                        for qc in range(4):
                            pso = P.psum(4 + (it % 2) * 2, [64, 512])
                            psz = P.psum(5 + (it % 2) * 2, [64, 512])
                            it += 1
                            nk = 4 * qc + 4
                            def score(kt):
                                c0 = max(0, kt - 4 * qc) * 128
                                pss = P.psum(kt % 4, [128, 512])
                                P.mm(pss[:, c0:512], kT[:, h, kt * 128:(kt + 1) * 128], qT[:, h, qc * 512 + c0:(qc + 1) * 512])
                                return pss
                            nxt = score(0)
                            for kt in range(nk):
                                c0 = max(0, kt - 4 * qc) * 128
                                pss = nxt
                                if kt + 1 < nk:
                                    nxt = score(kt + 1)
                                pT = pTb[kt % 4]
                                P.act(pT[:, c0:512], pss[:, c0:512], AF.Exp, scale=float(96 ** -0.5))
                                if kt >= 4 * qc:
                                    P.tt(pT[:, c0:c0 + 128], pT[:, c0:c0 + 128], cmaskb, ALU.mult, eng="pool")
                                P.mm(pso[:, c0:512], vtm[:, kt, h, :], pT[:, c0:512], start=(kt == 0), stop=(kt == nk - 1))
                                P.mm(psz[:, c0:512], onesb, pT[:, c0:512], start=(kt == 0), stop=(kt == nk - 1))
                            P.recip(rz, psz)
                            P.tt(oT[:, hh * 4 + h, qc * 512:(qc + 1) * 512], pso, rz, ALU.mult)
                print('MLA hwm', P.hwm, 'cur', P.sb_off)
                if dbg == f"oT{l}":
                    P.release(layer_mark2)
                    tmpd = P.tile([64, 4096], F32, "tmpd")
                    P.cp(tmpd[:, 0:2048], oT[:, 0, :])
                    P.cp(tmpd[:, 2048:4096], oT[:, 7, :])
                    dump(tmpd, 4096)
            if "merge" in phases:
                P.release(layer_mark2)
                mT = P.tile([128, 8, S], BF16, "mT")
                merge_mark = P.mark()
                Wo = P.tile([64, 8, D], BF16, "Wo")
                wload(Wo, "mla_w_o", l, pat="(h p) n -> p h n", p=64)
                Wro = P.tile([128, 2, D], BF16, "Wro")
                wload(Wro, "rwkv_w_o", l, pat="(c p) n -> p c n", p=128)
                Wco = P.tile([128, 2, D], BF16, "Wco")
                wload(Wco, "conv_w_o", l, pat="(c p) n -> p c n", p=128)
                Wg = [P.tile([128, 8, 3, 128], BF16, f"Wg{j}") for j in range(2)]
                gs = [[P.tile([128, 512], F32, f"gs{s_}{j}") for j in range(3)] for s_ in range(2)]
                mas = [P.tile([128, 512], F32, f"ma{s_}") for s_ in range(2)]
                mbs = [P.tile([128, 512], F32, f"mb{s_}") for s_ in range(2)]

                def load_wg(dc_):
                    for j in range(3):
                        wload(Wg[dc_ % 2][:, :, j, :], "w_in", l, cols=(j * D + dc_ * 128, j * D + (dc_ + 1) * 128),
                              pat="(kc p) n -> p kc n", p=128)
                load_wg(0)
                it_ = 0
                gcnt = 0
                for dc in range(8):
                    wg = Wg[dc % 2]
                    if dc + 1 < 8:
                        load_wg(dc + 1)
                    for tc in range(4):
                        tk = slice(tc * 512, (tc + 1) * 512)
                        set_ = it_ % 2
                        it_ += 1
                        ma, mb, gs_ = mas[set_], mbs[set_], gs[set_]
                        for j in range(3):
                            psG = P.psum(6 + gcnt % 2, [128, 512])
                            gcnt += 1
                            for kc in range(8):
                                P.mm(psG, wg[:, kc, j, :], hT[:, kc, tk], start=(kc == 0), stop=(kc == 7))
                            P.act(gs_[j], psG, AF.Sigmoid)
                        psA = P.psum(3 * set_, [128, 512])
                        psB = P.psum(3 * set_ + 1, [128, 512])
                        psC = P.psum(3 * set_ + 2, [128, 512])
                        for h in range(8):
                            P.mm(psA, Wo[:, h, dc * 128:(dc + 1) * 128], oT[:, h, tk], start=(h == 0), stop=(h == 7))
                        for c in range(2):
                            P.mm(psB, Wro[:, c, dc * 128:(dc + 1) * 128], ygT[:, c, tk], start=(c == 0), stop=(c == 1))
                        for c in range(2):
                            P.mm(psC, Wco[:, c, dc * 128:(dc + 1) * 128], cvT[:, c, tk], start=(c == 0), stop=(c == 1))
                        P.tt(ma, psA, gs_[0], ALU.mult)
                        P.tt(mb, psB, gs_[1], ALU.mult)
                        P.tt(ma, ma, mb, ALU.add)
                        P.tt(mb, psC, gs_[2], ALU.mult)
                        P.tt(mT[:, dc, tk], ma, mb, ALU.add)
                if dbg == f"mT{l}":
                    tmpd = P.tile([128, 2048], F32, "tmpd")
                    P.cp(tmpd, mT[:, 3, :])
                    dump(tmpd, 2048)
                P.release(merge_mark)
                Wout = P.tile([128, 8, D], BF16, "Wout")
                wload(Wout, "w_out", l, pat="(kc p) n -> p kc n", p=128)
                xin = [P.tile([128, D], F32, f"xin{j}") for j in range(3)]
                nst = norm_setup("mlp_norm", l)
                for i in range(NT):
                    tl = slice(i * 128, (i + 1) * 128)
                    x_ = xin[i % 3]
                    P.dma(x_, xsrc(i))
                    for nb in range(2):
                        ps = P.psum((i % 2) * 2 + nb, [128, 512])
                        for dc in range(8):
                            P.mm(ps, mT[:, dc, tl], Wout[:, dc, nb * 512:(nb + 1) * 512], start=(dc == 0), stop=(dc == 7))
                        P.tt(x_[:, nb * 512:(nb + 1) * 512], x_[:, nb * 512:(nb + 1) * 512], ps, ALU.add)
                    P.dma(XMID(xmid_ap[tl, :], i, i + 1), x_)
                    norm_tile(i, x_, nst, 4 + i % 2)

            if "mlp" in phases:
                P.release(layer_mark0)
                if "merge" not in phases:
                    norm_phase(lambda i: XMID(xmid_ap[i * 128:(i + 1) * 128, :], i, i + 1), "mlp_norm", l)
                aT = P.tile([128, 32, 1024], BF16, "aT")
                Wu = [P.tile([128, 8, 1024], BF16, f"Wu{j}") for j in range(2)]
                Wd = [P.tile([128, 4, D], BF16, f"Wd{j}") for j in range(2)]
                rl = [P.tile([128, 512], F32, f"rl{j}") for j in range(2)]
                xin = [P.tile([128, D], F32, f"xin{j}") for j in range(2)]
                sched = []
                for th in range(2):
                    for fg in range(4):
                        sched.append(("u", th, fg))
                    for tg in range(2):
                        for fg in range(8):
                            sched.append(("d", th, tg, fg))
                bufs = {}
                cnt = {"u": 0, "d": 0}

                def issue(k):
                    it_ = sched[k]
                    if it_[0] == "u":
                        w_ = Wu[cnt["u"] % 2]
                        cnt["u"] += 1
                        wload(w_, "w_up", l, cols=(it_[2] * 1024, (it_[2] + 1) * 1024), pat="(kc p) n -> p kc n", p=128)
                    else:
                        w_ = Wd[cnt["d"] % 2]
                        cnt["d"] += 1
                        wload(w_, "w_down", l, rows=(it_[3] * 512, (it_[3] + 1) * 512), pat="(f p) n -> p f n", p=128)
                    bufs[k] = w_

                issue(0)
                for k, it_ in enumerate(sched):
                    if k + 1 < len(sched):
                        issue(k + 1)
                    w_ = bufs.pop(k)
                    if it_[0] == "u":
                        _, th, fg = it_
                        t0 = th * 1024
                        for fi in range(8):
                            f = fg * 8 + fi
                            for tc in range(2):
                                ps = P.psum((fi * 2 + tc) % 4, [128, 512])
                                for kc in range(8):
                                    P.mm(ps, w_[:, kc, fi * 128:(fi + 1) * 128], hT[:, kc, t0 + tc * 512:t0 + (tc + 1) * 512],
                                         start=(kc == 0), stop=(kc == 7))
                                r_ = rl[(fi * 2 + tc) % 2]
                                P.act(r_, ps, AF.Relu)
                                P.tt(aT[:, f, tc * 512:(tc + 1) * 512], r_, r_, ALU.mult)
                    else:
                        _, th, tg, fg = it_
                        for ti in range(4):
                            tl = slice((tg * 4 + ti) * 128, (tg * 4 + ti + 1) * 128)
                            for fi in range(4):
                                for nb in range(2):
                                    ps = P.psum(ti * 2 + nb, [128, 512])
                                    P.mm(ps, aT[:, fg * 4 + fi, tl], w_[:, fi, nb * 512:(nb + 1) * 512],
                                         start=(fg == 0 and fi == 0), stop=(fg == 7 and fi == 3))
                        if fg == 7:
                            for ti in range(4):
                                i = th * 8 + tg * 4 + ti
                                x_ = xin[ti % 2]
                                P.dma(x_, XMID(xmid_ap[i * 128:(i + 1) * 128, :], i, i + 1))
                                for nb in range(2):
                                    ps = P.psum(ti * 2 + nb, [128, 512])
                                    P.tt(x_[:, nb * 512:(nb + 1) * 512], x_[:, nb * 512:(nb + 1) * 512], ps, ALU.add)
                                P.dma(xdst(i), x_)
        P.final_wait()
        import os
        if "SHOWSP" in os.environ:
            nm = {id(v): k for k, v in P.sem.items()}
            for q in ("sp", "pool"):
                for j, sm in enumerate(P.dsem[q]):
                    nm[id(sm)] = f"d{q}{j}"
            for waits, fn, sem, inc in P.ops["sp"][-8:]:
                print("SP:", [(nm[id(a)], b) for a, b in waits], "dma" if fn else "-", nm.get(id(sem)), inc)
        P.replay(block)
        print("hwm", getattr(P, "hwm", 0), "instrs", P.ninstr, "waits", P.nwaits, {e: P.cnt[e] for e in P.ENGS}, P.dma_n)
    return nc


def make_inputs(inp):
    inp = {k: np.asarray(v) for k, v in inp.items()}
    shared = {"consts": host_consts(),
              "pvec": np.stack([host_pvec(inp, l) for l in range(DEPTH)]),
              "bvec": np.stack([host_bvec(inp, l) for l in range(DEPTH)])}
    for k in ("attn_norm", "mlp_norm"):
        shared[k] = np.ascontiguousarray(inp[k], dtype=np.float32)
    for k in WNAMES:
        shared[k] = np.ascontiguousarray(inp[k], dtype=np.float32)
    maps = []
    for b in range(8):
        m = dict(shared)
        m["x"] = np.ascontiguousarray(inp["x"][b], dtype=np.float32)
        m["pos"] = np.ascontiguousarray(inp["positions"][b].astype(np.int32).reshape(NT, 128).T)
        maps.append(m)
    return maps


_NC_CACHE = {}


def kernel(**inputs):
    maps = make_inputs(inputs)
    shapes = {k: v.shape for k, v in maps[0].items()}
    key = "main"
    if key not in _NC_CACHE:
        _NC_CACHE[key] = build(shapes)
    nc = _NC_CACHE[key]
    res = run_bass_kernel_spmd(nc, maps, core_ids=list(range(8)))
    return np.stack([r["out"] for r in res.results]).astype(np.float32)
```

```python
import numpy as np
import concourse.bass as bass
import concourse.mybir as mybir
from concourse.bass_utils import run_bass_kernel_spmd

F32 = mybir.dt.float32
BF16 = mybir.dt.bfloat16
I32 = mybir.dt.int32
AF = mybir.ActivationFunctionType
ALU = mybir.AluOpType
AX = mybir.AxisListType
DTB = {F32: 4, BF16: 2, I32: 4}


class V:
    def __init__(self, ap, arena, s, e):
        self.ap, self.arena, self.s, self.e = ap, arena, s, e

    def __getitem__(self, k):
        return V(self.ap[k], self.arena, self.s, self.e)

    def re(self, pat, **kw):
        return V(self.ap.rearrange(pat, **kw), self.arena, self.s, self.e)

    def bc(self, shape):
        return V(self.ap.to_broadcast(shape), self.arena, self.s, self.e)

    def unsq(self, ax):
        return V(self.ap.unsqueeze(ax), self.arena, self.s, self.e)

    def sub(self, s, e):
        return V(self.ap, self.arena, self.s + s, self.s + e)


class Prog:
    ENGS = ("pe", "act", "dve", "pool", "sp")
    NPOOL = 12

    def __init__(self, nc, sbuf_bytes=196608, same_engine_sync=True):
        self.nc = nc
        self.same = same_engine_sync
        self.ops = {e: [] for e in self.ENGS}
        self.cnt = {e: 0 for e in self.ENGS}
        self.seen = {e: {} for e in self.ENGS}
        self.recs = {}
        self.sb_off = 0
        self.sb_bytes = sbuf_bytes
        self.dma_n = {"sp": 0, "pool": 0}
        self.nwaits = 0
        self.ninstr = 0

    def setup(self, sb, ps, sems, dsems):
        self.sb, self.ps = sb, ps
        self.sem = sems
        self.dsem = dsems

    def mark(self):
        return self.sb_off

    def release(self, m):
        self.sb_off = m

    def tile(self, shape, dt, name=""):
        n = int(np.prod(shape[1:])) * DTB[dt]
        n4 = (n + 3) // 4 * 4
        s = self.sb_off
        assert s + n4 <= self.sb_bytes, f"SBUF overflow at {name}: {s}+{n4}"
        self.sb_off += n4
        self.hwm = max(getattr(self, 'hwm', 0), self.sb_off)
        ap = self.sb[0:shape[0], s // 4:(s + n4) // 4]
        if dt != F32:
            ap = ap.bitcast(dt)
        ap = ap[:, 0:int(np.prod(shape[1:]))]
        if len(shape) > 2:
            names = " ".join(f"d{i}" for i in range(len(shape) - 1))
            ap = ap.rearrange(f"p ({names}) -> p {names}", **{f"d{i}": shape[i + 1] for i in range(len(shape) - 1)})
        return V(ap, "sb", s, s + n4)

    def psum(self, bank, shape, dt=F32, off=0):
        n = int(np.prod(shape[1:])) * DTB[dt]
        s = bank * 2048 + off
        assert off + n <= 2048 * (8 - bank)
        ap = self.ps[0:shape[0], s // 4:(s + n + 3) // 4]
        if dt != F32:
            ap = ap.bitcast(dt)
        ap = ap[:, 0:int(np.prod(shape[1:]))]
        if len(shape) > 2:
            names = " ".join(f"d{i}" for i in range(len(shape) - 1))
            ap = ap.rearrange(f"p ({names}) -> p {names}", **{f"d{i}": shape[i + 1] for i in range(len(shape) - 1)})
        return V(ap, "ps", (s // 2048) * 2048, ((s + n + 2047) // 2048) * 2048)

    def dram(self, ap, name, s=0, e=1):
        return V(ap, "dram_" + name, s, e)

    def _deps(self, reads, writes):
        toks = []
        for v in reads:
            for r in self.recs.get(v.arena, []):
                if r[4] and r[0] < v.e and v.s < r[1]:
                    toks.append(r[2:4])
        for v in writes:
            for r in self.recs.get(v.arena, []):
                if r[0] < v.e and v.s < r[1]:
                    toks.append(r[2:4])
        return toks

    def _record(self, reads, writes, tok):
        for v in writes:
            L = self.recs.setdefault(v.arena, [])
            L[:] = [r for r in L if not (v.s <= r[0] and r[1] <= v.e)]
            L.append([v.s, v.e, tok[0], tok[1], True])
        for v in reads:
            L = self.recs.setdefault(v.arena, [])
            L[:] = [r for r in L if not ((not r[4]) and r[2] == tok[0] and v.s <= r[0] and r[1] <= v.e)]
            L.append([v.s, v.e, tok[0], tok[1], False])

    def _waits(self, eng, toks, skip_sem=None):
        need = {}
        for sem, val in toks:
            if sem is skip_sem:
                continue
            if self.seen[eng].get(sem, 0) < val:
                need[sem] = max(need.get(sem, 0), val)
        for sem, val in need.items():
            self.seen[eng][sem] = val
        self.nwaits += len(need)
        return list(need.items())

    def op(self, eng, fn, outs, ins):
        outs = [o for o in outs if o is not None]
        ins = [i for i in ins if isinstance(i, V)]
        toks = self._deps(ins, outs)
        sem = self.sem[eng]
        skip = sem if (eng == "pe" or not self.same) else None
        waits = self._waits(eng, toks, skip_sem=skip)
        self.cnt[eng] += 1
        seq = self.cnt[eng]
        self.seen[eng][sem] = max(self.seen[eng].get(sem, 0), 0)
        self.ops[eng].append((waits, fn, sem, 1))
        self._record(ins, outs, (sem, seq))
        self.ninstr += 1

    def dma(self, out, in_, q="sp", **kw):
        i = self.dma_n[q]
        self.dma_n[q] += 1
        pool = self.dsem[q]
        k = i % len(pool)
        sem = pool[k]
        val = 16 * (i // len(pool) + 1)
        toks = self._deps([in_], [out])
        toks.append((sem, val - 16))
        waits = self._waits(q, toks)
        oa, ia = out.ap, in_.ap
        self.ops[q].append((waits, lambda e: e.dma_start(out=oa, in_=ia, **kw), sem, 16))
        self._record([in_], [out], (sem, val))
        self.ninstr += 1

    def final_wait(self, eng="sp"):
        toks = []
        for q in ("sp", "pool"):
            n = self.dma_n[q]
            pool = self.dsem[q]
            for k in range(len(pool)):
                cntk = (n - k + len(pool) - 1) // len(pool) if n > k else 0
                if cntk:
                    toks.append((pool[k], 16 * cntk))
        for e in self.ENGS:
            if self.cnt[e]:
                toks.append((self.sem[e], self.cnt[e]))
        waits = self._waits(eng, toks)
        self.ops[eng].append((waits, None, None, 0))

    def replay(self, block):
        nc = self.nc

        def run(engobj, lst):
            for waits, fn, sem, inc in lst:
                for s, v in waits:
                    engobj.wait_ge(s, v)
                if fn is not None:
                    fn(engobj).then_inc(sem, inc)

        block.sync(lambda e: run(e, self.ops["sp"]))
        block.gpsimd(lambda e: run(e, self.ops["pool"]))
        block.tensor(lambda e: run(e, self.ops["pe"]))
        block.scalar(lambda e: run(e, self.ops["act"]))
        block.vector(lambda e: run(e, self.ops["dve"]))

    def mm(self, out, lhsT, rhs, start=True, stop=True):
        o, l, r = out.ap, lhsT.ap, rhs.ap
        self.op("pe", lambda e: e.matmul(o, l, r, start=start, stop=stop), [out], [lhsT, rhs] + ([] if start else [out]))

    def tr(self, out, in_, ident):
        o, i, d = out.ap, in_.ap, ident.ap
        self.op("pe", lambda e: e.transpose(o, i, d), [out], [in_, ident])

    def act(self, out, in_, func, bias=None, scale=None, accum=None, eng="act"):
        o, i = out.ap, in_.ap
        kw = {}
        if bias is not None:
            kw["bias"] = bias.ap if isinstance(bias, V) else bias
        if scale is not None:
            kw["scale"] = scale.ap if isinstance(scale, V) else scale
        if accum is not None:
            kw["accum_out"] = accum.ap
        self.op(eng, lambda e: e.activation(o, i, func, **kw), [out, accum], [in_, bias, scale])

    def tt(self, out, a, b, op, eng="dve"):
        o, x, y = out.ap, a.ap, b.ap
        self.op(eng, lambda e: e.tensor_tensor(o, x, y, op), [out], [a, b])

    def ts(self, out, a, s1, op0, s2=None, op1=None, accum=None, eng="dve"):
        o, x = out.ap, a.ap
        c1 = s1.ap if isinstance(s1, V) else s1
        c2 = s2.ap if isinstance(s2, V) else s2
        kw = {}
        if op1 is not None:
            kw["op1"] = op1
        if accum is not None:
            kw["accum_out"] = accum.ap
        self.op(eng, lambda e: e.tensor_scalar(o, x, c1, c2, op0, **kw), [out, accum], [a, s1, s2])

    def stt(self, out, a, s, b, op0, op1, eng="dve"):
        o, x, y = out.ap, a.ap, b.ap
        c = s.ap if isinstance(s, V) else s
        self.op(eng, lambda e: e.scalar_tensor_tensor(o, x, c, y, op0, op1), [out], [a, s, b])

    def cp(self, out, in_, eng="dve"):
        o, i = out.ap, in_.ap
        if eng == "act":
            self.op(eng, lambda e: e.copy(o, i), [out], [in_])
        else:
            self.op(eng, lambda e: e.tensor_copy(o, i), [out], [in_])

    def memset(self, out, val, eng="dve"):
        o = out.ap
        self.op(eng, lambda e: e.memset(o, val), [out], [])

    def red(self, out, in_, op=None, axis=None, eng="dve"):
        o, i = out.ap, in_.ap
        op = op or ALU.add
        axis = axis or AX.X
        self.op(eng, lambda e: e.tensor_reduce(o, i, axis, op), [out], [in_])

    def recip(self, out, in_):
        o, i = out.ap, in_.ap
        self.op("dve", lambda e: e.reciprocal(o, i), [out], [in_])
from contextlib import ExitStack

D = 1024
SBUF_BYTES = 211968
S = 2048
NT = 16
DEPTH = 2
IN_COLS = 5536
EPS = 1e-6
GN_EPS = 64e-5
PI = 3.141592653589793

PV_GQA, PV_GKVA, PV_MU, PV_KK, PV_KA, PV_A0, PV_RK, PV_V0, PV_VMU, PV_CW = 0, 3, 5, 13, 15, 17, 19, 21, 23, 24
PV_N = 30
BV_GQ, BV_GK, BV_W0, BV_LNW, BV_LNB = 0, 96, 192, 448, 704
BV_N = 960


def host_consts():
    i = np.arange(128)
    idf = np.eye(128, dtype=np.float32)
    strict = (i[:, None] < i[None, :]).astype(np.float32)
    incl = (i[:, None] <= i[None, :]).astype(np.float32)
    mask2 = np.concatenate([strict, incl], 1)
    maskT = (i[:, None] > i[None, :]).astype(np.float32)
    tri = np.concatenate([incl, strict], 1)
    blk = (i[:, None] // 64 == i[None, :] // 64).astype(np.float32)
    ind = np.zeros((128, 8), np.float32)
    for c_ in range(2):
        ind[i, c_ * 4 + 2 * c_ + i // 64] = 1.0
    freq = (10000.0 ** (-(np.arange(16, dtype=np.float32) * 2.0 / 32))).astype(np.float32)
    freq = np.broadcast_to(freq[None, :], (128, 16)).copy()
    c = np.concatenate([idf, mask2, maskT, tri, blk, ind, freq], 1).astype(np.float32)
    return np.ascontiguousarray(c)


C_IDF, C_MASK2, C_MASKT, C_TRI, C_BLK, C_IND, C_FREQ, C_N = 0, 128, 384, 512, 768, 896, 904, 920


def host_pvec(inp, l):
    pv = np.zeros((128, PV_N), np.float32)

    def put(col, vec):
        n = vec.shape[0] // 128
        pv[:, col:col + n] = vec.reshape(n, 128).T

    put(PV_GQA, inp["mla_q_a_norm"][l])
    put(PV_GKVA, inp["mla_kv_a_norm"][l])
    put(PV_MU, inp["rwkv_mu"][l])
    put(PV_KK, inp["rwkv_k_k"][l])
    put(PV_KA, inp["rwkv_k_a"][l])
    put(PV_A0, inp["rwkv_a0"][l])
    put(PV_RK, inp["rwkv_r_k"][l].reshape(256))
    if l > 0:
        put(PV_V0, inp["rwkv_v0"][l - 1])
        pv[0:32, PV_VMU] = inp["rwkv_v_mu"][l - 1]
    cw = inp["conv_w"][l]
    for fc in range(2):
        for j in range(3):
            pv[:, PV_CW + fc * 3 + j] = cw[j, fc * 128:(fc + 1) * 128]
    return pv


def host_bvec(inp, l):
    return np.concatenate([inp["mla_q_norm"][l], inp["mla_k_norm"][l],
                           inp["rwkv_w0"][l], inp["rwkv_ln_w"][l], inp["rwkv_ln_b"][l]]).astype(np.float32)[None, :]


WNAMES = ["w_in", "mla_wq_b", "mla_wkv_b", "mla_w_o", "rwkv_w2", "rwkv_a2", "rwkv_g2", "rwkv_w_o", "rwkv_v1",
          "rwkv_v2", "conv_w_o", "w_out", "w_up", "w_down"]


def build(shapes, dbg=None, nlayers=DEPTH, phases=("mla", "conv", "rwkv", "merge", "mlp")):
    nc = bass.Bass("TRN2", target_bir_lowering=False)
    dr = {}
    for k, shp in shapes.items():
        dt = I32 if k == "pos" else F32
        dr[k] = nc.dram_tensor(k, list(shp), dt, kind="ExternalInput").ap()
    out_ap = nc.dram_tensor("out", [S, D], F32, kind="ExternalOutput").ap()
    xmid_ap = nc.dram_tensor("xmid", [S, D], F32, kind="Internal").ap()
    xl1_ap = nc.dram_tensor("xl1", [S, D], F32, kind="Internal").ap()
    vf_ap = nc.dram_tensor("vfirst", [128, 2, S], F32, kind="Internal").ap()
    dbg_ap = nc.dram_tensor("dbg", [128, 4096], F32, kind="ExternalOutput").ap() if dbg else None

    with ExitStack() as es:
        sb = es.enter_context(nc.sbuf_tensor("sb", [128, SBUF_BYTES // 4], F32))
        ps_t = es.enter_context(nc.psum_tensor("ps", [128, 4096], F32))
        P = Prog(nc, sbuf_bytes=SBUF_BYTES, same_engine_sync=True)
        sems = {e: es.enter_context(nc.semaphore("s_" + e)) for e in Prog.ENGS}
        dsems = {q: [es.enter_context(nc.semaphore(f"d_{q}{i}")) for i in range(Prog.NPOOL)] for q in ("sp", "pool")}
        P.setup(sb, ps_t, sems, dsems)
        block = es.enter_context(nc.Block())

        def DR(name, ap=None, s=0, e=1):
            return P.dram(dr[name] if ap is None else ap, name, s, e)

        OUT = lambda ap, s, e: P.dram(ap, "out", s, e)
        XMID = lambda ap, s, e: P.dram(ap, "xmid", s, e)
        XL1 = lambda ap, s, e: P.dram(ap, "xl1", s, e)

        def dump(v, cols):
            if dbg_ap is not None:
                P.dma(P.dram(dbg_ap[0:v.ap.shape[0], 0:cols], "dbg"), v)

        cst = P.tile([128, C_N], F32, "cst")
        P.dma(cst, DR("consts"))
        idf = cst[:, C_IDF:C_IDF + 128]
        mask2 = cst[:, C_MASK2:C_MASK2 + 256]
        maskT = cst[:, C_MASKT:C_MASKT + 128]
        tri = cst[:, C_TRI:C_TRI + 256]
        blk = cst[:, C_BLK:C_BLK + 128]
        ind = cst[:, C_IND:C_IND + 8]
        freq = cst[:, C_FREQ:C_FREQ + 16]
        idb = P.tile([128, 128], BF16, "idb")
        P.cp(idb, idf)
        trib = P.tile([128, 256], BF16, "trib")
        P.cp(trib, tri)
        blkb = P.tile([128, 128], BF16, "blkb")
        P.cp(blkb, blk)
        cmaskb = P.tile([128, 128], BF16, "cmaskb")
        P.cp(cmaskb, mask2[:, 128:256])
        onesb = P.tile([128, 64], BF16, "onesb")
        P.memset(onesb, 1.0)
        pos_i = P.tile([128, NT], I32, "pos_i")
        P.dma(pos_i, DR("pos"))
        pos_f = P.tile([128, NT], F32, "pos_f")
        P.cp(pos_f, pos_i)
        ang = P.tile([128, NT, 16], F32, "ang")
        P.tt(ang, pos_f.unsq(2).bc([128, NT, 16]), freq.unsq(1).bc([128, NT, 16]), ALU.mult)
        sinT = P.tile([128, NT, 16], F32, "sinT")
        cosT = P.tile([128, NT, 16], F32, "cosT")
        tmpa = P.tile([128, NT, 16], F32, "tmpa")
        tmpk = P.tile([128, NT, 16], F32, "tmpk")
        tmpi = P.tile([128, NT, 16], I32, "tmpi")
        eps_t = P.tile([128, 1], F32, "eps_t")
        P.memset(eps_t, EPS)
        gneps_t = P.tile([128, 1], F32, "gneps_t")
        P.memset(gneps_t, GN_EPS)

        def sin_table(dst, shift):
            P.ts(tmpa, ang, shift, ALU.add)
            P.ts(tmpk, tmpa, 1.0 / (2 * PI), ALU.mult)
            P.cp(tmpi, tmpk)
            P.cp(tmpk, tmpi)
            P.stt(tmpa, tmpk, -6.28125, tmpa, ALU.mult, ALU.add)
            P.stt(tmpa, tmpk, -(2 * PI - 6.28125), tmpa, ALU.mult, ALU.add)
            P.ts(tmpk, tmpa, PI, ALU.is_gt, 2 * PI, ALU.mult)
            P.tt(tmpa, tmpa, tmpk, ALU.subtract)
            P.ts(tmpk, tmpa, -PI, ALU.is_lt, 2 * PI, ALU.mult)
            P.tt(tmpa, tmpa, tmpk, ALU.add)
            P.ts(tmpa, tmpa, PI, ALU.min, -PI, ALU.max)
            P.act(dst, tmpa, AF.Sin)

        sin_table(sinT, 0.0)
        sin_table(cosT, PI / 2)

        def rsqrt(out, in_, scale, bias_t):
            P.act(out, in_, AF.Sqrt, bias=bias_t, scale=scale)
            P.recip(out, out)

        hT = P.tile([128, 8, S], BF16, "hT")
        base_mark = P.mark()

        def wload(dst, name, l, rows=None, cols=None, pat=None, **kw):
            ap = dr[name][l]
            if rows is not None:
                ap = ap[rows[0]:rows[1]]
            if cols is not None:
                ap = ap[:, cols[0]:cols[1]]
            if pat:
                ap = ap.rearrange(pat, **kw)
            P.dma(dst, P.dram(ap, name), q="pool")

        def norm_setup(gname, l):
            gb = P.tile([128, D], F32, "gb")
            P.dma(gb, P.dram(dr[gname][l:l + 1, :].partition_broadcast(128), gname))
            junk = P.tile([128, D], F32, "junk")
            hb = [P.tile([128, D], BF16, f"hb{i}") for i in range(2)]
            ss = P.tile([128, NT], F32, "ss")
            P.memset(ss, 0.0)
            return gb, junk, hb, ss

        def norm_tile(i, x_, st_, psbank):
            gb, junk, hb, ss = st_
            P.act(junk, x_, AF.Square, accum=ss[:, i:i + 1])
            rsqrt(ss[:, i:i + 1], ss[:, i:i + 1], 1.0 / D, eps_t)
            h_ = hb[i % 2]
            P.stt(h_, x_, ss[:, i:i + 1], gb, ALU.mult, ALU.mult)
            pt = P.psum(psbank, [128, 8, 128], BF16)
            for c in range(8):
                P.tr(pt[:, c, :], h_[:, c * 128:(c + 1) * 128], idb)
            P.cp(hT[:, :, i * 128:(i + 1) * 128], pt, eng="act")

        def norm_phase(xsrc, gname, l):
            m0 = P.mark()
            st_ = norm_setup(gname, l)
            xt = [P.tile([128, D], F32, f"xt{i}") for i in range(2)]
            for i in range(NT):
                x_ = xt[i % 2]
                P.dma(x_, xsrc(i))
                norm_tile(i, x_, st_, i % 2)
            P.release(m0)

        def rwkv_phase(l, pv, bvt, ygT):
            STOP = 99
            PL = 'pool'
            DBGPT = ''
            dbgt = P.tile([128, 512], F32, "dbgt") if DBGPT else None
            dbg_done = []

            def dbgcopy(name, v, ncols):
                if name == DBGPT and not dbg_done:
                    dbg_done.append(1)
                    P.memset(dbgt, 0.0)
                    P.cp(dbgt[0:v.ap.shape[0], 0:ncols] if v.ap.base_partition() == 0 else dbgt[v.ap.base_partition():v.ap.base_partition() + v.ap.shape[0], 0:ncols], v)
                    dump(dbgt, 512)
            Wr = P.tile([128, 8, 1024], BF16, "Wr")
            wload(Wr, "w_in", l, cols=(3744, 4768), pat="(kc p) n -> p kc n", p=128)
            w2b = P.tile([64, 256], BF16, "w2b")
            wload(w2b, "rwkv_w2", l)
            a2b = P.tile([128, 256], BF16, "a2b")
            wload(a2b[64:128, :], "rwkv_a2", l)
            g2b = P.tile([128, 256], BF16, "g2b")
            wload(g2b, "rwkv_g2", l)
            if l > 0:
                v1b = P.tile([128, 8, 32], BF16, "v1b")
                wload(v1b, "rwkv_v1", l - 1, pat="(kc p) n -> p kc n", p=128)
                v2b = P.tile([32, 256], BF16, "v2b")
                wload(v2b, "rwkv_v2", l - 1)
                carv = P.tile([32, 1], F32, "carv")
                P.memset(carv, 0.0)
            carry = P.tile([128, 8], F32, "carry")
            P.memset(carry, 0.0)
            Hm = P.tile([128, 2, 128], F32, "Hm")
            Hb = P.tile([128, 2, 128], BF16, "Hb")
            P.memset(Hm, 0.0)
            P.memset(Hb, 0.0)
            raws = [P.tile([128, 513], F32, f"raw{j_}") for j_ in range(2)]
            raw = raws[0]
            mixed = [P.tile([128, 512], F32, f"mixed{m}") for m in range(8)]
            tanhb = P.tile([64, 512], BF16, "tanhb")
            xab = P.tile([128, 512], BF16, "xab")
            sgb = P.tile([128, 512], BF16, "sgb")
            aT = [P.tile([128, 512], F32, f"aT{c}") for c in range(2)]
            kk = [P.tile([128, 512], F32, f"kk{c}") for c in range(2)]
            bT = [P.tile([128, 512], F32, f"bT{c}") for c in range(2)]
            kp = [P.tile([128, 512], F32, f"kp{c}") for c in range(2)]
            rkr = [P.tile([128, 512], F32, f"rkr{c}") for c in range(2)]
            tAs = [P.tile([128, 512], F32, f"tA{j_}") for j_ in range(2)]
            tA = tAs[0]
            difs = [P.tile([128, 512], F32, f"dif{j_}") for j_ in range(2)]
            sphs = [P.tile([128, 512], BF16, f"sph{j_}") for j_ in range(2)]
            spls = [P.tile([128, 512], BF16, f"spl{j_}") for j_ in range(2)]
            vh = [P.tile([128, 512], BF16, f"vh{c}") for c in range(2)]
            vl = [P.tile([128, 512], BF16, f"vl{c}") for c in range(2)]
            bh = [P.tile([128, 512], BF16, f"bh{c}") for c in range(2)]
            bl = [P.tile([128, 512], BF16, f"bl{c}") for c in range(2)]
            C0 = 0.6065306597126334

            def split(x, hi, lo, eng="dve"):
                P.cp(hi, x, eng=eng)
                P.tt(lo, x, hi, ALU.subtract, eng=eng)
            tBs = [P.tile([128, 512], F32, f"tB{j_}") for j_ in range(2)]
            tB = tBs[0]
            def two(shape, dt, nm):
                return [P.tile(shape, dt, f"{nm}{j}") for j in range(2)]
            def one(shape, dt, nm):
                t_ = P.tile(shape, dt, nm)
                return [t_, t_]
            ld_tm = two([128, 256], F32, "ld_tm")
            ldh_ = two([128, 256], BF16, "ldh")
            ldl_ = two([128, 256], BF16, "ldl")
            g_tm = two([128, 256], F32, "g_tm")
            E = two([128, 2, 384], F32, "E")
            AR = two([128, 4, 256], BF16, "AR")
            BTh = two([128, 4, 128], BF16, "BTh")
            KTh = two([128, 4, 128], BF16, "KTh")
            for j_ in range(2):
                P.memset(AR[j_], 0.0)
                P.memset(BTh[j_], 0.0, eng=PL)
                P.memset(KTh[j_], 0.0, eng=PL)
            BT = two([128, 2, 128], BF16, "BT")
            KT = two([128, 2, 128], BF16, "KT")
            Btm = two([128, 256], BF16, "Btm")
            Ktm = two([128, 256], BF16, "Ktm")
            Vb = two([128, 256], BF16, "Vb")
            bon = two([128, 256], F32, "bon")
            AbT = two([128, 4, 256], BF16, "AbT")
            AkT = two([128, 4, 256], BF16, "AkT")
            Qp0 = two([128, 4, 128], BF16, "Qp0")
            Qb = [two([128, 4, 128], BF16, f"Qb{t_}") for t_ in range(2)]
            Qpb = [two([128, 4, 128], BF16, f"Qpb{t_}") for t_ in range(2)]
            Tt = two([128, 4, 128], BF16, "Tt")
            Tn = two([128, 4, 128], BF16, "Tn")
            Xb = one([128, 256], BF16, "Xb")
            Ub = one([128, 256], BF16, "Ub")
            ysb = one([128, 4, 64], F32, "ysb")
            ysq = one([128, 4, 64], F32, "ysq")
            yn = one([128, 4, 64], F32, "yn")
            ygb = one([128, 256], BF16, "ygb")
            st = two([128, 5, 4], F32, "st")
            tH = P.tile([128, 128], F32, "tH")
            if l > 0:
                rawv = raw[0:32, :]
                difv = tB[0:32, :]
                xvb = P.tile([32, 512], BF16, "xvb")
                vft = tA
                vg = rkr[0]
            idb4 = idb.unsq(1).bc([128, 4, 128])

            for g in range(4):
                tk = slice(g * 512, (g + 1) * 512)
                for m in range(8):
                    ps = P.psum(m % 2, [128, 512])
                    for kc in range(8):
                        P.mm(ps, Wr[:, kc, m * 128:(m + 1) * 128], hT[:, kc, tk], start=(kc == 0), stop=(kc == 7))
                    rw, df = raws[m % 2], difs[m % 2]
                    P.cp(rw[:, 0:1], carry[:, m:m + 1])
                    P.cp(rw[:, 1:513], ps, eng="act")
                    P.cp(carry[:, m:m + 1], rw[:, 512:513])
                    P.tt(df, rw[:, 0:512], rw[:, 1:513], ALU.subtract)
                    P.stt(mixed[m], df, pv[:, PV_MU + m:PV_MU + m + 1], rw[:, 1:513], ALU.mult, ALU.add)
                dbgcopy('mixed0', mixed[0], 512); dbgcopy('mixed6', mixed[6], 512); dbgcopy('mixed7', mixed[7], 512)
                r_, k_, v_ = mixed[0:2], mixed[2:4], mixed[4:6]
                if STOP <= 1: return
                if l == 0:
                    for c in range(2):
                        P.dma(P.dram(vf_ap[:, c, tk], "vfirst", g * 2 + c, g * 2 + c + 1), v_[c])
                else:
                    psv = P.psum(0, [32, 512])
                    for kc in range(8):
                        P.mm(psv, v1b[:, kc, :], hT[:, kc, tk], start=(kc == 0), stop=(kc == 7))
                    P.cp(rawv[:, 0:1], carv)
                    P.cp(rawv[:, 1:513], psv, eng="act")
                    P.cp(carv, rawv[:, 512:513])
                    P.tt(difv, rawv[:, 0:512], rawv[:, 1:513], ALU.subtract)
                    P.stt(difv, difv, pv[0:32, PV_VMU:PV_VMU + 1], rawv[:, 1:513], ALU.mult, ALU.add)
                    P.cp(xvb, difv)
                    for c in range(2):
                        psg_ = P.psum(1, [128, 512])
                        P.mm(psg_, v2b[:, c * 128:(c + 1) * 128], xvb)
                        P.act(vg, psg_, AF.Sigmoid, bias=pv[:, PV_V0 + c:PV_V0 + c + 1])
                        P.dma(vft, P.dram(vf_ap[:, c, tk], "vfirst", g * 2 + c, g * 2 + c + 1))
                        P.tt(vft, vft, v_[c], ALU.subtract)
                        P.tt(vft, vft, vg, ALU.mult)
                        P.tt(v_[c], v_[c], vft, ALU.add)
                if STOP <= 2: return
                P.act(tanhb, mixed[6][0:64, :], AF.Tanh)
                P.cp(xab[64:128, :], mixed[6][64:128, :])
                P.act(sgb, mixed[7], AF.Sigmoid)
                for c in range(2):
                    psa = P.psum(c, [128, 512])
                    P.mm(psa, a2b[64:128, c * 128:(c + 1) * 128], xab[64:128, :])
                    P.act(aT[c], psa, AF.Sigmoid, bias=pv[:, PV_A0 + c:PV_A0 + c + 1])
                dbgcopy('aT0', aT[0], 512); dbgcopy('aT1', aT[1], 512); dbgcopy('tanhb', tanhb, 512)
                if STOP <= 3: return
                def elem(c):
                    tA, tB, sph, spl = tAs[c], tBs[c], sphs[c], spls[c]
                    P.ts(kk[c], k_[c], pv[:, PV_KK + c:PV_KK + c + 1], ALU.mult)
                    P.tt(tA, kk[c], kk[c], ALU.mult)
                    yield
                    pss = P.psum(c, [128, 512])
                    split(tA, sph, spl)
                    P.mm(pss, blkb, sph, start=True, stop=False)
                    P.mm(pss, blkb, spl, start=False, stop=True)
                    yield
                    P.act(tB, pss, AF.Sqrt)
                    yield
                    P.ts(tB, tB, 1e-12, ALU.max)
                    P.recip(tB, tB)
                    yield
                    P.tt(kk[c], kk[c], tB, ALU.mult)
                    P.tt(bT[c], kk[c], aT[c], ALU.mult)
                    yield
                    P.ts(tA, aT[c], -1.0, ALU.add, pv[:, PV_KA + c:PV_KA + c + 1], ALU.mult)
                    P.stt(kp[c], tA, 1.0, k_[c], ALU.add, ALU.mult)
                    yield
                    P.stt(rkr[c], r_[c], pv[:, PV_RK + c:PV_RK + c + 1], kp[c], ALU.mult, ALU.mult)
                    yield
                    psr = P.psum(c, [128, 512])
                    split(rkr[c], sph, spl)
                    P.mm(psr, blkb, sph, start=True, stop=False)
                    P.mm(psr, blkb, spl, start=False, stop=True)
                    yield
                    P.tt(rkr[c], psr, v_[c], ALU.mult)
                    yield
                    split(rkr[c], bh[c], bl[c])
                    yield
                    split(v_[c], vh[c], vl[c])
                    yield

                gens_ = [elem(0), elem(1)]
                while gens_:
                    for g_ in list(gens_):
                        try:
                            next(g_)
                        except StopIteration:
                            gens_.remove(g_)

                m2b = mask2.unsq(1).bc([128, 4, 256])
                mTb = maskT.unsq(1).bc([128, 4, 128])

                def prep_pieces(j):
                    i = g * 4 + j
                    pb_ = i % 2
                    tj = slice(j * 128, (j + 1) * 128)
                    ldh, ldl = ldh_[pb_], ldl_[pb_]

                    def p1():
                        psl = P.psum(1, [128, 256])
                        P.mm(psl, tanhb[:, tj], w2b)
                        P.tt(ld_tm[pb_], psl, bvt[:, BV_W0:BV_W0 + 256], ALU.add)
                        P.act(ld_tm[pb_], ld_tm[pb_], AF.Sigmoid)
                        yield
                        psg2 = P.psum(1, [128, 256], off=1024)
                        P.mm(psg2, sgb[:, tj], g2b)
                        P.cp(g_tm[pb_], psg2, eng="act")
                        yield
                        split(ld_tm[pb_], ldh, ldl)
                        yield

                    def p2():
                        psc = P.psum(0, [128, 2, 256])
                        for c in range(2):
                            P.mm(psc[:, c, :], ldh[:, c * 128:(c + 1) * 128], trib, start=True, stop=False)
                            P.mm(psc[:, c, :], ldl[:, c * 128:(c + 1) * 128], trib, start=False, stop=True)
                        P.act(E[pb_][:, :, 0:256], psc, AF.Exp, scale=-C0)
                        P.act(E[pb_][:, :, 256:384], psc[:, :, 0:128], AF.Exp, scale=C0)
                        yield

                    def p3():
                        for c in range(2):
                            P.tt(BT[pb_][:, c, :], bT[c][:, tj], E[pb_][:, c, 256:384], ALU.mult, eng=PL)
                            P.tt(KT[pb_][:, c, :], kp[c][:, tj], E[pb_][:, c, 256:384], ALU.mult, eng=PL)
                            for hh in range(2):
                                h, rows = 2 * c + hh, slice(64 * hh, 64 * hh + 64)
                                P.stt(AR[pb_][rows, h, 0:128], kk[c][rows, tj], -1.0, E[pb_][rows, c, 128:256], ALU.mult, ALU.mult)
                                P.tt(AR[pb_][rows, h, 128:256], r_[c][rows, tj], E[pb_][rows, c, 0:128], ALU.mult)
                                P.cp(BTh[pb_][rows, h, :], BT[pb_][rows, c, :], eng=PL)
                                P.cp(KTh[pb_][rows, h, :], KT[pb_][rows, c, :], eng=PL)
                                yield

                    def p4():
                        ptb = P.psum(2, [128, 4, 128], BF16)
                        ptv = P.psum(2, [128, 4, 128], BF16, off=1024)
                        for c in range(2):
                            P.tr(ptb[:, c, :], BT[pb_][:, c, :], idb)
                            P.tr(ptb[:, 2 + c, :], KT[pb_][:, c, :], idb)
                            P.tr(ptv[:, c, :], vh[c][:, tj], idb)
                            P.tr(ptv[:, 2 + c, :], vl[c][:, tj], idb)
                        P.cp(Btm[pb_], ptb[:, 0:2, :].re("p a b -> p (a b)"), eng="act")
                        P.cp(Ktm[pb_], ptb[:, 2:4, :].re("p a b -> p (a b)"), eng="act")
                        P.cp(Vb[pb_], ptv[:, 0:2, :].re("p a b -> p (a b)"), eng="act")
                        yield

                    def p5():
                        psb = P.psum(2, [128, 4, 128], BF16)
                        for c in range(2):
                            P.tr(psb[:, c, :], bh[c][:, tj], idb)
                            P.tr(psb[:, 2 + c, :], bl[c][:, tj], idb)
                        P.cp(bon[pb_], psb[:, 0:2, :].re("p a b -> p (a b)"))
                        P.tt(bon[pb_], bon[pb_], psb[:, 2:4, :].re("p a b -> p (a b)"), ALU.add)
                        yield

                    return [p1, p2, p3, p4, p5]

                def stage_a(j):
                    pb_ = (g * 4 + j) % 2
                    psAb = P.psum(4, [128, 4, 256])
                    psAk = P.psum(6, [128, 4, 256])
                    psN = P.psum(3, [128, 4, 128])
                    for h in range(4):
                        c = h // 2
                        P.mm(psAb[:, h, :], BTh[pb_][:, h, :], AR[pb_][:, h, :])
                        P.mm(psAk[:, h, :], KTh[pb_][:, h, :], AR[pb_][:, h, :])
                        P.mm(psN[:, h, :], AR[pb_][:, h, 0:128], BT[pb_][:, c, :])
                    P.tt(AbT[pb_], psAb, m2b, ALU.mult)
                    P.tt(Qp0[pb_], psN, mTb, ALU.mult)
                    P.tt(AkT[pb_], psAk, m2b, ALU.mult)

                def inverse_pair(js):
                    pbs = [(g * 4 + j) % 2 for j in js]
                    Qs = [AbT[pb_][:, :, 0:128] for pb_ in pbs]
                    Qps = [Qp0[pb_] for pb_ in pbs]
                    for t_, pb_ in enumerate(pbs):
                        P.tt(Tt[pb_], Qs[t_], idb4, ALU.add, eng=PL)
                        P.tt(Tn[pb_], Qps[t_], idb4, ALU.add, eng=PL)
                    for s_ in range(6):
                        lastst = (s_ == 5)
                        psQ = [P.psum(4 * t_, [128, 4, 128]) for t_ in range(2)]
                        psQp = [P.psum(4 * t_ + 1, [128, 4, 128]) for t_ in range(2)]
                        psT = [P.psum(4 * t_ + 2, [128, 4, 128]) for t_ in range(2)]
                        psTn = [P.psum(4 * t_ + 3, [128, 4, 128]) for t_ in range(2)]
                        for t_, pb_ in enumerate(pbs):
                            for h in range(4):
                                P.mm(psQ[t_][:, h, :], Qps[t_][:, h, :], Qs[t_][:, h, :])
                            if not lastst:
                                for h in range(4):
                                    P.mm(psQp[t_][:, h, :], Qs[t_][:, h, :], Qps[t_][:, h, :])
                        Qn = [Qb[pb_][s_ % 2] for pb_ in pbs]
                        Qpn = [Qpb[pb_][s_ % 2] for pb_ in pbs]
                        for t_, pb_ in enumerate(pbs):
                            P.cp(Qn[t_], psQ[t_], eng="act")
                            if not lastst:
                                P.cp(Qpn[t_], psQp[t_])
                        for t_, pb_ in enumerate(pbs):
                            for h in range(4):
                                P.mm(psT[t_][:, h, :], Tn[pb_][:, h, :], Qn[t_][:, h, :])
                            if not lastst:
                                for h in range(4):
                                    P.mm(psTn[t_][:, h, :], Tt[pb_][:, h, :], Qpn[t_][:, h, :])
                        for t_, pb_ in enumerate(pbs):
                            P.tt(Tt[pb_], Tt[pb_], psT[t_], ALU.add)
                            if not lastst:
                                P.tt(Tn[pb_], Tn[pb_], psTn[t_], ALU.add)
                        Qs, Qps = Qn, Qpn

                psY_of = {}

                def chain(j):
                    pb_ = (g * 4 + j) % 2
                    psX = P.psum(0, [128, 4, 64])
                    psU = P.psum(0, [128, 4, 64], off=1024)
                    psY = P.psum(1, [128, 4, 64])
                    psH = P.psum(1, [128, 2, 128], off=1024)
                    for h in range(4):
                        c = h // 2
                        hs = slice(h * 64, (h + 1) * 64)
                        P.mm(psX[:, h, :], AR[pb_][:, h, 0:128], Hb[:, c, 64 * (h % 2):64 * (h % 2) + 64], start=True, stop=False)
                        P.mm(psX[:, h, :], AkT[pb_][:, h, 0:128], Vb[pb_][:, hs], start=False, stop=True)
                    yield
                    P.cp(Xb[pb_], psX.re("p a b -> p (a b)"), eng="act")
                    yield
                    for h in range(4):
                        hs = slice(h * 64, (h + 1) * 64)
                        P.mm(psU[:, h, :], Tt[pb_][:, h, :], Xb[pb_][:, hs])
                    yield
                    P.cp(Ub[pb_], psU.re("p a b -> p (a b)"), eng="act")
                    yield
                    for c in range(2):
                        P.mm(psH[:, c, :], Btm[pb_][:, c * 128:(c + 1) * 128], Ub[pb_][:, c * 128:(c + 1) * 128], start=True, stop=False)
                        P.mm(psH[:, c, :], Ktm[pb_][:, c * 128:(c + 1) * 128], Vb[pb_][:, c * 128:(c + 1) * 128], start=False, stop=True)
                    for h in range(4):
                        c = h // 2
                        hs = slice(h * 64, (h + 1) * 64)
                        P.mm(psY[:, h, :], AR[pb_][:, h, 128:256], Hb[:, c, 64 * (h % 2):64 * (h % 2) + 64], start=True, stop=False)
                        P.mm(psY[:, h, :], AbT[pb_][:, h, 128:256], Ub[pb_][:, hs], start=False, stop=False)
                        P.mm(psY[:, h, :], AkT[pb_][:, h, 128:256], Vb[pb_][:, hs], start=False, stop=True)
                    yield
                    for c in range(2):
                        P.tt(tH, psH[:, c, :], Hm[:, c, :], ALU.add)
                        P.ts(Hm[:, c, :], tH, E[pb_][:, c, 127:128], ALU.mult)
                        P.cp(Hb[:, c, :], Hm[:, c, :])
                        yield
                    psY_of[j] = psY

                def epilogue(j):
                    psY = psY_of[j]
                    i = g * 4 + j
                    pb_ = i % 2
                    tl = slice(i * 128, (i + 1) * 128)
                    y_, q_, n_, s_t = ysb[pb_], ysq[pb_], yn[pb_], st[pb_]
                    P.cp(y_, psY)
                    yield
                    P.act(q_, y_, AF.Square)
                    P.red(s_t[:, 0, :], y_)
                    P.red(s_t[:, 1, :], q_)
                    yield
                    P.ts(s_t[:, 2, :], s_t[:, 0, :], 1.0 / 64, ALU.mult)
                    P.tt(s_t[:, 3, :], s_t[:, 2, :], s_t[:, 2, :], ALU.mult)
                    P.stt(s_t[:, 4, :], s_t[:, 1, :], 1.0 / 64, s_t[:, 3, :], ALU.mult, ALU.subtract)
                    rsqrt(s_t[:, 4, :], s_t[:, 4, :], 1.0, gneps_t)
                    yield
                    P.tt(n_, y_, s_t[:, 2, :].unsq(2).bc([128, 4, 64]), ALU.subtract)
                    P.tt(n_, n_, s_t[:, 4, :].unsq(2).bc([128, 4, 64]), ALU.mult)
                    yield
                    nf = n_.re("p a b -> p (a b)")
                    P.tt(nf, nf, bvt[:, BV_LNW:BV_LNW + 256], ALU.mult)
                    P.tt(nf, nf, bvt[:, BV_LNB:BV_LNB + 256], ALU.add)
                    P.tt(nf, nf, bon[pb_], ALU.add)
                    P.tt(ygb[pb_], nf, g_tm[pb_], ALU.mult)
                    yield
                    pty = P.psum(2, [128, 2, 128], BF16)
                    for c in range(2):
                        P.tr(pty[:, c, :], ygb[pb_][:, c * 128:(c + 1) * 128], idb)
                    P.cp(ygT[:, :, tl], pty, eng="act")
                    yield

                def run_all(*gens):
                    gens = list(gens)
                    while gens:
                        for g_ in list(gens):
                            try:
                                next(g_)
                            except StopIteration:
                                gens.remove(g_)

                def prep_gen(j):
                    for p_ in prep_pieces(j):
                        yield from p_()

                def ep_gen(j):
                    yield from epilogue(j)

                for jp in (0, 2):
                    run_all(prep_gen(jp), prep_gen(jp + 1))
                    for j in (jp, jp + 1):
                        stage_a(j)
                    inverse_pair((jp, jp + 1))
                    run_all(chain(jp))
                    run_all(chain(jp + 1), ep_gen(jp))
                    run_all(ep_gen(jp + 1))
        for l in range(nlayers):
            last = (l == DEPTH - 1)
            P.release(base_mark)
            pv = P.tile([128, PV_N], F32, "pv")
            P.dma(pv, DR("pvec", dr["pvec"][l]))
            bvt = P.tile([128, BV_N], F32, "bvt")
            P.dma(bvt, DR("bvec", dr["bvec"][l].partition_broadcast(128)))
            layer_mark0 = P.mark()
            ygT = P.tile([128, 2, S], BF16, "ygT")
            cvT = P.tile([128, 2, S], BF16, "cvT")
            layer_mark = P.mark()

            if l == 0:
                xsrc = lambda i: DR("x", dr["x"][i * 128:(i + 1) * 128, :], i, i + 1)
            else:
                xsrc = lambda i: XL1(xl1_ap[i * 128:(i + 1) * 128, :], i, i + 1)
            xdst = (lambda i: OUT(out_ap[i * 128:(i + 1) * 128, :], i, i + 1)) if last else \
                   (lambda i: XL1(xl1_ap[i * 128:(i + 1) * 128, :], i, i + 1))
            norm_phase(xsrc, "attn_norm", l)
            if dbg == f"hT{l}":
                tmpd = P.tile([128, 2048], F32, "tmpd")
                P.cp(tmpd, hT[:, 0, :])
                dump(tmpd, 2048)

            if "rwkv" in phases:
                P.release(layer_mark)
                rwkv_phase(l, pv, bvt, ygT)
                if dbg == f"ygT{l}":
                    P.release(layer_mark + 20480)
                    tmpd = P.tile([128, 4096], F32, "tmpd")
                    P.cp(tmpd[:, 0:2048], ygT[:, 0, :])
                    P.cp(tmpd[:, 2048:4096], ygT[:, 1, :])
                    dump(tmpd, 4096)

            if "conv" in phases:
                P.release(layer_mark)
                Wcv = P.tile([128, 8, 768], BF16, "Wcv")
                wload(Wcv, "w_in", l, cols=(4768, 5536), pat="(kc p) n -> p kc n", p=128)
                bg = P.tile([128, S], F32, "bg")
                u = P.tile([128, S + 2], F32, "u")
                xc = P.tile([128, S], F32, "xc")
                yv = P.tile([128, S], F32, "yv")
                P.memset(u[:, 0:2], 0.0)
                for fc in range(2):
                    for part, dst in ((0, bg), (1, u[:, 2:S + 2]), (2, xc)):
                        for tc in range(4):
                            ps = P.psum((part * 4 + tc) % 4, [128, 512])
                            c0 = part * 256 + fc * 128
                            for kc in range(8):
                                P.mm(ps, Wcv[:, kc, c0:c0 + 128], hT[:, kc, tc * 512:(tc + 1) * 512],
                                     start=(kc == 0), stop=(kc == 7))
                            P.cp(dst[:, tc * 512:(tc + 1) * 512], ps, eng="act")
                    P.tt(u[:, 2:S + 2], u[:, 2:S + 2], xc, ALU.mult)
                    cw = lambda j: pv[:, PV_CW + fc * 3 + j:PV_CW + fc * 3 + j + 1]
                    P.ts(yv, u[:, 2:S + 2], cw(2), ALU.mult)
                    P.stt(yv, u[:, 1:S + 1], cw(1), yv, ALU.mult, ALU.add)
                    P.stt(yv, u[:, 0:S], cw(0), yv, ALU.mult, ALU.add)
                    P.tt(cvT[:, fc, :], yv, bg, ALU.mult)
                if dbg == f"cvT{l}":
                    P.release(layer_mark)
                    tmpd = P.tile([128, 4096], F32, "tmpd")
                    P.cp(tmpd[:, 0:2048], cvT[:, 0, :])
                    P.cp(tmpd[:, 2048:4096], cvT[:, 1, :])
                    dump(tmpd, 4096)

            P.release(layer_mark)
            oT = P.tile([64, 8, S], BF16, "oT")
            layer_mark2 = P.mark()
            if "mla" in phases:
                P.release(layer_mark2)
                Wc = P.tile([128, 8, 672], BF16, "Wc")
                wload(Wc, "w_in", l, cols=(3072, 3744), pat="(kc p) n -> p kc n", p=128)
                Wqb = P.tile([128, 3, 768], BF16, "Wqb")
                wload(Wqb, "mla_wq_b", l, pat="(kc p) n -> p kc n", p=128)
                Wkvb = P.tile([128, 2, 1024], BF16, "Wkvb")
                wload(Wkvb, "mla_wkv_b", l, pat="(kc p) n -> p kc n", p=128)
                cT = P.tile([128, 5, S], BF16, "cT")
                msa = P.tile([128, 3, NT], F32, "msa")
                rq = P.tile([128, NT], F32, "rq")
                rq2 = P.tile([128, NT], F32, "rq2")
                rkv = P.tile([128, NT], F32, "rkv")
                P.memset(msa, 0.0)
                for m in range(5):
                    for tc in range(4):
                        ps = P.psum((m * 4 + tc) % 2, [128, 512])
                        for kc in range(8):
                            P.mm(ps, Wc[:, kc, m * 128:(m + 1) * 128], hT[:, kc, tc * 512:(tc + 1) * 512],
                                 start=(kc == 0), stop=(kc == 7))
                        P.act(cT[:, m, tc * 512:(tc + 1) * 512], ps, AF.Copy, scale=pv[:, PV_GQA + m:PV_GQA + m + 1])
                mla_mark = P.mark()
                sqj = P.tile([128, 640], F32, "sqj")
                for i in range(NT):
                    ps1 = P.psum(2 + (i % 2) * 2, [128, 512])
                    ps2 = P.psum(3 + (i % 2) * 2, [128, 128])
                    for kc in range(8):
                        P.mm(ps1, hT[:, kc, i * 128:(i + 1) * 128], Wc[:, kc, 0:512], start=(kc == 0), stop=(kc == 7))
                    for kc in range(8):
                        P.mm(ps2, hT[:, kc, i * 128:(i + 1) * 128], Wc[:, kc, 512:640], start=(kc == 0), stop=(kc == 7))
                    P.act(sqj[:, 0:384], ps1[:, 0:384], AF.Square, accum=msa[:, 0, i:i + 1])
                    P.act(sqj[:, 384:512], ps1[:, 384:512], AF.Square, accum=msa[:, 1, i:i + 1])
                    P.act(sqj[:, 512:640], ps2, AF.Square, accum=msa[:, 2, i:i + 1])
                rsqrt(rq, msa[:, 0, :], 1.0 / 384, eps_t)
                P.tt(rq2, rq, rq, ALU.mult)
                P.ts(rq2, rq2, 1.0 / 96, ALU.mult)
                P.tt(rkv, msa[:, 1, :], msa[:, 2, :], ALU.add)
                rsqrt(rkv, rkv, 1.0 / 256, eps_t)
                gq = bvt[:, BV_GQ:BV_GQ + 96]
                gk = bvt[:, BV_GK:BV_GK + 96]
                for hh in range(2):
                    P.release(mla_mark)
                    qT = P.tile([96, 4, S], BF16, "qT")
                    kT = P.tile([96, 4, S], BF16, "kT")
                    vtm = P.tile([128, NT, 4, 64], BF16, "vtm")
                    def two_(shape, dt, nm):
                        return [P.tile(shape, dt, f"{nm}{j_}") for j_ in range(2)]
                    sq_ = two_([128, 4, 96], F32, "sq")
                    sqk_ = two_([128, 4, 96], F32, "sqk")
                    qn_ = two_([128, 4, 96], F32, "qn")
                    kn_ = two_([128, 4, 96], F32, "kn")
                    qb_ = two_([128, 4, 96], BF16, "qb")
                    kb_ = two_([128, 4, 96], BF16, "kb")
                    s4_ = two_([128, 4], F32, "s4")
                    s4k_ = two_([128, 4], F32, "s4k")
                    r1_ = two_([128, 4, 16], F32, "r1")
                    r2_ = two_([128, 4, 16], F32, "r2")
                    r3_ = two_([128, 4, 16], F32, "r3")
                    r4_ = two_([128, 4, 16], F32, "r4")

                    def rope(xn_, xb_, i, ta, tb, eng):
                        cosb = cosT[:, i, :].unsq(1).bc([128, 4, 16])
                        sinb = sinT[:, i, :].unsq(1).bc([128, 4, 16])
                        x1, x2 = xn_[:, :, 64:80], xn_[:, :, 80:96]
                        return [
                            lambda: P.cp(xb_[:, :, 0:64], xn_[:, :, 0:64], eng=eng),
                            lambda: P.tt(ta, x1, cosb, ALU.mult, eng=eng),
                            lambda: P.tt(tb, x2, sinb, ALU.mult, eng=eng),
                            lambda: P.tt(xb_[:, :, 64:80], ta, tb, ALU.subtract, eng=eng),
                            lambda: P.tt(ta, x1, sinb, ALU.mult, eng=eng),
                            lambda: P.tt(tb, x2, cosb, ALU.mult, eng=eng),
                            lambda: P.tt(xb_[:, :, 80:96], ta, tb, ALU.add, eng=eng),
                        ]

                    def proj(i):
                        tl_ = slice(i * 128, (i + 1) * 128)
                        b0 = 0 if i % 2 == 0 else 5
                        psq_ = P.psum(b0, [128, 4, 96])
                        pskv_ = P.psum(b0 + 1, [128, 4, 128])
                        pspe_ = P.psum(b0 + 2, [128, 32])
                        for kc in range(3):
                            P.mm(psq_, cT[:, kc, tl_], Wqb[:, kc, hh * 384:(hh + 1) * 384], start=(kc == 0), stop=(kc == 2))
                        for kc in range(2):
                            P.mm(pskv_, cT[:, 3 + kc, tl_], Wkvb[:, kc, hh * 512:(hh + 1) * 512], start=(kc == 0), stop=(kc == 1))
                        for kc in range(8):
                            P.mm(pspe_, hT[:, kc, tl_], Wc[:, kc, 640:672], start=(kc == 0), stop=(kc == 7))
                        return psq_, pskv_, pspe_

                    chains = []
                    for i in range(NT):
                        tl = slice(i * 128, (i + 1) * 128)
                        psq, pskv, pspe = proj(i)
                        pi_ = i % 2
                        sq, sqk, qn, kn, qb, kb = sq_[pi_], sqk_[pi_], qn_[pi_], kn_[pi_], qb_[pi_], kb_[pi_]
                        s4, s4k, r1, r2, r3, r4 = s4_[pi_], s4k_[pi_], r1_[pi_], r2_[pi_], r3_[pi_], r4_[pi_]
                        pt = P.psum(3, [96, 4, 128], BF16, off=1024 * pi_)
                        pt2 = P.psum(4, [96, 4, 128], BF16, off=1024 * pi_)

                        def qpath(psq=psq, sq=sq, s4=s4, qn=qn, qb=qb, r1=r1, r2=r2, pt=pt, i=i, tl=tl):
                            ops = [
                                lambda: P.act(sq, psq, AF.Square),
                                lambda: P.red(s4, sq),
                                lambda: P.act(s4, s4, AF.Sqrt, bias=eps_t, scale=rq2[:, i:i + 1]),
                                lambda: P.recip(s4, s4),
                                lambda: P.ts(s4, s4, rq[:, i:i + 1], ALU.mult),
                                lambda: P.tt(qn, psq, s4.unsq(2).bc([128, 4, 96]), ALU.mult),
                                lambda: P.tt(qn, qn, gq.unsq(1).bc([128, 4, 96]), ALU.mult),
                            ] + rope(qn, qb, i, r1, r2, "pool")
                            ops += [(lambda h=h: P.tr(pt[:, h, :], qb[:, h, :], idb)) for h in range(4)]
                            ops.append(lambda: P.cp(qT[:, :, tl], pt, eng="act"))
                            return ops

                        def kpath(pskv=pskv, pspe=pspe, sqk=sqk, s4k=s4k, kn=kn, kb=kb, r3=r3, r4=r4, pt2=pt2, i=i, tl=tl):
                            ops = [
                                lambda: P.ts(kn[:, :, 0:64], pskv[:, :, 0:64], rkv[:, i:i + 1], ALU.mult),
                                lambda: P.ts(vtm[:, i, :, :], pskv[:, :, 64:128], rkv[:, i:i + 1], ALU.mult),
                                lambda: P.cp(kn[:, :, 64:96], pspe.unsq(1).bc([128, 4, 32])),
                                lambda: P.act(sqk, kn, AF.Square),
                                lambda: P.red(s4k, sqk),
                                lambda: P.act(s4k, s4k, AF.Sqrt, bias=eps_t, scale=1.0 / 96),
                                lambda: P.recip(s4k, s4k),
                                lambda: P.tt(kn, kn, s4k.unsq(2).bc([128, 4, 96]), ALU.mult),
                                lambda: P.tt(kn, kn, gk.unsq(1).bc([128, 4, 96]), ALU.mult),
                            ] + rope(kn, kb, i, r3, r4, "pool")
                            ops += [(lambda h=h: P.tr(pt2[:, h, :], kb[:, h, :], idb)) for h in range(4)]
                            ops.append(lambda: P.cp(kT[:, :, tl], pt2, eng="act"))
                            return ops

                        chains += [kpath(), qpath()]
                        if i % 2 == 1:
                            for n_ in range(max(len(c_) for c_ in chains)):
                                for c_ in chains:
                                    if n_ < len(c_):
                                        c_[n_]()
                            chains = []
                    pTb = [P.tile([128, 512], BF16, f"pT{j}") for j in range(4)]
                    rz = P.tile([64, 512], F32, "rz")
                    it = 0
                    for h in range(4):
                        for qc in range(4):
                            pso = P.psum(4 + (it % 2) * 2, [64, 512])
                            psz = P.psum(5 + (it % 2) * 2, [64, 512])
                            it += 1
                            nk = 4 * qc + 4
                            def score(kt):
                                c0 = max(0, kt - 4 * qc) * 128
                                pss = P.psum(kt % 4, [128, 512])
                                P.mm(pss[:, c0:512], kT[:, h, kt * 128:(kt + 1) * 128], qT[:, h, qc * 512 + c0:(qc + 1) * 512])
                                return pss
                            nxt = score(0)
                            for kt in range(nk):
                                c0 = max(0, kt - 4 * qc) * 128
                                pss = nxt
                                if kt + 1 < nk:
                                    nxt = score(kt + 1)
                                pT = pTb[kt % 4]
                                P.act(pT[:, c0:512], pss[:, c0:512], AF.Exp, scale=float(96 ** -0.5))
                                if kt >= 4 * qc:
                                    P.tt(pT[:, c0:c0 + 128], pT[:, c0:c0 + 128], cmaskb, ALU.mult, eng="pool")
                                P.mm(pso[:, c0:512], vtm[:, kt, h, :], pT[:, c0:512], start=(kt == 0), stop=(kt == nk - 1))
                                P.mm(psz[:, c0:512], onesb, pT[:, c0:512], start=(kt == 0), stop=(kt == nk - 1))
                            P.recip(rz, psz)
                            P.tt(oT[:, hh * 4 + h, qc * 512:(qc + 1) * 512], pso, rz, ALU.mult)
                if dbg == f"oT{l}":
                    P.release(layer_mark2)
                    tmpd = P.tile([64, 4096], F32, "tmpd")
                    P.cp(tmpd[:, 0:2048], oT[:, 0, :])
                    P.cp(tmpd[:, 2048:4096], oT[:, 7, :])
                    dump(tmpd, 4096)
            if "merge" in phases:
                P.release(layer_mark2)
                mT = P.tile([128, 8, S], BF16, "mT")
                merge_mark = P.mark()
                Wo = P.tile([64, 8, D], BF16, "Wo")
                wload(Wo, "mla_w_o", l, pat="(h p) n -> p h n", p=64)
                Wro = P.tile([128, 2, D], BF16, "Wro")
                wload(Wro, "rwkv_w_o", l, pat="(c p) n -> p c n", p=128)
                Wco = P.tile([128, 2, D], BF16, "Wco")
                wload(Wco, "conv_w_o", l, pat="(c p) n -> p c n", p=128)
                Wg = [P.tile([128, 8, 3, 128], BF16, f"Wg{j}") for j in range(2)]
                gs = [[P.tile([128, 512], F32, f"gs{s_}{j}") for j in range(3)] for s_ in range(2)]
                mas = [P.tile([128, 512], F32, f"ma{s_}") for s_ in range(2)]
                mbs = [P.tile([128, 512], F32, f"mb{s_}") for s_ in range(2)]

                def load_wg(dc_):
                    for j in range(3):
                        wload(Wg[dc_ % 2][:, :, j, :], "w_in", l, cols=(j * D + dc_ * 128, j * D + (dc_ + 1) * 128),
                              pat="(kc p) n -> p kc n", p=128)
                load_wg(0)
                it_ = 0
                gcnt = 0
                for dc in range(8):
                    wg = Wg[dc % 2]
                    if dc + 1 < 8:
                        load_wg(dc + 1)
                    for tc in range(4):
                        tk = slice(tc * 512, (tc + 1) * 512)
                        set_ = it_ % 2
                        it_ += 1
                        ma, mb, gs_ = mas[set_], mbs[set_], gs[set_]
                        for j in range(3):
                            psG = P.psum(6 + gcnt % 2, [128, 512])
                            gcnt += 1
                            for kc in range(8):
                                P.mm(psG, wg[:, kc, j, :], hT[:, kc, tk], start=(kc == 0), stop=(kc == 7))
                            P.act(gs_[j], psG, AF.Sigmoid)
                        psA = P.psum(3 * set_, [128, 512])
                        psB = P.psum(3 * set_ + 1, [128, 512])
                        psC = P.psum(3 * set_ + 2, [128, 512])
                        for h in range(8):
                            P.mm(psA, Wo[:, h, dc * 128:(dc + 1) * 128], oT[:, h, tk], start=(h == 0), stop=(h == 7))
                        for c in range(2):
                            P.mm(psB, Wro[:, c, dc * 128:(dc + 1) * 128], ygT[:, c, tk], start=(c == 0), stop=(c == 1))
                        for c in range(2):
                            P.mm(psC, Wco[:, c, dc * 128:(dc + 1) * 128], cvT[:, c, tk], start=(c == 0), stop=(c == 1))
                        P.tt(ma, psA, gs_[0], ALU.mult)
                        P.tt(mb, psB, gs_[1], ALU.mult)
                        P.tt(ma, ma, mb, ALU.add)
                        P.tt(mb, psC, gs_[2], ALU.mult)
                        P.tt(mT[:, dc, tk], ma, mb, ALU.add)
                if dbg == f"mT{l}":
                    tmpd = P.tile([128, 2048], F32, "tmpd")
                    P.cp(tmpd, mT[:, 3, :])
                    dump(tmpd, 2048)
                P.release(merge_mark)
                Wout = P.tile([128, 8, D], BF16, "Wout")
                wload(Wout, "w_out", l, pat="(kc p) n -> p kc n", p=128)
                xin = [P.tile([128, D], F32, f"xin{j}") for j in range(3)]
                nst = norm_setup("mlp_norm", l)
                for i in range(NT):
                    tl = slice(i * 128, (i + 1) * 128)
                    x_ = xin[i % 3]
                    P.dma(x_, xsrc(i))
                    for nb in range(2):
                        ps = P.psum((i % 2) * 2 + nb, [128, 512])
                        for dc in range(8):
                            P.mm(ps, mT[:, dc, tl], Wout[:, dc, nb * 512:(nb + 1) * 512], start=(dc == 0), stop=(dc == 7))
                        P.tt(x_[:, nb * 512:(nb + 1) * 512], x_[:, nb * 512:(nb + 1) * 512], ps, ALU.add)
                    P.dma(XMID(xmid_ap[tl, :], i, i + 1), x_)
                    norm_tile(i, x_, nst, 4 + i % 2)

            if "mlp" in phases:
                P.release(layer_mark0)
                if "merge" not in phases:
                    norm_phase(lambda i: XMID(xmid_ap[i * 128:(i + 1) * 128, :], i, i + 1), "mlp_norm", l)
                aT = P.tile([128, 32, 1024], BF16, "aT")
                Wu = [P.tile([128, 8, 1024], BF16, f"Wu{j}") for j in range(2)]
                Wd = [P.tile([128, 4, D], BF16, f"Wd{j}") for j in range(2)]
                rl = [P.tile([128, 512], F32, f"rl{j}") for j in range(2)]
                xin = [P.tile([128, D], F32, f"xin{j}") for j in range(2)]
                sched = []
                for th in range(2):
                    for fg in range(4):
                        sched.append(("u", th, fg))
                    for tg in range(2):
                        for fg in range(8):
                            sched.append(("d", th, tg, fg))
                bufs = {}
                cnt = {"u": 0, "d": 0}

                def issue(k):
                    it_ = sched[k]
                    if it_[0] == "u":
                        w_ = Wu[cnt["u"] % 2]
                        cnt["u"] += 1
                        wload(w_, "w_up", l, cols=(it_[2] * 1024, (it_[2] + 1) * 1024), pat="(kc p) n -> p kc n", p=128)
                    else:
                        w_ = Wd[cnt["d"] % 2]
                        cnt["d"] += 1
                        wload(w_, "w_down", l, rows=(it_[3] * 512, (it_[3] + 1) * 512), pat="(f p) n -> p f n", p=128)
                    bufs[k] = w_

                issue(0)
                for k, it_ in enumerate(sched):
                    if k + 1 < len(sched):
                        issue(k + 1)
                    w_ = bufs.pop(k)
                    if it_[0] == "u":
                        _, th, fg = it_
                        t0 = th * 1024
                        for fi in range(8):
                            f = fg * 8 + fi
                            for tc in range(2):
                                ps = P.psum((fi * 2 + tc) % 4, [128, 512])
                                for kc in range(8):
                                    P.mm(ps, w_[:, kc, fi * 128:(fi + 1) * 128], hT[:, kc, t0 + tc * 512:t0 + (tc + 1) * 512],
                                         start=(kc == 0), stop=(kc == 7))
                                r_ = rl[(fi * 2 + tc) % 2]
                                P.act(r_, ps, AF.Relu)
                                P.tt(aT[:, f, tc * 512:(tc + 1) * 512], r_, r_, ALU.mult)
                    else:
                        _, th, tg, fg = it_
                        for ti in range(4):
                            tl = slice((tg * 4 + ti) * 128, (tg * 4 + ti + 1) * 128)
                            for fi in range(4):
                                for nb in range(2):
                                    ps = P.psum(ti * 2 + nb, [128, 512])
                                    P.mm(ps, aT[:, fg * 4 + fi, tl], w_[:, fi, nb * 512:(nb + 1) * 512],
                                         start=(fg == 0 and fi == 0), stop=(fg == 7 and fi == 3))
                        if fg == 7:
                            for ti in range(4):
                                i = th * 8 + tg * 4 + ti
                                x_ = xin[ti % 2]
                                P.dma(x_, XMID(xmid_ap[i * 128:(i + 1) * 128, :], i, i + 1))
                                for nb in range(2):
                                    ps = P.psum(ti * 2 + nb, [128, 512])
                                    P.tt(x_[:, nb * 512:(nb + 1) * 512], x_[:, nb * 512:(nb + 1) * 512], ps, ALU.add)
                                P.dma(xdst(i), x_)
        P.final_wait()
        P.replay(block)
    return nc


def make_inputs(inp):
    inp = {k: np.asarray(v) for k, v in inp.items()}
    shared = {"consts": host_consts(),
              "pvec": np.stack([host_pvec(inp, l) for l in range(DEPTH)]),
              "bvec": np.stack([host_bvec(inp, l) for l in range(DEPTH)])}
    for k in ("attn_norm", "mlp_norm"):
        shared[k] = np.ascontiguousarray(inp[k], dtype=np.float32)
    for k in WNAMES:
        shared[k] = np.ascontiguousarray(inp[k], dtype=np.float32)
    maps = []
    for b in range(8):
        m = dict(shared)
        m["x"] = np.ascontiguousarray(inp["x"][b], dtype=np.float32)
        m["pos"] = np.ascontiguousarray(inp["positions"][b].astype(np.int32).reshape(NT, 128).T)
        maps.append(m)
    return maps


_NC_CACHE = {}


def kernel(**inputs):
    maps = make_inputs(inputs)
    shapes = {k: v.shape for k, v in maps[0].items()}
    key = "main"
    if key not in _NC_CACHE:
        _NC_CACHE[key] = build(shapes)
    nc = _NC_CACHE[key]
    res = run_bass_kernel_spmd(nc, maps, core_ids=list(range(8)))
    return np.stack([r["out"] for r in res.results]).astype(np.float32)
```

```python
import numpy as np
import concourse.bass as bass
import concourse.mybir as mybir
from concourse.bass_utils import run_bass_kernel_spmd

F32 = mybir.dt.float32
BF16 = mybir.dt.bfloat16
I32 = mybir.dt.int32
AF = mybir.ActivationFunctionType
ALU = mybir.AluOpType
AX = mybir.AxisListType
DTB = {F32: 4, BF16: 2, I32: 4}


class V:
    def __init__(self, ap, arena, s, e):
        self.ap, self.arena, self.s, self.e = ap, arena, s, e

    def __getitem__(self, k):
        return V(self.ap[k], self.arena, self.s, self.e)

    def re(self, pat, **kw):
        return V(self.ap.rearrange(pat, **kw), self.arena, self.s, self.e)

    def bc(self, shape):
        return V(self.ap.to_broadcast(shape), self.arena, self.s, self.e)

    def unsq(self, ax):
        return V(self.ap.unsqueeze(ax), self.arena, self.s, self.e)

    def sub(self, s, e):
        return V(self.ap, self.arena, self.s + s, self.s + e)


class Prog:
    ENGS = ("pe", "act", "dve", "pool", "sp")
    NPOOL = 12

    def __init__(self, nc, sbuf_bytes=196608, same_engine_sync=True):
        self.nc = nc
        self.same = same_engine_sync
        self.ops = {e: [] for e in self.ENGS}
        self.cnt = {e: 0 for e in self.ENGS}
        self.seen = {e: {} for e in self.ENGS}
        self.recs = {}
        self.sb_off = 0
        self.sb_bytes = sbuf_bytes
        self.dma_n = {"sp": 0, "pool": 0}
        self.nwaits = 0
        self.ninstr = 0

    def setup(self, sb, ps, sems, dsems):
        self.sb, self.ps = sb, ps
        self.sem = sems
        self.dsem = dsems

    def mark(self):
        return self.sb_off

    def release(self, m):
        self.sb_off = m

    def tile(self, shape, dt, name=""):
        n = int(np.prod(shape[1:])) * DTB[dt]
        n4 = (n + 3) // 4 * 4
        s = self.sb_off
        assert s + n4 <= self.sb_bytes, f"SBUF overflow at {name}: {s}+{n4}"
        self.sb_off += n4
        self.hwm = max(getattr(self, 'hwm', 0), self.sb_off)
        ap = self.sb[0:shape[0], s // 4:(s + n4) // 4]
        if dt != F32:
            ap = ap.bitcast(dt)
        ap = ap[:, 0:int(np.prod(shape[1:]))]
        if len(shape) > 2:
            names = " ".join(f"d{i}" for i in range(len(shape) - 1))
            ap = ap.rearrange(f"p ({names}) -> p {names}", **{f"d{i}": shape[i + 1] for i in range(len(shape) - 1)})
        return V(ap, "sb", s, s + n4)

    def psum(self, bank, shape, dt=F32, off=0):
        n = int(np.prod(shape[1:])) * DTB[dt]
        s = bank * 2048 + off
        assert off + n <= 2048 * (8 - bank)
        ap = self.ps[0:shape[0], s // 4:(s + n + 3) // 4]
        if dt != F32:
            ap = ap.bitcast(dt)
        ap = ap[:, 0:int(np.prod(shape[1:]))]
        if len(shape) > 2:
            names = " ".join(f"d{i}" for i in range(len(shape) - 1))
            ap = ap.rearrange(f"p ({names}) -> p {names}", **{f"d{i}": shape[i + 1] for i in range(len(shape) - 1)})
        return V(ap, "ps", (s // 2048) * 2048, ((s + n + 2047) // 2048) * 2048)

    def dram(self, ap, name, s=0, e=1):
        return V(ap, "dram_" + name, s, e)

    def _deps(self, reads, writes):
        toks = []
        for v in reads:
            for r in self.recs.get(v.arena, []):
                if r[4] and r[0] < v.e and v.s < r[1]:
                    toks.append(r[2:4])
        for v in writes:
            for r in self.recs.get(v.arena, []):
                if r[0] < v.e and v.s < r[1]:
                    toks.append(r[2:4])
        return toks

    def _record(self, reads, writes, tok):
        for v in writes:
            L = self.recs.setdefault(v.arena, [])
            L[:] = [r for r in L if not (v.s <= r[0] and r[1] <= v.e)]
            L.append([v.s, v.e, tok[0], tok[1], True])
        for v in reads:
            L = self.recs.setdefault(v.arena, [])
            L[:] = [r for r in L if not ((not r[4]) and r[2] == tok[0] and v.s <= r[0] and r[1] <= v.e)]
            L.append([v.s, v.e, tok[0], tok[1], False])

    def _waits(self, eng, toks, skip_sem=None):
        need = {}
        for sem, val in toks:
            if sem is skip_sem:
                continue
            if self.seen[eng].get(sem, 0) < val:
                need[sem] = max(need.get(sem, 0), val)
        for sem, val in need.items():
            self.seen[eng][sem] = val
        self.nwaits += len(need)
        return list(need.items())

    def op(self, eng, fn, outs, ins):
        outs = [o for o in outs if o is not None]
        ins = [i for i in ins if isinstance(i, V)]
        toks = self._deps(ins, outs)
        sem = self.sem[eng]
        skip = sem if (eng == "pe" or not self.same) else None
        waits = self._waits(eng, toks, skip_sem=skip)
        self.cnt[eng] += 1
        seq = self.cnt[eng]
        self.seen[eng][sem] = max(self.seen[eng].get(sem, 0), 0)
        self.ops[eng].append((waits, fn, sem, 1))
        self._record(ins, outs, (sem, seq))
        self.ninstr += 1

    def dma(self, out, in_, q="sp", **kw):
        i = self.dma_n[q]
        self.dma_n[q] += 1
        pool = self.dsem[q]
        k = i % len(pool)
        sem = pool[k]
        val = 16 * (i // len(pool) + 1)
        toks = self._deps([in_], [out])
        toks.append((sem, val - 16))
        waits = self._waits(q, toks)
        oa, ia = out.ap, in_.ap
        self.ops[q].append((waits, lambda e: e.dma_start(out=oa, in_=ia, **kw), sem, 16))
        self._record([in_], [out], (sem, val))
        self.ninstr += 1

    def final_wait(self, eng="sp"):
        toks = []
        for q in ("sp", "pool"):
            n = self.dma_n[q]
            pool = self.dsem[q]
            for k in range(len(pool)):
                cntk = (n - k + len(pool) - 1) // len(pool) if n > k else 0
                if cntk:
                    toks.append((pool[k], 16 * cntk))
        for e in self.ENGS:
            if self.cnt[e]:
                toks.append((self.sem[e], self.cnt[e]))
        waits = self._waits(eng, toks)
        self.ops[eng].append((waits, None, None, 0))

    def replay(self, block):
        nc = self.nc

        def run(engobj, lst):
            for waits, fn, sem, inc in lst:
                for s, v in waits:
                    engobj.wait_ge(s, v)
                if fn is not None:
                    fn(engobj).then_inc(sem, inc)

        block.sync(lambda e: run(e, self.ops["sp"]))
        block.gpsimd(lambda e: run(e, self.ops["pool"]))
        block.tensor(lambda e: run(e, self.ops["pe"]))
        block.scalar(lambda e: run(e, self.ops["act"]))
        block.vector(lambda e: run(e, self.ops["dve"]))

    def mm(self, out, lhsT, rhs, start=True, stop=True):
        o, l, r = out.ap, lhsT.ap, rhs.ap
        self.op("pe", lambda e: e.matmul(o, l, r, start=start, stop=stop), [out], [lhsT, rhs] + ([] if start else [out]))

    def tr(self, out, in_, ident):
        o, i, d = out.ap, in_.ap, ident.ap
        self.op("pe", lambda e: e.transpose(o, i, d), [out], [in_, ident])

    def act(self, out, in_, func, bias=None, scale=None, accum=None, eng="act"):
        o, i = out.ap, in_.ap
        kw = {}
        if bias is not None:
            kw["bias"] = bias.ap if isinstance(bias, V) else bias
        if scale is not None:
            kw["scale"] = scale.ap if isinstance(scale, V) else scale
        if accum is not None:
            kw["accum_out"] = accum.ap
        self.op(eng, lambda e: e.activation(o, i, func, **kw), [out, accum], [in_, bias, scale])

    def tt(self, out, a, b, op, eng="dve"):
        o, x, y = out.ap, a.ap, b.ap
        self.op(eng, lambda e: e.tensor_tensor(o, x, y, op), [out], [a, b])

    def ts(self, out, a, s1, op0, s2=None, op1=None, accum=None, eng="dve"):
        o, x = out.ap, a.ap
        c1 = s1.ap if isinstance(s1, V) else s1
        c2 = s2.ap if isinstance(s2, V) else s2
        kw = {}
        if op1 is not None:
            kw["op1"] = op1
        if accum is not None:
            kw["accum_out"] = accum.ap
        self.op(eng, lambda e: e.tensor_scalar(o, x, c1, c2, op0, **kw), [out, accum], [a, s1, s2])

    def stt(self, out, a, s, b, op0, op1, eng="dve"):
        o, x, y = out.ap, a.ap, b.ap
        c = s.ap if isinstance(s, V) else s
        self.op(eng, lambda e: e.scalar_tensor_tensor(o, x, c, y, op0, op1), [out], [a, s, b])

    def cp(self, out, in_, eng="dve"):
        o, i = out.ap, in_.ap
        if eng == "act":
            self.op(eng, lambda e: e.copy(o, i), [out], [in_])
        else:
            self.op(eng, lambda e: e.tensor_copy(o, i), [out], [in_])

    def memset(self, out, val, eng="dve"):
        o = out.ap
        self.op(eng, lambda e: e.memset(o, val), [out], [])

    def red(self, out, in_, op=None, axis=None, eng="dve"):
        o, i = out.ap, in_.ap
        op = op or ALU.add
        axis = axis or AX.X
        self.op(eng, lambda e: e.tensor_reduce(o, i, axis, op), [out], [in_])

    def recip(self, out, in_):
        o, i = out.ap, in_.ap
        self.op("dve", lambda e: e.reciprocal(o, i), [out], [in_])
from contextlib import ExitStack

D = 1024
SBUF_BYTES = 211968
S = 2048
NT = 16
DEPTH = 2
IN_COLS = 5536
EPS = 1e-6
GN_EPS = 64e-5
PI = 3.141592653589793

PV_GQA, PV_GKVA, PV_MU, PV_KK, PV_KA, PV_A0, PV_RK, PV_V0, PV_VMU, PV_CW = 0, 3, 5, 13, 15, 17, 19, 21, 23, 24
PV_N = 30
BV_GQ, BV_GK, BV_W0, BV_LNW, BV_LNB = 0, 96, 192, 448, 704
BV_N = 960


def host_consts():
    i = np.arange(128)
    idf = np.eye(128, dtype=np.float32)
    strict = (i[:, None] < i[None, :]).astype(np.float32)
    incl = (i[:, None] <= i[None, :]).astype(np.float32)
    mask2 = np.concatenate([strict, incl], 1)
    maskT = (i[:, None] > i[None, :]).astype(np.float32)
    tri = np.concatenate([incl, strict], 1)
    blk = (i[:, None] // 64 == i[None, :] // 64).astype(np.float32)
    ind = np.zeros((128, 8), np.float32)
    for c_ in range(2):
        ind[i, c_ * 4 + 2 * c_ + i // 64] = 1.0
    freq = (10000.0 ** (-(np.arange(16, dtype=np.float32) * 2.0 / 32))).astype(np.float32)
    freq = np.broadcast_to(freq[None, :], (128, 16)).copy()
    c = np.concatenate([idf, mask2, maskT, tri, blk, ind, freq], 1).astype(np.float32)
    return np.ascontiguousarray(c)


C_IDF, C_MASK2, C_MASKT, C_TRI, C_BLK, C_IND, C_FREQ, C_N = 0, 128, 384, 512, 768, 896, 904, 920


def host_pvec(inp, l):
    pv = np.zeros((128, PV_N), np.float32)

    def put(col, vec):
        n = vec.shape[0] // 128
        pv[:, col:col + n] = vec.reshape(n, 128).T

    put(PV_GQA, inp["mla_q_a_norm"][l])
    put(PV_GKVA, inp["mla_kv_a_norm"][l])
    put(PV_MU, inp["rwkv_mu"][l])
    put(PV_KK, inp["rwkv_k_k"][l])
    put(PV_KA, inp["rwkv_k_a"][l])
    put(PV_A0, inp["rwkv_a0"][l])
    put(PV_RK, inp["rwkv_r_k"][l].reshape(256))
    if l > 0:
        put(PV_V0, inp["rwkv_v0"][l - 1])
        pv[0:32, PV_VMU] = inp["rwkv_v_mu"][l - 1]
    cw = inp["conv_w"][l]
    for fc in range(2):
        for j in range(3):
            pv[:, PV_CW + fc * 3 + j] = cw[j, fc * 128:(fc + 1) * 128]
    return pv


def host_bvec(inp, l):
    return np.concatenate([inp["mla_q_norm"][l], inp["mla_k_norm"][l],
                           inp["rwkv_w0"][l], inp["rwkv_ln_w"][l], inp["rwkv_ln_b"][l]]).astype(np.float32)[None, :]


WNAMES = ["w_in", "mla_wq_b", "mla_wkv_b", "mla_w_o", "rwkv_w2", "rwkv_a2", "rwkv_g2", "rwkv_w_o", "rwkv_v1",
          "rwkv_v2", "conv_w_o", "w_out", "w_up", "w_down"]


def build(shapes, dbg=None, nlayers=DEPTH, phases=("mla", "conv", "rwkv", "merge", "mlp")):
    nc = bass.Bass("TRN2", target_bir_lowering=False)
    dr = {}
    for k, shp in shapes.items():
        dt = I32 if k == "pos" else F32
        dr[k] = nc.dram_tensor(k, list(shp), dt, kind="ExternalInput").ap()
    out_ap = nc.dram_tensor("out", [S, D], F32, kind="ExternalOutput").ap()
    xmid_ap = nc.dram_tensor("xmid", [S, D], F32, kind="Internal").ap()
    xl1_ap = nc.dram_tensor("xl1", [S, D], F32, kind="Internal").ap()
    vf_ap = nc.dram_tensor("vfirst", [128, 2, S], F32, kind="Internal").ap()
    dbg_ap = nc.dram_tensor("dbg", [128, 4096], F32, kind="ExternalOutput").ap() if dbg else None

    with ExitStack() as es:
        sb = es.enter_context(nc.sbuf_tensor("sb", [128, SBUF_BYTES // 4], F32))
        ps_t = es.enter_context(nc.psum_tensor("ps", [128, 4096], F32))
        P = Prog(nc, sbuf_bytes=SBUF_BYTES, same_engine_sync=True)
        sems = {e: es.enter_context(nc.semaphore("s_" + e)) for e in Prog.ENGS}
        dsems = {q: [es.enter_context(nc.semaphore(f"d_{q}{i}")) for i in range(Prog.NPOOL)] for q in ("sp", "pool")}
        P.setup(sb, ps_t, sems, dsems)
        block = es.enter_context(nc.Block())

        def DR(name, ap=None, s=0, e=1):
            return P.dram(dr[name] if ap is None else ap, name, s, e)

        OUT = lambda ap, s, e: P.dram(ap, "out", s, e)
        XMID = lambda ap, s, e: P.dram(ap, "xmid", s, e)
        XL1 = lambda ap, s, e: P.dram(ap, "xl1", s, e)

        def dump(v, cols):
            if dbg_ap is not None:
                P.dma(P.dram(dbg_ap[0:v.ap.shape[0], 0:cols], "dbg"), v)

        cst = P.tile([128, C_N], F32, "cst")
        P.dma(cst, DR("consts"))
        idf = cst[:, C_IDF:C_IDF + 128]
        mask2 = cst[:, C_MASK2:C_MASK2 + 256]
        maskT = cst[:, C_MASKT:C_MASKT + 128]
        tri = cst[:, C_TRI:C_TRI + 256]
        blk = cst[:, C_BLK:C_BLK + 128]
        ind = cst[:, C_IND:C_IND + 8]
        freq = cst[:, C_FREQ:C_FREQ + 16]
        idb = P.tile([128, 128], BF16, "idb")
        P.cp(idb, idf)
        trib = P.tile([128, 256], BF16, "trib")
        P.cp(trib, tri)
        blkb = P.tile([128, 128], BF16, "blkb")
        P.cp(blkb, blk)
        cmaskb = P.tile([128, 128], BF16, "cmaskb")
        P.cp(cmaskb, mask2[:, 128:256])
        onesb = P.tile([128, 64], BF16, "onesb")
        P.memset(onesb, 1.0)
        pos_i = P.tile([128, NT], I32, "pos_i")
        P.dma(pos_i, DR("pos"))
        pos_f = P.tile([128, NT], F32, "pos_f")
        P.cp(pos_f, pos_i)
        ang = P.tile([128, NT, 16], F32, "ang")
        P.tt(ang, pos_f.unsq(2).bc([128, NT, 16]), freq.unsq(1).bc([128, NT, 16]), ALU.mult)
        sinT = P.tile([128, NT, 16], F32, "sinT")
        cosT = P.tile([128, NT, 16], F32, "cosT")
        tmpa = P.tile([128, NT, 16], F32, "tmpa")
        tmpk = P.tile([128, NT, 16], F32, "tmpk")
        tmpi = P.tile([128, NT, 16], I32, "tmpi")
        eps_t = P.tile([128, 1], F32, "eps_t")
        P.memset(eps_t, EPS)
        gneps_t = P.tile([128, 1], F32, "gneps_t")
        P.memset(gneps_t, GN_EPS)

        def sin_table(dst, shift):
            P.ts(tmpa, ang, shift, ALU.add)
            P.ts(tmpk, tmpa, 1.0 / (2 * PI), ALU.mult)
            P.cp(tmpi, tmpk)
            P.cp(tmpk, tmpi)
            P.stt(tmpa, tmpk, -6.28125, tmpa, ALU.mult, ALU.add)
            P.stt(tmpa, tmpk, -(2 * PI - 6.28125), tmpa, ALU.mult, ALU.add)
            P.ts(tmpk, tmpa, PI, ALU.is_gt, 2 * PI, ALU.mult)
            P.tt(tmpa, tmpa, tmpk, ALU.subtract)
            P.ts(tmpk, tmpa, -PI, ALU.is_lt, 2 * PI, ALU.mult)
            P.tt(tmpa, tmpa, tmpk, ALU.add)
            P.ts(tmpa, tmpa, PI, ALU.min, -PI, ALU.max)
            P.act(dst, tmpa, AF.Sin)

        sin_table(sinT, 0.0)
        sin_table(cosT, PI / 2)

        def rsqrt(out, in_, scale, bias_t):
            P.act(out, in_, AF.Sqrt, bias=bias_t, scale=scale)
            P.recip(out, out)

        hT = P.tile([128, 8, S], BF16, "hT")
        base_mark = P.mark()

        def wload(dst, name, l, rows=None, cols=None, pat=None, **kw):
            ap = dr[name][l]
            if rows is not None:
                ap = ap[rows[0]:rows[1]]
            if cols is not None:
                ap = ap[:, cols[0]:cols[1]]
            if pat:
                ap = ap.rearrange(pat, **kw)
            P.dma(dst, P.dram(ap, name), q="pool")

        def norm_setup(gname, l):
            gb = P.tile([128, D], F32, "gb")
            P.dma(gb, P.dram(dr[gname][l:l + 1, :].partition_broadcast(128), gname))
            junk = P.tile([128, D], F32, "junk")
            hb = [P.tile([128, D], BF16, f"hb{i}") for i in range(2)]
            ss = P.tile([128, NT], F32, "ss")
            P.memset(ss, 0.0)
            return gb, junk, hb, ss

        def norm_tile(i, x_, st_, psbank):
            gb, junk, hb, ss = st_
            P.act(junk, x_, AF.Square, accum=ss[:, i:i + 1])
            rsqrt(ss[:, i:i + 1], ss[:, i:i + 1], 1.0 / D, eps_t)
            h_ = hb[i % 2]
            P.stt(h_, x_, ss[:, i:i + 1], gb, ALU.mult, ALU.mult)
            pt = P.psum(psbank, [128, 8, 128], BF16)
            for c in range(8):
                P.tr(pt[:, c, :], h_[:, c * 128:(c + 1) * 128], idb)
            P.cp(hT[:, :, i * 128:(i + 1) * 128], pt, eng="act")

        def norm_phase(xsrc, gname, l):
            m0 = P.mark()
            st_ = norm_setup(gname, l)
            xt = [P.tile([128, D], F32, f"xt{i}") for i in range(2)]
            for i in range(NT):
                x_ = xt[i % 2]
                P.dma(x_, xsrc(i))
                norm_tile(i, x_, st_, i % 2)
            P.release(m0)

        def rwkv_phase(l, pv, bvt, ygT):
            STOP = 99
            PL = 'pool'
            DBGPT = ''
            dbgt = P.tile([128, 512], F32, "dbgt") if DBGPT else None
            dbg_done = []

            def dbgcopy(name, v, ncols):
                if name == DBGPT and not dbg_done:
                    dbg_done.append(1)
                    P.memset(dbgt, 0.0)
                    P.cp(dbgt[0:v.ap.shape[0], 0:ncols] if v.ap.base_partition() == 0 else dbgt[v.ap.base_partition():v.ap.base_partition() + v.ap.shape[0], 0:ncols], v)
                    dump(dbgt, 512)
            Wr = P.tile([128, 8, 1024], BF16, "Wr")
            wload(Wr, "w_in", l, cols=(3744, 4768), pat="(kc p) n -> p kc n", p=128)
            w2b = P.tile([64, 256], BF16, "w2b")
            wload(w2b, "rwkv_w2", l)
            a2b = P.tile([128, 256], BF16, "a2b")
            wload(a2b[64:128, :], "rwkv_a2", l)
            g2b = P.tile([128, 256], BF16, "g2b")
            wload(g2b, "rwkv_g2", l)
            if l > 0:
                v1b = P.tile([128, 8, 32], BF16, "v1b")
                wload(v1b, "rwkv_v1", l - 1, pat="(kc p) n -> p kc n", p=128)
                v2b = P.tile([32, 256], BF16, "v2b")
                wload(v2b, "rwkv_v2", l - 1)
                carv = P.tile([32, 1], F32, "carv")
                P.memset(carv, 0.0)
            carry = P.tile([128, 8], F32, "carry")
            P.memset(carry, 0.0)
            Hm = P.tile([128, 2, 128], F32, "Hm")
            Hb = P.tile([128, 2, 128], BF16, "Hb")
            P.memset(Hm, 0.0)
            P.memset(Hb, 0.0)
            raws = [P.tile([128, 513], F32, f"raw{j_}") for j_ in range(2)]
            raw = raws[0]
            mixed = [P.tile([128, 512], F32, f"mixed{m}") for m in range(8)]
            tanhb = P.tile([64, 512], BF16, "tanhb")
            xab = P.tile([128, 512], BF16, "xab")
            sgb = P.tile([128, 512], BF16, "sgb")
            aT = [P.tile([128, 512], F32, f"aT{c}") for c in range(2)]
            kk = [P.tile([128, 512], F32, f"kk{c}") for c in range(2)]
            bT = [P.tile([128, 512], F32, f"bT{c}") for c in range(2)]
            kp = [P.tile([128, 512], F32, f"kp{c}") for c in range(2)]
            rkr = [P.tile([128, 512], F32, f"rkr{c}") for c in range(2)]
            tAs = [P.tile([128, 512], F32, f"tA{j_}") for j_ in range(2)]
            tA = tAs[0]
            difs = [P.tile([128, 512], F32, f"dif{j_}") for j_ in range(2)]
            sphs = [P.tile([128, 512], BF16, f"sph{j_}") for j_ in range(2)]
            spls = [P.tile([128, 512], BF16, f"spl{j_}") for j_ in range(2)]
            vh = [P.tile([128, 512], BF16, f"vh{c}") for c in range(2)]
            vl = [P.tile([128, 512], BF16, f"vl{c}") for c in range(2)]
            bh = [P.tile([128, 512], BF16, f"bh{c}") for c in range(2)]
            bl = [P.tile([128, 512], BF16, f"bl{c}") for c in range(2)]
            C0 = 0.6065306597126334

            def split(x, hi, lo, eng="dve"):
                P.cp(hi, x, eng=eng)
                P.tt(lo, x, hi, ALU.subtract, eng=eng)
            tBs = [P.tile([128, 512], F32, f"tB{j_}") for j_ in range(2)]
            tB = tBs[0]
            def two(shape, dt, nm):
                return [P.tile(shape, dt, f"{nm}{j}") for j in range(2)]
            def one(shape, dt, nm):
                t_ = P.tile(shape, dt, nm)
                return [t_, t_]
            ld_tm = two([128, 256], F32, "ld_tm")
            ldh_ = two([128, 256], BF16, "ldh")
            ldl_ = two([128, 256], BF16, "ldl")
            g_tm = two([128, 256], F32, "g_tm")
            E = two([128, 2, 384], F32, "E")
            AR = two([128, 4, 256], BF16, "AR")
            BTh = two([128, 4, 128], BF16, "BTh")
            KTh = two([128, 4, 128], BF16, "KTh")
            for j_ in range(2):
                P.memset(AR[j_], 0.0)
                P.memset(BTh[j_], 0.0, eng=PL)
                P.memset(KTh[j_], 0.0, eng=PL)
            BT = two([128, 2, 128], BF16, "BT")
            KT = two([128, 2, 128], BF16, "KT")
            Btm = two([128, 256], BF16, "Btm")
            Ktm = two([128, 256], BF16, "Ktm")
            Vb = two([128, 256], BF16, "Vb")
            bon = two([128, 256], F32, "bon")
            AbT = two([128, 4, 256], BF16, "AbT")
            AkT = two([128, 4, 256], BF16, "AkT")
            Qp0 = two([128, 4, 128], BF16, "Qp0")
            Qb = [two([128, 4, 128], BF16, f"Qb{t_}") for t_ in range(2)]
            Qpb = [two([128, 4, 128], BF16, f"Qpb{t_}") for t_ in range(2)]
            Tt = two([128, 4, 128], BF16, "Tt")
            Tn = two([128, 4, 128], BF16, "Tn")
            Xb = one([128, 256], BF16, "Xb")
            Ub = one([128, 256], BF16, "Ub")
            ysb = one([128, 4, 64], F32, "ysb")
            ysq = one([128, 4, 64], F32, "ysq")
            yn = one([128, 4, 64], F32, "yn")
            ygb = one([128, 256], BF16, "ygb")
            st = two([128, 5, 4], F32, "st")
            tH = P.tile([128, 128], F32, "tH")
            if l > 0:
                rawv = raw[0:32, :]
                difv = tB[0:32, :]
                xvb = P.tile([32, 512], BF16, "xvb")
                vft = tA
                vg = rkr[0]
            idb4 = idb.unsq(1).bc([128, 4, 128])

            for g in range(4):
                tk = slice(g * 512, (g + 1) * 512)
                for m in range(8):
                    ps = P.psum(m % 2, [128, 512])
                    for kc in range(8):
                        P.mm(ps, Wr[:, kc, m * 128:(m + 1) * 128], hT[:, kc, tk], start=(kc == 0), stop=(kc == 7))
                    rw, df = raws[m % 2], difs[m % 2]
                    P.cp(rw[:, 0:1], carry[:, m:m + 1])
                    P.cp(rw[:, 1:513], ps, eng="act")
                    P.cp(carry[:, m:m + 1], rw[:, 512:513])
                    P.tt(df, rw[:, 0:512], rw[:, 1:513], ALU.subtract)
                    P.stt(mixed[m], df, pv[:, PV_MU + m:PV_MU + m + 1], rw[:, 1:513], ALU.mult, ALU.add)
                dbgcopy('mixed0', mixed[0], 512); dbgcopy('mixed6', mixed[6], 512); dbgcopy('mixed7', mixed[7], 512)
                r_, k_, v_ = mixed[0:2], mixed[2:4], mixed[4:6]
                if STOP <= 1: return
                if l == 0:
                    for c in range(2):
                        P.dma(P.dram(vf_ap[:, c, tk], "vfirst", g * 2 + c, g * 2 + c + 1), v_[c])
                else:
                    psv = P.psum(0, [32, 512])
                    for kc in range(8):
                        P.mm(psv, v1b[:, kc, :], hT[:, kc, tk], start=(kc == 0), stop=(kc == 7))
                    P.cp(rawv[:, 0:1], carv)
                    P.cp(rawv[:, 1:513], psv, eng="act")
                    P.cp(carv, rawv[:, 512:513])
                    P.tt(difv, rawv[:, 0:512], rawv[:, 1:513], ALU.subtract)
                    P.stt(difv, difv, pv[0:32, PV_VMU:PV_VMU + 1], rawv[:, 1:513], ALU.mult, ALU.add)
                    P.cp(xvb, difv)
                    for c in range(2):
                        psg_ = P.psum(1, [128, 512])
                        P.mm(psg_, v2b[:, c * 128:(c + 1) * 128], xvb)
                        P.act(vg, psg_, AF.Sigmoid, bias=pv[:, PV_V0 + c:PV_V0 + c + 1])
                        P.dma(vft, P.dram(vf_ap[:, c, tk], "vfirst", g * 2 + c, g * 2 + c + 1))
                        P.tt(vft, vft, v_[c], ALU.subtract)
                        P.tt(vft, vft, vg, ALU.mult)
                        P.tt(v_[c], v_[c], vft, ALU.add)
                if STOP <= 2: return
                P.act(tanhb, mixed[6][0:64, :], AF.Tanh)
                P.cp(xab[64:128, :], mixed[6][64:128, :])
                P.act(sgb, mixed[7], AF.Sigmoid)
                for c in range(2):
                    psa = P.psum(c, [128, 512])
                    P.mm(psa, a2b[64:128, c * 128:(c + 1) * 128], xab[64:128, :])
                    P.act(aT[c], psa, AF.Sigmoid, bias=pv[:, PV_A0 + c:PV_A0 + c + 1])
                dbgcopy('aT0', aT[0], 512); dbgcopy('aT1', aT[1], 512); dbgcopy('tanhb', tanhb, 512)
                if STOP <= 3: return
                def elem(c):
                    tA, tB, sph, spl = tAs[c], tBs[c], sphs[c], spls[c]
                    P.ts(kk[c], k_[c], pv[:, PV_KK + c:PV_KK + c + 1], ALU.mult)
                    P.tt(tA, kk[c], kk[c], ALU.mult)
                    yield
                    pss = P.psum(c, [128, 512])
                    split(tA, sph, spl)
                    P.mm(pss, blkb, sph, start=True, stop=False)
                    P.mm(pss, blkb, spl, start=False, stop=True)
                    yield
                    P.act(tB, pss, AF.Sqrt)
                    yield
                    P.ts(tB, tB, 1e-12, ALU.max)
                    P.recip(tB, tB)
                    yield
                    P.tt(kk[c], kk[c], tB, ALU.mult)
                    P.tt(bT[c], kk[c], aT[c], ALU.mult)
                    yield
                    P.ts(tA, aT[c], -1.0, ALU.add, pv[:, PV_KA + c:PV_KA + c + 1], ALU.mult)
                    P.stt(kp[c], tA, 1.0, k_[c], ALU.add, ALU.mult)
                    yield
                    P.stt(rkr[c], r_[c], pv[:, PV_RK + c:PV_RK + c + 1], kp[c], ALU.mult, ALU.mult)
                    yield
                    psr = P.psum(c, [128, 512])
                    split(rkr[c], sph, spl)
                    P.mm(psr, blkb, sph, start=True, stop=False)
                    P.mm(psr, blkb, spl, start=False, stop=True)
                    yield
                    P.tt(rkr[c], psr, v_[c], ALU.mult)
                    yield
                    split(rkr[c], bh[c], bl[c])
                    yield
                    split(v_[c], vh[c], vl[c])
                    yield

                gens_ = [elem(0), elem(1)]
                while gens_:
                    for g_ in list(gens_):
                        try:
                            next(g_)
                        except StopIteration:
                            gens_.remove(g_)

                m2b = mask2.unsq(1).bc([128, 4, 256])
                mTb = maskT.unsq(1).bc([128, 4, 128])

                def prep_pieces(j):
                    i = g * 4 + j
                    pb_ = i % 2
                    tj = slice(j * 128, (j + 1) * 128)
                    ldh, ldl = ldh_[pb_], ldl_[pb_]

                    def p1():
                        psl = P.psum(1, [128, 256])
                        P.mm(psl, tanhb[:, tj], w2b)
                        P.tt(ld_tm[pb_], psl, bvt[:, BV_W0:BV_W0 + 256], ALU.add)
                        P.act(ld_tm[pb_], ld_tm[pb_], AF.Sigmoid)
                        yield
                        psg2 = P.psum(1, [128, 256], off=1024)
                        P.mm(psg2, sgb[:, tj], g2b)
                        P.cp(g_tm[pb_], psg2, eng="act")
                        yield
                        split(ld_tm[pb_], ldh, ldl)
                        yield

                    def p2():
                        psc = P.psum(0, [128, 2, 256])
                        for c in range(2):
                            P.mm(psc[:, c, :], ldh[:, c * 128:(c + 1) * 128], trib, start=True, stop=False)
                            P.mm(psc[:, c, :], ldl[:, c * 128:(c + 1) * 128], trib, start=False, stop=True)
                        P.act(E[pb_][:, :, 0:256], psc, AF.Exp, scale=-C0)
                        P.act(E[pb_][:, :, 256:384], psc[:, :, 0:128], AF.Exp, scale=C0)
                        yield

                    def p3():
                        for c in range(2):
                            P.tt(BT[pb_][:, c, :], bT[c][:, tj], E[pb_][:, c, 256:384], ALU.mult, eng=PL)
                            P.tt(KT[pb_][:, c, :], kp[c][:, tj], E[pb_][:, c, 256:384], ALU.mult, eng=PL)
                            for hh in range(2):
                                h, rows = 2 * c + hh, slice(64 * hh, 64 * hh + 64)
                                P.stt(AR[pb_][rows, h, 0:128], kk[c][rows, tj], -1.0, E[pb_][rows, c, 128:256], ALU.mult, ALU.mult)
                                P.tt(AR[pb_][rows, h, 128:256], r_[c][rows, tj], E[pb_][rows, c, 0:128], ALU.mult)
                                P.cp(BTh[pb_][rows, h, :], BT[pb_][rows, c, :], eng=PL)
                                P.cp(KTh[pb_][rows, h, :], KT[pb_][rows, c, :], eng=PL)
                                yield

                    def p4():
                        ptb = P.psum(2, [128, 4, 128], BF16)
                        ptv = P.psum(2, [128, 4, 128], BF16, off=1024)
                        for c in range(2):
                            P.tr(ptb[:, c, :], BT[pb_][:, c, :], idb)
                            P.tr(ptb[:, 2 + c, :], KT[pb_][:, c, :], idb)
                            P.tr(ptv[:, c, :], vh[c][:, tj], idb)
                            P.tr(ptv[:, 2 + c, :], vl[c][:, tj], idb)
                        P.cp(Btm[pb_], ptb[:, 0:2, :].re("p a b -> p (a b)"), eng="act")
                        P.cp(Ktm[pb_], ptb[:, 2:4, :].re("p a b -> p (a b)"), eng="act")
                        P.cp(Vb[pb_], ptv[:, 0:2, :].re("p a b -> p (a b)"), eng="act")
                        yield

                    def p5():
                        psb = P.psum(2, [128, 4, 128], BF16)
                        for c in range(2):
                            P.tr(psb[:, c, :], bh[c][:, tj], idb)
                            P.tr(psb[:, 2 + c, :], bl[c][:, tj], idb)
                        P.cp(bon[pb_], psb[:, 0:2, :].re("p a b -> p (a b)"))
                        P.tt(bon[pb_], bon[pb_], psb[:, 2:4, :].re("p a b -> p (a b)"), ALU.add)
                        yield

                    return [p1, p2, p3, p4, p5]

                def stage_a(j):
                    pb_ = (g * 4 + j) % 2
                    psAb = P.psum(4, [128, 4, 256])
                    psAk = P.psum(6, [128, 4, 256])
                    psN = P.psum(3, [128, 4, 128])
                    for h in range(4):
                        c = h // 2
                        P.mm(psAb[:, h, :], BTh[pb_][:, h, :], AR[pb_][:, h, :])
                        P.mm(psAk[:, h, :], KTh[pb_][:, h, :], AR[pb_][:, h, :])
                        P.mm(psN[:, h, :], AR[pb_][:, h, 0:128], BT[pb_][:, c, :])
                    P.tt(AbT[pb_], psAb, m2b, ALU.mult)
                    P.tt(Qp0[pb_], psN, mTb, ALU.mult)
                    P.tt(AkT[pb_], psAk, m2b, ALU.mult)

                def inverse_pair(js):
                    pbs = [(g * 4 + j) % 2 for j in js]
                    Qs = [AbT[pb_][:, :, 0:128] for pb_ in pbs]
                    Qps = [Qp0[pb_] for pb_ in pbs]
                    for t_, pb_ in enumerate(pbs):
                        P.tt(Tt[pb_], Qs[t_], idb4, ALU.add, eng=PL)
                        P.tt(Tn[pb_], Qps[t_], idb4, ALU.add, eng=PL)
                    for s_ in range(6):
                        lastst = (s_ == 5)
                        psQ = [P.psum(4 * t_, [128, 4, 128]) for t_ in range(2)]
                        psQp = [P.psum(4 * t_ + 1, [128, 4, 128]) for t_ in range(2)]
                        psT = [P.psum(4 * t_ + 2, [128, 4, 128]) for t_ in range(2)]
                        psTn = [P.psum(4 * t_ + 3, [128, 4, 128]) for t_ in range(2)]
                        for t_, pb_ in enumerate(pbs):
                            for h in range(4):
                                P.mm(psQ[t_][:, h, :], Qps[t_][:, h, :], Qs[t_][:, h, :])
                            if not lastst:
                                for h in range(4):
                                    P.mm(psQp[t_][:, h, :], Qs[t_][:, h, :], Qps[t_][:, h, :])
                        Qn = [Qb[pb_][s_ % 2] for pb_ in pbs]
                        Qpn = [Qpb[pb_][s_ % 2] for pb_ in pbs]
                        for t_, pb_ in enumerate(pbs):
                            P.cp(Qn[t_], psQ[t_], eng="act")
                            if not lastst:
                                P.cp(Qpn[t_], psQp[t_])
                        for t_, pb_ in enumerate(pbs):
                            for h in range(4):
                                P.mm(psT[t_][:, h, :], Tn[pb_][:, h, :], Qn[t_][:, h, :])
                            if not lastst:
                                for h in range(4):
                                    P.mm(psTn[t_][:, h, :], Tt[pb_][:, h, :], Qpn[t_][:, h, :])
                        for t_, pb_ in enumerate(pbs):
                            P.tt(Tt[pb_], Tt[pb_], psT[t_], ALU.add)
                            if not lastst:
                                P.tt(Tn[pb_], Tn[pb_], psTn[t_], ALU.add)
                        Qs, Qps = Qn, Qpn

                psY_of = {}

                def chain(j):
                    pb_ = (g * 4 + j) % 2
                    psX = P.psum(0, [128, 4, 64])
                    psU = P.psum(0, [128, 4, 64], off=1024)
                    psY = P.psum(1, [128, 4, 64])
                    psH = P.psum(1, [128, 2, 128], off=1024)
                    for h in range(4):
                        c = h // 2
                        hs = slice(h * 64, (h + 1) * 64)
                        P.mm(psX[:, h, :], AR[pb_][:, h, 0:128], Hb[:, c, 64 * (h % 2):64 * (h % 2) + 64], start=True, stop=False)
                        P.mm(psX[:, h, :], AkT[pb_][:, h, 0:128], Vb[pb_][:, hs], start=False, stop=True)
                    yield
                    P.cp(Xb[pb_], psX.re("p a b -> p (a b)"), eng="act")
                    yield
                    for h in range(4):
                        hs = slice(h * 64, (h + 1) * 64)
                        P.mm(psU[:, h, :], Tt[pb_][:, h, :], Xb[pb_][:, hs])
                    yield
                    P.cp(Ub[pb_], psU.re("p a b -> p (a b)"), eng="act")
                    yield
                    for c in range(2):
                        P.mm(psH[:, c, :], Btm[pb_][:, c * 128:(c + 1) * 128], Ub[pb_][:, c * 128:(c + 1) * 128], start=True, stop=False)
                        P.mm(psH[:, c, :], Ktm[pb_][:, c * 128:(c + 1) * 128], Vb[pb_][:, c * 128:(c + 1) * 128], start=False, stop=True)
                    for h in range(4):
                        c = h // 2
                        hs = slice(h * 64, (h + 1) * 64)
                        P.mm(psY[:, h, :], AR[pb_][:, h, 128:256], Hb[:, c, 64 * (h % 2):64 * (h % 2) + 64], start=True, stop=False)
                        P.mm(psY[:, h, :], AbT[pb_][:, h, 128:256], Ub[pb_][:, hs], start=False, stop=False)
                        P.mm(psY[:, h, :], AkT[pb_][:, h, 128:256], Vb[pb_][:, hs], start=False, stop=True)
                    yield
                    for c in range(2):
                        P.tt(tH, psH[:, c, :], Hm[:, c, :], ALU.add)
                        P.ts(Hm[:, c, :], tH, E[pb_][:, c, 127:128], ALU.mult)
                        P.cp(Hb[:, c, :], Hm[:, c, :])
                        yield
                    psY_of[j] = psY

                def epilogue(j):
                    psY = psY_of[j]
                    i = g * 4 + j
                    pb_ = i % 2
                    tl = slice(i * 128, (i + 1) * 128)
                    y_, q_, n_, s_t = ysb[pb_], ysq[pb_], yn[pb_], st[pb_]
                    P.cp(y_, psY)
                    yield
                    P.act(q_, y_, AF.Square)
                    P.red(s_t[:, 0, :], y_)
                    P.red(s_t[:, 1, :], q_)
                    yield
                    P.ts(s_t[:, 2, :], s_t[:, 0, :], 1.0 / 64, ALU.mult)
                    P.tt(s_t[:, 3, :], s_t[:, 2, :], s_t[:, 2, :], ALU.mult)
                    P.stt(s_t[:, 4, :], s_t[:, 1, :], 1.0 / 64, s_t[:, 3, :], ALU.mult, ALU.subtract)
                    rsqrt(s_t[:, 4, :], s_t[:, 4, :], 1.0, gneps_t)
                    yield
                    P.tt(n_, y_, s_t[:, 2, :].unsq(2).bc([128, 4, 64]), ALU.subtract)
                    P.tt(n_, n_, s_t[:, 4, :].unsq(2).bc([128, 4, 64]), ALU.mult)
                    yield
                    nf = n_.re("p a b -> p (a b)")
                    P.tt(nf, nf, bvt[:, BV_LNW:BV_LNW + 256], ALU.mult)
                    P.tt(nf, nf, bvt[:, BV_LNB:BV_LNB + 256], ALU.add)
                    P.tt(nf, nf, bon[pb_], ALU.add)
                    P.tt(ygb[pb_], nf, g_tm[pb_], ALU.mult)
                    yield
                    pty = P.psum(2, [128, 2, 128], BF16)
                    for c in range(2):
                        P.tr(pty[:, c, :], ygb[pb_][:, c * 128:(c + 1) * 128], idb)
                    P.cp(ygT[:, :, tl], pty, eng="act")
                    yield

                def run_all(*gens):
                    gens = list(gens)
                    while gens:
                        for g_ in list(gens):
                            try:
                                next(g_)
                            except StopIteration:
                                gens.remove(g_)

                def prep_gen(j):
                    for p_ in prep_pieces(j):
                        yield from p_()

                def ep_gen(j):
                    yield from epilogue(j)

                for jp in (0, 2):
                    run_all(prep_gen(jp), prep_gen(jp + 1))
                    for j in (jp, jp + 1):
                        stage_a(j)
                    inverse_pair((jp, jp + 1))
                    run_all(chain(jp))
                    run_all(chain(jp + 1), ep_gen(jp))
                    run_all(ep_gen(jp + 1))
        for l in range(nlayers):
            last = (l == DEPTH - 1)
            P.release(base_mark)
            pv = P.tile([128, PV_N], F32, "pv")
            P.dma(pv, DR("pvec", dr["pvec"][l]))
            bvt = P.tile([128, BV_N], F32, "bvt")
            P.dma(bvt, DR("bvec", dr["bvec"][l].partition_broadcast(128)))
            layer_mark0 = P.mark()
            ygT = P.tile([128, 2, S], BF16, "ygT")
            cvT = P.tile([128, 2, S], BF16, "cvT")
            layer_mark = P.mark()

            if l == 0:
                xsrc = lambda i: DR("x", dr["x"][i * 128:(i + 1) * 128, :], i, i + 1)
            else:
                xsrc = lambda i: XL1(xl1_ap[i * 128:(i + 1) * 128, :], i, i + 1)
            xdst = (lambda i: OUT(out_ap[i * 128:(i + 1) * 128, :], i, i + 1)) if last else \
                   (lambda i: XL1(xl1_ap[i * 128:(i + 1) * 128, :], i, i + 1))
            norm_phase(xsrc, "attn_norm", l)
            if dbg == f"hT{l}":
                tmpd = P.tile([128, 2048], F32, "tmpd")
                P.cp(tmpd, hT[:, 0, :])
                dump(tmpd, 2048)

            if "rwkv" in phases:
                P.release(layer_mark)
                rwkv_phase(l, pv, bvt, ygT)
                if dbg == f"ygT{l}":
                    P.release(layer_mark + 20480)
                    tmpd = P.tile([128, 4096], F32, "tmpd")
                    P.cp(tmpd[:, 0:2048], ygT[:, 0, :])
                    P.cp(tmpd[:, 2048:4096], ygT[:, 1, :])
                    dump(tmpd, 4096)

            if "conv" in phases:
                P.release(layer_mark)
                Wcv = P.tile([128, 8, 768], BF16, "Wcv")
                wload(Wcv, "w_in", l, cols=(4768, 5536), pat="(kc p) n -> p kc n", p=128)
                bg = P.tile([128, S], F32, "bg")
                u = P.tile([128, S + 2], F32, "u")
                xc = P.tile([128, S], F32, "xc")
                yv = P.tile([128, S], F32, "yv")
                P.memset(u[:, 0:2], 0.0)
                for fc in range(2):
                    for part, dst in ((0, bg), (1, u[:, 2:S + 2]), (2, xc)):
                        for tc in range(4):
                            ps = P.psum((part * 4 + tc) % 4, [128, 512])
                            c0 = part * 256 + fc * 128
                            for kc in range(8):
                                P.mm(ps, Wcv[:, kc, c0:c0 + 128], hT[:, kc, tc * 512:(tc + 1) * 512],
                                     start=(kc == 0), stop=(kc == 7))
                            P.cp(dst[:, tc * 512:(tc + 1) * 512], ps, eng="act")
                    P.tt(u[:, 2:S + 2], u[:, 2:S + 2], xc, ALU.mult)
                    cw = lambda j: pv[:, PV_CW + fc * 3 + j:PV_CW + fc * 3 + j + 1]
                    P.ts(yv, u[:, 2:S + 2], cw(2), ALU.mult)
                    P.stt(yv, u[:, 1:S + 1], cw(1), yv, ALU.mult, ALU.add)
                    P.stt(yv, u[:, 0:S], cw(0), yv, ALU.mult, ALU.add)
                    P.tt(cvT[:, fc, :], yv, bg, ALU.mult)
                if dbg == f"cvT{l}":
                    P.release(layer_mark)
                    tmpd = P.tile([128, 4096], F32, "tmpd")
                    P.cp(tmpd[:, 0:2048], cvT[:, 0, :])
                    P.cp(tmpd[:, 2048:4096], cvT[:, 1, :])
                    dump(tmpd, 4096)

            P.release(layer_mark)
            oT = P.tile([64, 8, S], BF16, "oT")
            layer_mark2 = P.mark()
            if "mla" in phases:
                P.release(layer_mark2)
                Wc = P.tile([128, 8, 672], BF16, "Wc")
                wload(Wc, "w_in", l, cols=(3072, 3744), pat="(kc p) n -> p kc n", p=128)
                Wqb = P.tile([128, 3, 768], BF16, "Wqb")
                wload(Wqb, "mla_wq_b", l, pat="(kc p) n -> p kc n", p=128)
                Wkvb = P.tile([128, 2, 1024], BF16, "Wkvb")
                wload(Wkvb, "mla_wkv_b", l, pat="(kc p) n -> p kc n", p=128)
                cT = P.tile([128, 5, S], BF16, "cT")
                msa = P.tile([128, 3, NT], F32, "msa")
                rq = P.tile([128, NT], F32, "rq")
                rq2 = P.tile([128, NT], F32, "rq2")
                rkv = P.tile([128, NT], F32, "rkv")
                P.memset(msa, 0.0)
                for m in range(5):
                    for tc in range(4):
                        ps = P.psum((m * 4 + tc) % 2, [128, 512])
                        for kc in range(8):
                            P.mm(ps, Wc[:, kc, m * 128:(m + 1) * 128], hT[:, kc, tc * 512:(tc + 1) * 512],
                                 start=(kc == 0), stop=(kc == 7))
                        P.act(cT[:, m, tc * 512:(tc + 1) * 512], ps, AF.Copy, scale=pv[:, PV_GQA + m:PV_GQA + m + 1])
                mla_mark = P.mark()
                sqj = P.tile([128, 640], F32, "sqj")
                for i in range(NT):
                    ps1 = P.psum(2 + (i % 2) * 2, [128, 512])
                    ps2 = P.psum(3 + (i % 2) * 2, [128, 128])
                    for kc in range(8):
                        P.mm(ps1, hT[:, kc, i * 128:(i + 1) * 128], Wc[:, kc, 0:512], start=(kc == 0), stop=(kc == 7))
                    for kc in range(8):
                        P.mm(ps2, hT[:, kc, i * 128:(i + 1) * 128], Wc[:, kc, 512:640], start=(kc == 0), stop=(kc == 7))
                    P.act(sqj[:, 0:384], ps1[:, 0:384], AF.Square, accum=msa[:, 0, i:i + 1])
                    P.act(sqj[:, 384:512], ps1[:, 384:512], AF.Square, accum=msa[:, 1, i:i + 1])
                    P.act(sqj[:, 512:640], ps2, AF.Square, accum=msa[:, 2, i:i + 1])
                rsqrt(rq, msa[:, 0, :], 1.0 / 384, eps_t)
                P.tt(rq2, rq, rq, ALU.mult)
                P.ts(rq2, rq2, 1.0 / 96, ALU.mult)
                P.tt(rkv, msa[:, 1, :], msa[:, 2, :], ALU.add)
                rsqrt(rkv, rkv, 1.0 / 256, eps_t)
                gq = bvt[:, BV_GQ:BV_GQ + 96]
                gk = bvt[:, BV_GK:BV_GK + 96]
                for hh in range(2):
                    P.release(mla_mark)
                    qT = P.tile([96, 4, S], BF16, "qT")
                    kT = P.tile([96, 4, S], BF16, "kT")
                    vtm = P.tile([128, NT, 4, 64], BF16, "vtm")
                    def two_(shape, dt, nm):
                        return [P.tile(shape, dt, f"{nm}{j_}") for j_ in range(2)]
                    sq_ = two_([128, 4, 96], F32, "sq")
                    sqk_ = two_([128, 4, 96], F32, "sqk")
                    qn_ = two_([128, 4, 96], F32, "qn")
                    kn_ = two_([128, 4, 96], F32, "kn")
                    qb_ = two_([128, 4, 96], BF16, "qb")
                    kb_ = two_([128, 4, 96], BF16, "kb")
                    s4_ = two_([128, 4], F32, "s4")
                    s4k_ = two_([128, 4], F32, "s4k")
                    r1_ = two_([128, 4, 16], F32, "r1")
                    r2_ = two_([128, 4, 16], F32, "r2")
                    r3_ = two_([128, 4, 16], F32, "r3")
                    r4_ = two_([128, 4, 16], F32, "r4")

                    def rope(xn_, xb_, i, ta, tb, eng):
                        cosb = cosT[:, i, :].unsq(1).bc([128, 4, 16])
                        sinb = sinT[:, i, :].unsq(1).bc([128, 4, 16])
                        x1, x2 = xn_[:, :, 64:80], xn_[:, :, 80:96]
                        return [
                            lambda: P.cp(xb_[:, :, 0:64], xn_[:, :, 0:64], eng=eng),
                            lambda: P.tt(ta, x1, cosb, ALU.mult, eng=eng),
                            lambda: P.tt(tb, x2, sinb, ALU.mult, eng=eng),
                            lambda: P.tt(xb_[:, :, 64:80], ta, tb, ALU.subtract, eng=eng),
                            lambda: P.tt(ta, x1, sinb, ALU.mult, eng=eng),
                            lambda: P.tt(tb, x2, cosb, ALU.mult, eng=eng),
                            lambda: P.tt(xb_[:, :, 80:96], ta, tb, ALU.add, eng=eng),
                        ]

                    def proj(i):
                        tl_ = slice(i * 128, (i + 1) * 128)
                        b0 = 0 if i % 2 == 0 else 5
                        psq_ = P.psum(b0, [128, 4, 96])
                        pskv_ = P.psum(b0 + 1, [128, 4, 128])
                        pspe_ = P.psum(b0 + 2, [128, 32])
                        for kc in range(3):
                            P.mm(psq_, cT[:, kc, tl_], Wqb[:, kc, hh * 384:(hh + 1) * 384], start=(kc == 0), stop=(kc == 2))
                        for kc in range(2):
                            P.mm(pskv_, cT[:, 3 + kc, tl_], Wkvb[:, kc, hh * 512:(hh + 1) * 512], start=(kc == 0), stop=(kc == 1))
                        for kc in range(8):
                            P.mm(pspe_, hT[:, kc, tl_], Wc[:, kc, 640:672], start=(kc == 0), stop=(kc == 7))
                        return psq_, pskv_, pspe_

                    chains = []
                    for i in range(NT):
                        tl = slice(i * 128, (i + 1) * 128)
                        psq, pskv, pspe = proj(i)
                        pi_ = i % 2
                        sq, sqk, qn, kn, qb, kb = sq_[pi_], sqk_[pi_], qn_[pi_], kn_[pi_], qb_[pi_], kb_[pi_]
                        s4, s4k, r1, r2, r3, r4 = s4_[pi_], s4k_[pi_], r1_[pi_], r2_[pi_], r3_[pi_], r4_[pi_]
                        pt = P.psum(3, [96, 4, 128], BF16, off=1024 * pi_)
                        pt2 = P.psum(4, [96, 4, 128], BF16, off=1024 * pi_)

                        def qpath(psq=psq, sq=sq, s4=s4, qn=qn, qb=qb, r1=r1, r2=r2, pt=pt, i=i, tl=tl):
                            ops = [
                                lambda: P.act(sq, psq, AF.Square),
                                lambda: P.red(s4, sq),
                                lambda: P.act(s4, s4, AF.Sqrt, bias=eps_t, scale=rq2[:, i:i + 1]),
                                lambda: P.recip(s4, s4),
                                lambda: P.ts(s4, s4, rq[:, i:i + 1], ALU.mult),
                                lambda: P.tt(qn, psq, s4.unsq(2).bc([128, 4, 96]), ALU.mult),
                                lambda: P.tt(qn, qn, gq.unsq(1).bc([128, 4, 96]), ALU.mult),
                            ] + rope(qn, qb, i, r1, r2, "pool")
                            ops += [(lambda h=h: P.tr(pt[:, h, :], qb[:, h, :], idb)) for h in range(4)]
                            ops.append(lambda: P.cp(qT[:, :, tl], pt, eng="act"))
                            return ops

                        def kpath(pskv=pskv, pspe=pspe, sqk=sqk, s4k=s4k, kn=kn, kb=kb, r3=r3, r4=r4, pt2=pt2, i=i, tl=tl):
                            ops = [
                                lambda: P.ts(kn[:, :, 0:64], pskv[:, :, 0:64], rkv[:, i:i + 1], ALU.mult),
                                lambda: P.ts(vtm[:, i, :, :], pskv[:, :, 64:128], rkv[:, i:i + 1], ALU.mult),
                                lambda: P.cp(kn[:, :, 64:96], pspe.unsq(1).bc([128, 4, 32])),
                                lambda: P.act(sqk, kn, AF.Square),
                                lambda: P.red(s4k, sqk),
                                lambda: P.act(s4k, s4k, AF.Sqrt, bias=eps_t, scale=1.0 / 96),
                                lambda: P.recip(s4k, s4k),
                                lambda: P.tt(kn, kn, s4k.unsq(2).bc([128, 4, 96]), ALU.mult),
                                lambda: P.tt(kn, kn, gk.unsq(1).bc([128, 4, 96]), ALU.mult),
                            ] + rope(kn, kb, i, r3, r4, "pool")
                            ops += [(lambda h=h: P.tr(pt2[:, h, :], kb[:, h, :], idb)) for h in range(4)]
                            ops.append(lambda: P.cp(kT[:, :, tl], pt2, eng="act"))
                            return ops

                        chains += [kpath(), qpath()]
                        if i % 2 == 1:
                            for n_ in range(max(len(c_) for c_ in chains)):
                                for c_ in chains:
                                    if n_ < len(c_):
                                        c_[n_]()
                            chains = []
                    pTb = [P.tile([128, 512], BF16, f"pT{j}") for j in range(4)]
                    rz = P.tile([64, 512], F32, "rz")
                    it = 0
                    for h in range(4):
                        for qc in range(4):
                            pso = P.psum(4 + (it % 2) * 2, [64, 512])
                            psz = P.psum(5 + (it % 2) * 2, [64, 512])
                            it += 1
                            nk = 4 * qc + 4
                            def score(kt):
                                c0 = max(0, kt - 4 * qc) * 128
                                pss = P.psum(kt % 4, [128, 512])
                                P.mm(pss[:, c0:512], kT[:, h, kt * 128:(kt + 1) * 128], qT[:, h, qc * 512 + c0:(qc + 1) * 512])
                                return pss
                            nxt = score(0)
                            for kt in range(nk):
                                c0 = max(0, kt - 4 * qc) * 128
                                pss = nxt
                                if kt + 1 < nk:
                                    nxt = score(kt + 1)
                                pT = pTb[kt % 4]
                                P.act(pT[:, c0:512], pss[:, c0:512], AF.Exp, scale=float(96 ** -0.5))
                                if kt >= 4 * qc:
                                    P.tt(pT[:, c0:c0 + 128], pT[:, c0:c0 + 128], cmaskb, ALU.mult, eng="pool")
                                P.mm(pso[:, c0:512], vtm[:, kt, h, :], pT[:, c0:512], start=(kt == 0), stop=(kt == nk - 1))
                                P.mm(psz[:, c0:512], onesb, pT[:, c0:512], start=(kt == 0), stop=(kt == nk - 1))
                            P.recip(rz, psz)
                            P.tt(oT[:, hh * 4 + h, qc * 512:(qc + 1) * 512], pso, rz, ALU.mult)
                if dbg == f"oT{l}":
                    P.release(layer_mark2)
                    tmpd = P.tile([64, 4096], F32, "tmpd")
                    P.cp(tmpd[:, 0:2048], oT[:, 0, :])
                    P.cp(tmpd[:, 2048:4096], oT[:, 7, :])
                    dump(tmpd, 4096)
            if "merge" in phases:
                P.release(layer_mark2)
                mT = P.tile([128, 8, S], BF16, "mT")
                merge_mark = P.mark()
                Wo = P.tile([64, 8, D], BF16, "Wo")
                wload(Wo, "mla_w_o", l, pat="(h p) n -> p h n", p=64)
                Wro = P.tile([128, 2, D], BF16, "Wro")
                wload(Wro, "rwkv_w_o", l, pat="(c p) n -> p c n", p=128)
                Wco = P.tile([128, 2, D], BF16, "Wco")
                wload(Wco, "conv_w_o", l, pat="(c p) n -> p c n", p=128)
                Wg = [P.tile([128, 8, 3, 128], BF16, f"Wg{j}") for j in range(2)]
                gs = [[P.tile([128, 512], F32, f"gs{s_}{j}") for j in range(3)] for s_ in range(2)]
                mas = [P.tile([128, 512], F32, f"ma{s_}") for s_ in range(2)]
                mbs = [P.tile([128, 512], F32, f"mb{s_}") for s_ in range(2)]

                def load_wg(dc_):
                    for j in range(3):
                        wload(Wg[dc_ % 2][:, :, j, :], "w_in", l, cols=(j * D + dc_ * 128, j * D + (dc_ + 1) * 128),
                              pat="(kc p) n -> p kc n", p=128)
                load_wg(0)
                it_ = 0
                gcnt = 0
                for dc in range(8):
                    wg = Wg[dc % 2]
                    if dc + 1 < 8:
                        load_wg(dc + 1)
                    for tc in range(4):
                        tk = slice(tc * 512, (tc + 1) * 512)
                        set_ = it_ % 2
                        it_ += 1
                        ma, mb, gs_ = mas[set_], mbs[set_], gs[set_]
                        for j in range(3):
                            psG = P.psum(6 + gcnt % 2, [128, 512])
                            gcnt += 1
                            for kc in range(8):
                                P.mm(psG, wg[:, kc, j, :], hT[:, kc, tk], start=(kc == 0), stop=(kc == 7))
                            P.act(gs_[j], psG, AF.Sigmoid)
                        psA = P.psum(3 * set_, [128, 512])
                        psB = P.psum(3 * set_ + 1, [128, 512])
                        psC = P.psum(3 * set_ + 2, [128, 512])
                        for h in range(8):
                            P.mm(psA, Wo[:, h, dc * 128:(dc + 1) * 128], oT[:, h, tk], start=(h == 0), stop=(h == 7))
                        for c in range(2):
                            P.mm(psB, Wro[:, c, dc * 128:(dc + 1) * 128], ygT[:, c, tk], start=(c == 0), stop=(c == 1))
                        for c in range(2):
                            P.mm(psC, Wco[:, c, dc * 128:(dc + 1) * 128], cvT[:, c, tk], start=(c == 0), stop=(c == 1))
                        P.tt(ma, psA, gs_[0], ALU.mult)
                        P.tt(mb, psB, gs_[1], ALU.mult)
                        P.tt(ma, ma, mb, ALU.add)
                        P.tt(mb, psC, gs_[2], ALU.mult)
                        P.tt(mT[:, dc, tk], ma, mb, ALU.add)
                if dbg == f"mT{l}":
                    tmpd = P.tile([128, 2048], F32, "tmpd")
                    P.cp(tmpd, mT[:, 3, :])
                    dump(tmpd, 2048)
                P.release(merge_mark)
                Wout = P.tile([128, 8, D], BF16, "Wout")
                wload(Wout, "w_out", l, pat="(kc p) n -> p kc n", p=128)
                xin = [P.tile([128, D], F32, f"xin{j}") for j in range(3)]
                nst = norm_setup("mlp_norm", l)
                for i in range(NT):
                    tl = slice(i * 128, (i + 1) * 128)
                    x_ = xin[i % 3]
                    P.dma(x_, xsrc(i))
                    for nb in range(2):
                        ps = P.psum((i % 2) * 2 + nb, [128, 512])
                        for dc in range(8):
                            P.mm(ps, mT[:, dc, tl], Wout[:, dc, nb * 512:(nb + 1) * 512], start=(dc == 0), stop=(dc == 7))
                        P.tt(x_[:, nb * 512:(nb + 1) * 512], x_[:, nb * 512:(nb + 1) * 512], ps, ALU.add)
                    P.dma(XMID(xmid_ap[tl, :], i, i + 1), x_)
                    norm_tile(i, x_, nst, 4 + i % 2)

            if "mlp" in phases:
                P.release(layer_mark0)
                if "merge" not in phases:
                    norm_phase(lambda i: XMID(xmid_ap[i * 128:(i + 1) * 128, :], i, i + 1), "mlp_norm", l)
                aT = P.tile([128, 32, 1024], BF16, "aT")
                Wu = [P.tile([128, 8, 1024], BF16, f"Wu{j}") for j in range(2)]
                Wd = [P.tile([128, 4, D], BF16, f"Wd{j}") for j in range(3)]
                rl = [P.tile([128, 512], F32, f"rl{j}") for j in range(2)]
                xin = [P.tile([128, D], F32, f"xin{j}") for j in range(2)]
                sched = []
                for th in range(2):
                    for fg in range(4):
                        sched.append(("u", th, fg))
                    for tg in range(2):
                        for fg in range(8):
                            sched.append(("d", th, tg, fg))
                bufs = {}
                cnt = {"u": 0, "d": 0}

                def issue(k):
                    if k in bufs or k >= len(sched):
                        return
                    it_ = sched[k]
                    if it_[0] == "u":
                        w_ = Wu[cnt["u"] % 2]
                        cnt["u"] += 1
                        wload(w_, "w_up", l, cols=(it_[2] * 1024, (it_[2] + 1) * 1024), pat="(kc p) n -> p kc n", p=128)
                    else:
                        w_ = Wd[cnt["d"] % 3]
                        cnt["d"] += 1
                        wload(w_, "w_down", l, rows=(it_[3] * 512, (it_[3] + 1) * 512), pat="(f p) n -> p f n", p=128)
                    bufs[k] = w_

                issue(0)
                for k, it_ in enumerate(sched):
                    issue(k + 1)
                    if k + 2 < len(sched) and sched[k + 2][0] == "d" and sched[k + 1][0] == "d":
                        issue(k + 2)
                    w_ = bufs[k]
                    if it_[0] == "u":
                        _, th, fg = it_
                        t0 = th * 1024
                        for fi in range(8):
                            f = fg * 8 + fi
                            for tc in range(2):
                                ps = P.psum((fi * 2 + tc) % 4, [128, 512])
                                for kc in range(8):
                                    P.mm(ps, w_[:, kc, fi * 128:(fi + 1) * 128], hT[:, kc, t0 + tc * 512:t0 + (tc + 1) * 512],
                                         start=(kc == 0), stop=(kc == 7))
                                r_ = rl[(fi * 2 + tc) % 2]
                                P.act(r_, ps, AF.Relu)
                                P.tt(aT[:, f, tc * 512:(tc + 1) * 512], r_, r_, ALU.mult)
                    else:
                        _, th, tg, fg = it_
                        for ti in range(4):
                            tl = slice((tg * 4 + ti) * 128, (tg * 4 + ti + 1) * 128)
                            for fi in range(4):
                                for nb in range(2):
                                    ps = P.psum(ti * 2 + nb, [128, 512])
                                    P.mm(ps, aT[:, fg * 4 + fi, tl], w_[:, fi, nb * 512:(nb + 1) * 512],
                                         start=(fg == 0 and fi == 0), stop=(fg == 7 and fi == 3))
                        if fg == 7:
                            for ti in range(4):
                                i = th * 8 + tg * 4 + ti
                                x_ = xin[ti % 2]
                                P.dma(x_, XMID(xmid_ap[i * 128:(i + 1) * 128, :], i, i + 1))
                                for nb in range(2):
                                    ps = P.psum(ti * 2 + nb, [128, 512])
                                    P.tt(x_[:, nb * 512:(nb + 1) * 512], x_[:, nb * 512:(nb + 1) * 512], ps, ALU.add)
                                P.dma(xdst(i), x_)
        P.final_wait()
        P.replay(block)
    return nc


def make_inputs(inp):
    inp = {k: np.asarray(v) for k, v in inp.items()}
    shared = {"consts": host_consts(),
              "pvec": np.stack([host_pvec(inp, l) for l in range(DEPTH)]),
              "bvec": np.stack([host_bvec(inp, l) for l in range(DEPTH)])}
    for k in ("attn_norm", "mlp_norm"):
        shared[k] = np.ascontiguousarray(inp[k], dtype=np.float32)
    for k in WNAMES:
        shared[k] = np.ascontiguousarray(inp[k], dtype=np.float32)
    maps = []
    for b in range(8):
        m = dict(shared)
        m["x"] = np.ascontiguousarray(inp["x"][b], dtype=np.float32)
        m["pos"] = np.ascontiguousarray(inp["positions"][b].astype(np.int32).reshape(NT, 128).T)
        maps.append(m)
    return maps


_NC_CACHE = {}


def kernel(**inputs):
    maps = make_inputs(inputs)
    shapes = {k: v.shape for k, v in maps[0].items()}
    key = "main"
    if key not in _NC_CACHE:
        _NC_CACHE[key] = build(shapes)
    nc = _NC_CACHE[key]
    res = run_bass_kernel_spmd(nc, maps, core_ids=list(range(8)))
    return np.stack([r["out"] for r in res.results]).astype(np.float32)
```
